# Optimizing a Trainium2 kernel written in Bass

```python
import math
import jax, jax.numpy as jnp
from jax import lax
import numpy as np

D_MODEL = 2048
BATCH = 4
SEQ = 2048
DEPTH = 1
DEC_BATCH = 128
DEC_SEQ = 4
PAST_LEN = 16384
PAGE_SIZE = 128

D_MIX = D_MODEL
D_POOL = D_MIX // 2
D_SSM = D_MIX - D_POOL
POOL_WINDOWS = (2, 4, 8, 16)
N_POOL_GROUPS = len(POOL_WINDOWS)
POOL_GROUP_DIM = D_POOL // N_POOL_GROUPS
POOL_BUF = max(POOL_WINDOWS) - 1
SSM_GROUP_CH = 16
N_SSM_GROUPS = D_SSM // SSM_GROUP_CH
SSM_STATE = 64
N_MEM = 256
N_XHEADS = 4
XHEAD_DIM = D_MODEL // N_XHEADS
D_FF = 5632
CONV_W = 3
CONV_BUF = CONV_W - 1
ALPHA = (2.0 * DEPTH) ** 0.25
BETA = (8.0 * DEPTH) ** -0.25
LN_EPS = 1e-5

kernel_name = "hymba_pool_s5_memxattn_convffn_step"


def layer_norm(x, g, b):
    xf = x.astype(jnp.float32)
    mu = jnp.mean(xf, axis=-1, keepdims=True)
    var = jnp.mean(jnp.square(xf - mu), axis=-1, keepdims=True)
    y = (xf - mu) * lax.rsqrt(var + LN_EPS) * g.astype(jnp.float32) + b.astype(jnp.float32)
    return y.astype(x.dtype)


def pool_mixer(u, buf, pos0, w_pool, pool_scale):
    nb, t, _ = u.shape
    ext = jnp.concatenate([buf.astype(u.dtype), u], axis=1)
    cs = jnp.cumsum(ext.astype(jnp.float32), axis=1)
    cs = jnp.pad(cs, ((0, 0), (1, 0), (0, 0)))
    end = cs[:, POOL_BUF + 1:POOL_BUF + 1 + t]
    pos = pos0 + jnp.arange(t, dtype=jnp.int32)
    outs = []
    for g, w in enumerate(POOL_WINDOWS):
        sl = slice(g * POOL_GROUP_DIM, (g + 1) * POOL_GROUP_DIM)
        start = cs[:, POOL_BUF + 1 - w:POOL_BUF + 1 - w + t, sl]
        cnt = jnp.minimum(pos + 1, w).astype(jnp.float32)[None, :, None]
        outs.append((end[..., sl] - start) / cnt)
    pooled = jnp.concatenate(outs, axis=-1) - u.astype(jnp.float32)
    pooled = pooled.astype(u.dtype).reshape(nb, t, N_POOL_GROUPS, POOL_GROUP_DIM)
    mixed = jnp.einsum('btgc,gcd->btgd', pooled, w_pool).reshape(nb, t, D_POOL)
    return mixed * pool_scale, ext[:, -POOL_BUF:]


def _ssm_combine(e1, e2):
    a1, b1 = e1
    a2, b2 = e2
    return a1 * a2, a2 * b1 + b2


def ssm_mixer(u, h0_re, h0_im, lambda_re, lambda_im, log_step, b_re, b_im, c_re, c_im, d_skip, w_glu, b_glu):
    f32 = jnp.float32
    nb, t, _ = u.shape
    uf = u.astype(f32).reshape(nb, t, N_SSM_GROUPS, SSM_GROUP_CH)
    lam = lax.complex(lambda_re.astype(f32), lambda_im.astype(f32))
    dt = jnp.exp(log_step.astype(f32))[:, None]
    lam_bar = jnp.exp(lam * dt)
    b_bar = ((lam_bar - 1.0) / lam)[:, :, None] * lax.complex(b_re.astype(f32), b_im.astype(f32))
    c = lax.complex(c_re.astype(f32), c_im.astype(f32))
    bu = jnp.einsum('gnc,btgc->btgn', b_bar, uf.astype(jnp.complex64))
    h0 = lax.complex(h0_re.astype(f32), h0_im.astype(f32))
    bu = bu.at[:, 0].add(lam_bar[None] * h0)
    a = jnp.broadcast_to(lam_bar, bu.shape)
    _, h = lax.associative_scan(_ssm_combine, (a, bu), axis=1)
    y = jnp.real(jnp.einsum('gcn,btgn->btgc', c, h)) + d_skip.astype(f32).reshape(N_SSM_GROUPS, SSM_GROUP_CH) * uf
    z = jax.nn.gelu(y.reshape(nb, t, D_SSM)).astype(u.dtype)
    out = z * jax.nn.sigmoid(z @ w_glu + b_glu)
    h_last = h[:, -1]
    return out, jnp.real(h_last).astype(h0_re.dtype), jnp.imag(h_last).astype(h0_re.dtype)


def mem_kv(mem, w_k, w_v):
    nb = mem.shape[0]
    k = (mem @ w_k).reshape(nb, N_MEM, N_XHEADS, XHEAD_DIM)
    v = (mem @ w_v).reshape(nb, N_MEM, N_XHEADS, XHEAD_DIM)
    return k, v


def cross_attn(h, k, v, w_q, w_o):
    nb, t, _ = h.shape
    q = (h @ w_q).reshape(nb, t, N_XHEADS, XHEAD_DIM)
    s = jnp.einsum('bthd,bmhd->bhtm', q, k).astype(jnp.float32) * (XHEAD_DIM ** -0.5)
    p = jax.nn.softmax(s, axis=-1).astype(v.dtype)
    o = jnp.einsum('bhtm,bmhd->bthd', p, v).reshape(nb, t, N_XHEADS * XHEAD_DIM)
    return o @ w_o


def conv_ffn(h, buf, w_gate, w_up, conv_w, conv_b, w_down):
    t = h.shape[1]
    g = h @ w_gate
    ext = jnp.concatenate([buf.astype(g.dtype), g], axis=1)
    gc = conv_b + ext[:, 0:t] * conv_w[0]
    for k in range(1, CONV_W):
        gc = gc + ext[:, k:k + t] * conv_w[k]
    out = (jax.nn.silu(gc) * (h @ w_up)) @ w_down
    return out, ext[:, -CONV_BUF:]


def layer(x, mk, mv, pool_buf, pos0, h0_re, h0_im, conv_buf,
          w_in, w_pool, pool_scale, lambda_re, lambda_im, log_step, b_re, b_im, c_re, c_im,
          d_skip, w_glu, b_glu, w_out, ln1_g, ln1_b, w_q, w_o, ln2_g, ln2_b,
          w_gate, w_up, conv_w, conv_b, w_down, ln3_g, ln3_b):
    u = x @ w_in
    a_out, new_pool = pool_mixer(u[..., :D_POOL], pool_buf, pos0, w_pool, pool_scale)
    b_out, new_re, new_im = ssm_mixer(u[..., D_POOL:], h0_re, h0_im, lambda_re, lambda_im, log_step,
                                      b_re, b_im, c_re, c_im, d_skip, w_glu, b_glu)
    mix = jnp.concatenate([a_out, b_out], axis=-1) @ w_out
    h = layer_norm(ALPHA * x + mix, ln1_g, ln1_b)
    h = layer_norm(ALPHA * h + cross_attn(h, mk, mv, w_q, w_o), ln2_g, ln2_b)
    f, new_conv = conv_ffn(h, conv_buf, w_gate, w_up, conv_w, conv_b, w_down)
    y = layer_norm(ALPHA * h + f, ln3_g, ln3_b)
    return y, new_pool, new_re, new_im, new_conv


def setup_inputs(seed: int = 0) -> dict:
    key = jax.random.key(seed)
    ks = iter(jax.random.split(key, 48))
    f32 = jnp.float32
    L, G, N, C = DEPTH, N_SSM_GROUPS, SSM_STATE, SSM_GROUP_CH

    def nrm(shape, scale):
        return jax.random.normal(next(ks), shape, f32) * scale

    def gain(shape):
        return 1.0 + nrm(shape, 0.05)

    x_prompt = nrm((BATCH, SEQ, D_MODEL), 1.0)
    x_sample = nrm((DEC_BATCH, DEC_SEQ, D_MODEL), 1.0)
    mem_prompt = nrm((BATCH, N_MEM, D_MODEL), 1.0)
    state_pool = nrm((L, DEC_BATCH, POOL_BUF, D_POOL), 1.0)
    state_ssm_re = nrm((L, DEC_BATCH, G, N), 0.2)
    state_ssm_im = nrm((L, DEC_BATCH, G, N), 0.2)
    state_conv = nrm((L, DEC_BATCH, CONV_BUF, D_FF), 1.0)
    cache_mem_k = nrm((L, DEC_BATCH, N_MEM, N_XHEADS, XHEAD_DIM), 1.0)
    cache_mem_v = nrm((L, DEC_BATCH, N_MEM, N_XHEADS, XHEAD_DIM), BETA)

    w_in = nrm((L, D_MODEL, D_MIX), D_MODEL ** -0.5)
    w_pool = nrm((L, N_POOL_GROUPS, POOL_GROUP_DIM, POOL_GROUP_DIM), POOL_GROUP_DIM ** -0.5)
    pool_scale = gain((L, D_POOL))
    lambda_re = -0.5 + nrm((L, G, N), 0.01)
    lambda_im = math.pi * jnp.broadcast_to(jnp.arange(N, dtype=f32), (L, G, N)) + nrm((L, G, N), 0.01)
    log_step = jax.random.uniform(next(ks), (L, G), f32, math.log(1e-3), math.log(1e-1))
    b_re = nrm((L, G, N, C), (2.0 * C) ** -0.5)
    b_im = nrm((L, G, N, C), (2.0 * C) ** -0.5)
    c_re = nrm((L, G, C, N), (2.0 * N) ** -0.5)
    c_im = nrm((L, G, C, N), (2.0 * N) ** -0.5)
    d_skip = nrm((L, D_SSM), 1.0)
    w_glu = nrm((L, D_SSM, D_SSM), D_SSM ** -0.5)
    b_glu = nrm((L, D_SSM), 0.01)
    w_out = nrm((L, D_MIX, D_MODEL), BETA * D_MIX ** -0.5)
    ln1_g = gain((L, D_MODEL))
    ln1_b = nrm((L, D_MODEL), 0.01)
    w_q = nrm((L, D_MODEL, N_XHEADS * XHEAD_DIM), D_MODEL ** -0.5)
    w_k = nrm((L, D_MODEL, N_XHEADS * XHEAD_DIM), D_MODEL ** -0.5)
    w_v = nrm((L, D_MODEL, N_XHEADS * XHEAD_DIM), BETA * D_MODEL ** -0.5)
    w_o = nrm((L, N_XHEADS * XHEAD_DIM, D_MODEL), BETA * D_MODEL ** -0.5)
    ln2_g = gain((L, D_MODEL))
    ln2_b = nrm((L, D_MODEL), 0.01)
    w_gate = nrm((L, D_MODEL, D_FF), D_MODEL ** -0.5)
    w_up = nrm((L, D_MODEL, D_FF), D_MODEL ** -0.5)
    conv_w = nrm((L, CONV_W, D_FF), CONV_W ** -0.5)
    conv_b = nrm((L, D_FF), 0.01)
    w_down = nrm((L, D_FF, D_MODEL), BETA * D_FF ** -0.5)
    ln3_g = gain((L, D_MODEL))
    ln3_b = nrm((L, D_MODEL), 0.01)
    return {
        "x_prompt": x_prompt, "x_sample": x_sample, "mem_prompt": mem_prompt,
        "state_pool": state_pool, "state_ssm_re": state_ssm_re, "state_ssm_im": state_ssm_im,
        "state_conv": state_conv, "cache_mem_k": cache_mem_k, "cache_mem_v": cache_mem_v,
        "w_in": w_in, "w_pool": w_pool, "pool_scale": pool_scale,
        "lambda_re": lambda_re, "lambda_im": lambda_im, "log_step": log_step,
        "b_re": b_re, "b_im": b_im, "c_re": c_re, "c_im": c_im, "d_skip": d_skip,
        "w_glu": w_glu, "b_glu": b_glu, "w_out": w_out, "ln1_g": ln1_g, "ln1_b": ln1_b,
        "w_q": w_q, "w_k": w_k, "w_v": w_v, "w_o": w_o, "ln2_g": ln2_g, "ln2_b": ln2_b,
        "w_gate": w_gate, "w_up": w_up, "conv_w": conv_w, "conv_b": conv_b, "w_down": w_down,
        "ln3_g": ln3_g, "ln3_b": ln3_b,
    }


def reference(x_prompt, x_sample, mem_prompt, state_pool, state_ssm_re, state_ssm_im, state_conv,
              cache_mem_k, cache_mem_v, w_in, w_pool, pool_scale, lambda_re, lambda_im, log_step,
              b_re, b_im, c_re, c_im, d_skip, w_glu, b_glu, w_out, ln1_g, ln1_b,
              w_q, w_k, w_v, w_o, ln2_g, ln2_b, w_gate, w_up, conv_w, conv_b, w_down, ln3_g, ln3_b):
    yp, ys = x_prompt, x_sample
    nbp = x_prompt.shape[0]
    p_pool, p_re, p_im, p_conv, p_mk, p_mv = [], [], [], [], [], []
    s_pool, s_re, s_im, s_conv = [], [], [], []
    for l in range(DEPTH):
        prm = dict(w_in=w_in[l], w_pool=w_pool[l], pool_scale=pool_scale[l],
                   lambda_re=lambda_re[l], lambda_im=lambda_im[l], log_step=log_step[l],
                   b_re=b_re[l], b_im=b_im[l], c_re=c_re[l], c_im=c_im[l], d_skip=d_skip[l],
                   w_glu=w_glu[l], b_glu=b_glu[l], w_out=w_out[l], ln1_g=ln1_g[l], ln1_b=ln1_b[l],
                   w_q=w_q[l], w_o=w_o[l], ln2_g=ln2_g[l], ln2_b=ln2_b[l],
                   w_gate=w_gate[l], w_up=w_up[l], conv_w=conv_w[l], conv_b=conv_b[l],
                   w_down=w_down[l], ln3_g=ln3_g[l], ln3_b=ln3_b[l])
        mk_p, mv_p = mem_kv(mem_prompt, w_k[l], w_v[l])
        zpool = jnp.zeros((nbp, POOL_BUF, D_POOL), x_prompt.dtype)
        zssm = jnp.zeros((nbp, N_SSM_GROUPS, SSM_STATE), state_ssm_re.dtype)
        zconv = jnp.zeros((nbp, CONV_BUF, D_FF), x_prompt.dtype)
        yp, pb, pre, pim, pc = layer(yp, mk_p, mv_p, zpool, 0, zssm, zssm, zconv, **prm)
        ys, sb, sre, sim, sc = layer(ys, cache_mem_k[l], cache_mem_v[l], state_pool[l], PAST_LEN,
                                     state_ssm_re[l], state_ssm_im[l], state_conv[l], **prm)
        p_pool.append(pb); p_re.append(pre); p_im.append(pim); p_conv.append(pc)
        p_mk.append(mk_p); p_mv.append(mv_p)
        s_pool.append(sb); s_re.append(sre); s_im.append(sim); s_conv.append(sc)
    return (yp, ys,
            jnp.stack(p_pool), jnp.stack(p_re), jnp.stack(p_im), jnp.stack(p_conv),
            jnp.stack(p_mk), jnp.stack(p_mv),
            jnp.stack(s_pool), jnp.stack(s_re), jnp.stack(s_im), jnp.stack(s_conv))
```

```python
import numpy as np
import contextlib
import concourse.bass as bass
import concourse.mybir as mybir
from concourse.bass_utils import run_bass_kernel_spmd

F32 = mybir.dt.float32
BF16 = mybir.dt.bfloat16
AF = mybir.ActivationFunctionType
ALU = mybir.AluOpType
AX = mybir.AxisListType

NCORES = 8
D = 2048
KC = 16
DP = 1024
G = 64
NST = 64
CH = 16
DFF = 5632
FC = 44
NMEM = 256
NH = 4
HD = 512
NOWN = 1024
NSMP = 64
NHALO = 16
NCOL = NOWN + NSMP + NHALO
C_SMP = NOWN
C_HALO = NOWN + NSMP
ALPHA = 2.0 ** 0.25
EPS = 1e-5
BLOCKS = ((0, 512), (512, 1024), (1024, NCOL))
ENGS = ("pe", "act", "dve", "pool", "sp")
SB_BASE = 16640


class _Op:
    __slots__ = ("eng", "fn", "deps", "is_dma", "semkey", "sig", "idx", "has_dep")

    def __init__(self, eng, fn, is_dma, semkey):
        self.eng, self.fn, self.is_dma, self.semkey = eng, fn, is_dma, semkey
        self.deps = set()
        self.sig = None
        self.has_dep = False


class FW:
    def __init__(self, nc):
        self.nc = nc
        self.ops = []
        self.lastw = {}
        self.readers = {}

    def _add(self, op, r, w):
        op.idx = len(self.ops)
        r = list(r)
        w = list(w) + [k for k in r if k.startswith("ps")]
        r = [k for k in r if not k.startswith("ps")]
        for k in r:
            lw = self.lastw.get(k)
            if lw is not None:
                op.deps.update(lw)
        for k in w:
            lw = self.lastw.get(k)
            if lw is not None:
                op.deps.update(lw)
            op.deps.update(self.readers.get(k, ()))
        op.deps.discard(op.idx)
        if op.eng == "pe" and not op.is_dma:
            op.deps = {d for d in op.deps if not (self.ops[d].eng == "pe" and not self.ops[d].is_dma)}
        for k in r:
            lst = self.readers.setdefault(k, [])
            if not op.is_dma:
                lst[:] = [i for i in lst if self.ops[i].is_dma or self.ops[i].eng != op.eng]
            lst.append(op.idx)
        for k in w:
            self.lastw[k] = [op.idx]
            self.readers[k] = []
        self.ops.append(op)
        return op

    def op(self, eng, fn, r=(), w=(), **kw):
        if isinstance(fn, str):
            name = fn
            fn = lambda e, name=name, kw=kw: getattr(e, name)(**kw)
        return self._add(_Op(eng, fn, False, None), r, w)

    def dma(self, eng, fn, semkey, r=(), w=(), **kw):
        if fn is None:
            fn = lambda e, kw=kw: e.dma_start(**kw)
        return self._add(_Op(eng, fn, True, semkey), r, w)

    def retire(self, old, new):
        pend = []
        for k in old:
            pend += self.lastw.get(k, []) + self.readers.get(k, [])
        for k in new:
            self.lastw[k] = sorted(set(self.lastw.get(k, []) + pend))

    def emit(self, final_wait_eng="sp"):
        nc, ops = self.nc, self.ops
        for o in ops:
            for d in o.deps:
                ops[d].has_dep = True
        dma_keys = []
        for o in ops:
            if o.is_dma and o.semkey not in dma_keys:
                dma_keys.append(o.semkey)
        with contextlib.ExitStack() as st:
            esem = {e: st.enter_context(nc.semaphore("s_" + e)) for e in ENGS}
            dsem = {k: st.enter_context(nc.semaphore("d_%d" % i)) for i, k in enumerate(dma_keys)}
            ecnt = {e: 0 for e in ENGS}
            dcnt = {k: 0 for k in dma_keys}
            for o in ops:
                if o.is_dma:
                    dcnt[o.semkey] += 16
                    o.sig = (dsem[o.semkey], dcnt[o.semkey])
                elif o.has_dep:
                    ecnt[o.eng] += 1
                    o.sig = (esem[o.eng], ecnt[o.eng])
            block = st.enter_context(nc.Block())
            handles = {"pe": block.tensor, "act": block.scalar, "dve": block.vector,
                       "pool": block.gpsimd, "sp": block.sync}
            for ename in ENGS:
                mine = [o for o in ops if o.eng == ename]

                def body(e, mine=mine, ename=ename):
                    waited = {}
                    for o in mine:
                        need = {}
                        for d in o.deps:
                            sem, val = ops[d].sig
                            key = id(sem)
                            if waited.get(key, 0) >= val:
                                continue
                            if key not in need or need[key][1] < val:
                                need[key] = (sem, val)
                        for key, (sem, val) in need.items():
                            e.wait_ge(sem, val)
                            waited[key] = val
                        ins = o.fn(e)
                        if o.sig is not None:
                            ins.then_inc(o.sig[0], 16 if o.is_dma else 1)
                    if ename == final_wait_eng:
                        for k in dma_keys:
                            e.wait_ge(dsem[k], dcnt[k])
                handles[ename](body)
        self.counts = (ecnt, dcnt)


class Builder:
    def __init__(self, debug=()):
        self.nc = nc = bass.Bass("TRN2", target_bir_lowering=False)
        self.fw = FW(nc)
        self.debug = set(debug)
        self.ins = {}
        self.outs = {}
        self.ev_i = 0
        self.ps_i = 0
        self.psum = [nc.alloc_psum_tensor("psb%d" % i, [128, 512], F32).ap() for i in range(6)]
        self.psbf = nc.alloc_psum_tensor("psbf", [128, 1024], BF16).ap()
        self.psbf2 = nc.alloc_psum_tensor("psbf2", [128, 1024], BF16).ap()

    def din(self, name, shape, dt=F32):
        self.ins[name] = self.nc.dram_tensor(name, list(shape), dt, kind="ExternalInput").ap()
        return self.ins[name]

    def dout(self, name, shape, dt=F32):
        self.outs[name] = self.nc.dram_tensor(name, list(shape), dt, kind="ExternalOutput").ap()
        return self.outs[name]

    def sb(self, name, shape, dt, off):
        assert off % 32 == 0, (name, off)
        nbytes = int(np.prod(shape[1:])) * (2 if dt == BF16 else 4)
        assert off + nbytes <= 212736, (name, off, nbytes)
        return self.nc.alloc_sbuf_tensor_at(name, list(shape), dt, offset=SB_BASE + off).ap()

    def bank(self):
        i = self.ps_i % 6
        self.ps_i += 1
        return self.psum[i], "ps%d" % i

    def evac_eng(self):
        self.ev_i += 1
        return "act" if self.ev_i % 2 else "dve"

    def copy(self, eng, out, in_, r, w):
        if eng == "act":
            self.fw.op("act", lambda e: e.activation(out=out, in_=in_, func=AF.Copy), r=r, w=w)
        else:
            self.fw.op(eng, lambda e: e.tensor_copy(out=out, in_=in_), r=r, w=w)

    def dbg(self, name, ap, keys, dt=F32):
        if name not in self.debug:
            return
        o = self.dout("dbg_" + name, list(ap.shape), dt)
        self.fw.dma("sp", lambda e: e.dma_start(out=o, in_=ap), "dbg_" + name, r=keys)


OFF_RT = 0
OFF_UU = 0
OFF_SP = 32768
OFF_PT = 32768
OFF_E = 32768 + 17664
OFF_AT = 70656
OFF_BT = 70656 + 35328
OFF_WS = 141312
OFF_XT = 141312 + 32768
OFF_MISC = 190464
WS_SLOT = 8192
NWS = 4


class Kern0(Builder):
    def setup(self):
        nc = self.nc
        self.ws_i = 0
        self.misc_off = OFF_MISC
        self.ws = [self.sb("ws%d" % i, [128, WS_SLOT // 2], BF16, OFF_WS + i * WS_SLOT) for i in range(NWS)]

    def misc(self, name, shape, dt):
        nbytes = int(np.prod(shape[1:])) * (2 if dt == BF16 else 4)
        off = self.misc_off
        self.misc_off += (nbytes + 31) // 32 * 32
        return self.sb(name, shape, dt, off)

    def wload(self, src, kcw, ncols, ring="main"):
        if not hasattr(self, "rings"):
            h = WS_SLOT // 4
            self.rings = {"main": [(self.ws[i], "ws%d" % i) for i in range(NWS)],
                          "gu": [(self.ws[i], "ws%d" % i) for i in range(3)],
                          "dn": [(self.ws[3][:, 0:h], "ws3a"), (self.ws[3][:, h:2 * h], "ws3b")]}
            self.ring_i = {k: 0 for k in self.rings}
        slots = self.rings[ring]
        buf, key = slots[self.ring_i[ring] % len(slots)]
        self.ring_i[ring] += 1
        dst = buf[:, 0:kcw * ncols].rearrange("p (k n) -> p k n", n=ncols)
        self.fw.dma("pool", lambda e: e.dma_start(out=dst, in_=src, max_dma_last_dim=8192), key, w=[key])
        return dst, key

    def wstream(self, specs, ahead=1, ring="main"):
        specs = list(specs)
        q = []
        nxt = 0
        for i in range(len(specs)):
            while nxt < len(specs) and nxt <= i + ahead:
                q.append(self.wload(*specs[nxt], ring=ring))
                nxt += 1
            yield q.pop(0)

    def phase1(self):
        fw = self.fw
        xa = self.din("xa", [128, KC, 16, 128])
        xs = self.din("xs", [128, KC, NSMP])
        w_in_s = self.din("w_in_s", [4, 128, KC, 256])
        w_in_p = self.din("w_in_p", [4, 128, KC, 256])
        poolT = self.din("poolT", [8, 128, 16, 15])
        invc = self.din("invc", [128, 8, 16])
        XA = self.XA = self.sb("XA", [128, KC, 16, 128], BF16, OFF_AT)
        XS = self.XS = self.sb("XS", [128, KC, NSMP], BF16, OFF_XT)
        UU = self.UU = self.sb("UU", [128, G, 16, CH], BF16, OFF_UU)
        UUs = self.UUs = self.misc("UUs", [16, G, 4, CH], BF16)
        PT = self.PT = self.sb("PT", [128, 8, NCOL], BF16, OFF_PT)
        INVC = self.sb("INVC", [128, 8, 16], F32, OFF_XT + 2048)
        for q in range(4):
            fw.dma("pool", lambda e, q=q: e.dma_start(out=XA[:, 4 * q:4 * q + 4], in_=xa[:, 4 * q:4 * q + 4],
                                                       max_dma_last_dim=8192), "XA%d" % q, w=["XA%d" % q])
        fw.dma("pool", lambda e: e.dma_start(out=XS, in_=xs, max_dma_last_dim=8192), "XS", w=["XS"])
        fw.dma("sp", lambda e: e.dma_start(out=INVC, in_=invc), "INVC", w=["INVC"])
        xak = ["XA%d" % q for q in range(4)]
        ws_ = self.wstream([(w_in_s[b], KC, 256) for b in range(4)])
        for blk in range(4):
            W, wk = next(ws_)
            for s in range(16):
                ps, pk = self.bank()
                for kc in range(KC):
                    fw.op("pe", lambda e, ps=ps, kc=kc, s=s, W=W: e.matmul(
                        ps[:, 0:256], XA[:, kc, s, :], W[:, kc, :], start=(kc == 0), stop=(kc == KC - 1)),
                        r=[wk] + xak, w=[pk])
                self.copy(self.evac_eng(), UU[:, blk * 16:(blk + 1) * 16, s, :],
                          ps[:, 0:256].rearrange("p (g c) -> p g c", c=CH), r=[pk], w=["UU"])
            for i in range(4):
                ps, pk = self.bank()
                for kc in range(KC):
                    fw.op("pe", lambda e, ps=ps, kc=kc, i=i, W=W: e.matmul(
                        ps[0:16, 0:256], XS[:, kc, i * 16:(i + 1) * 16], W[:, kc, :], start=(kc == 0),
                        stop=(kc == KC - 1)), r=[wk, "XS"], w=[pk])
                self.copy(self.evac_eng(), UUs[:, blk * 16:(blk + 1) * 16, i, :],
                          ps[0:16, 0:256].rearrange("p (g c) -> p g c", c=CH), r=[pk], w=["UUs"])
        self.dbg("UU", UU, ["UU"], BF16)
        self.dbg("UUs", UUs, ["UUs"], BF16)
        E = [self.sb("E%d" % i, [128, 16, 66], F32, OFF_E + i * 4224) for i in range(4)]
        Es = [self.sb("Es%d" % i, [128, 16, 19], F32, OFF_XT + 3072 + i * 1216) for i in range(4)]
        p_pool = self.dout("p_pool", [128, 8, 15])
        PPS = self.sb("PPS", [128, 8, 15], F32, OFF_XT + 2560)
        s_pool = self.dout("s_pool", [8, 128, 16, 15])
        for i in range(4):
            fw.op("dve", lambda e, i=i: e.memset(E[i], 0.0), w=["E%d" % i])
            fw.op("dve", lambda e, i=i: e.memset(Es[i], 0.0), w=["Es%d" % i])
        wp_ = self.wstream([(w_in_p[b], KC, 256) for b in range(4)])
        for blk in range(4):
            W, wk = next(wp_)
            for mm in range(2):
                m = 2 * blk + mm
                grp = m // 2
                nst = grp + 1
                pa, ka = self.bank()
                pb, kb = self.bank()
                pc, kc_ = self.bank()
                lw = lambda kc, W=W, mm=mm: W[:, kc, mm * 128:(mm + 1) * 128]
                for kc in range(KC):
                    st, sp_ = (kc == 0), (kc == KC - 1)
                    fw.op("pe", lambda e, kc=kc, st=st, sp_=sp_, lw=lw, pa=pa: e.matmul(
                        pa, lw(kc), XA[:, kc, 0:8, 64:128], start=st, stop=sp_), r=[wk] + xak, w=[ka])
                    fw.op("pe", lambda e, kc=kc, st=st, sp_=sp_, lw=lw, pb=pb: e.matmul(
                        pb, lw(kc), XA[:, kc, 8:16, 64:128], start=st, stop=sp_), r=[wk] + xak, w=[kb])
                for kc in range(KC):
                    fw.op("pe", lambda e, kc=kc, lw=lw, pc=pc: e.matmul(
                        pc[:, 0:32], lw(kc), XA[:, kc, :, 62:64], start=(kc == 0), stop=(kc == KC - 1)),
                        r=[wk] + xak, w=[kc_])
                for kc in range(KC):
                    fw.op("pe", lambda e, kc=kc, lw=lw, pc=pc: e.matmul(
                        pc[:, 32:96], lw(kc), XS[:, kc, :], start=(kc == 0), stop=(kc == KC - 1)),
                        r=[wk, "XS"], w=[kc_])
                b0 = m % 2
                e0, es0 = E[b0], Es[b0]
                E0k, Es0k = "E%d" % b0, "Es%d" % b0
                self.copy("act", e0[:, 0:8, 2:66], pa.rearrange("p (s k) -> p s k", k=64), r=[ka], w=[E0k])
                self.copy("act", e0[:, 8:16, 2:66], pb.rearrange("p (s k) -> p s k", k=64), r=[kb], w=[E0k])
                self.copy("dve", e0[:, :, 0:2], pc[:, 0:32].rearrange("p (s k) -> p s k", k=2), r=[kc_], w=[E0k])
                self.copy("dve", es0[:, :, 15:19], pc[:, 32:96].rearrange("p (i q) -> p q i", q=16),
                          r=[kc_], w=[Es0k])
                fw.dma("sp", lambda e, m=m, es0=es0: e.dma_start(out=es0[:, :, 0:15], in_=poolT[m]),
                       "Es0h%d" % b0, w=[Es0k])
                if m == 7:
                    self.dbg("E0", e0, [E0k])
                    self.dbg("Es0", es0, [Es0k])
                self.copy("dve", PPS[:, m, :], e0[:, 1:16, 65], r=[E0k], w=["PPS"])
                fw.dma("sp", lambda e, m=m, es0=es0: e.dma_start(out=s_pool[m], in_=es0[:, :, 4:19]),
                       "spool%d" % b0, r=[Es0k])
                cur, curs, ci = e0, es0, b0
                for sti in range(nst):
                    d = 1 << sti
                    ni = 2 if ci != 2 else 3
                    nx, nxs = E[ni], Es[ni]
                    eng = "dve" if (m % 2 == 0) else "pool"
                    fw.op(eng, lambda e, nx=nx, cur=cur, d=d: e.tensor_tensor(
                        out=nx[:, d:16, :], in0=cur[:, d:16, :], in1=cur[:, 0:16 - d, :], op=ALU.add),
                        r=["E%d" % ci], w=["E%d" % ni])
                    fw.op(eng, lambda e, nx=nx, cur=cur, d=d: e.tensor_tensor(
                        out=nx[:, 0:d, 1:66], in0=cur[:, 0:d, 1:66], in1=cur[:, 16 - d:16, 0:65], op=ALU.add),
                        r=["E%d" % ci], w=["E%d" % ni])
                    fw.op(eng, lambda e, nxs=nxs, curs=curs, d=d: e.tensor_tensor(
                        out=nxs[:, :, d:19], in0=curs[:, :, d:19], in1=curs[:, :, 0:19 - d], op=ALU.add),
                        r=["Es%d" % ci], w=["Es%d" % ni])
                    cur, curs, ci = nx, nxs, ni
                rw = 1.0 / float(1 << nst)
                rk = ["E%d" % ci, E0k, "Es%d" % ci, Es0k]
                pk_ = "PT%d" % m
                fw.op("dve", lambda e, cur=cur, e0=e0, m=m, rw=rw: e.scalar_tensor_tensor(
                    out=PT[:, m, 0:NOWN].rearrange("p (s k) -> p s k", k=64), in0=cur[:, :, 2:66], scalar=rw,
                    in1=e0[:, :, 2:66], op0=ALU.mult, op1=ALU.subtract), r=rk, w=[pk_])
                fw.op("dve", lambda e, cur=cur, e0=e0, m=m, rw=rw: e.scalar_tensor_tensor(
                    out=PT[:, m, C_HALO:NCOL], in0=cur[:, :, 1], scalar=rw,
                    in1=e0[:, :, 1], op0=ALU.mult, op1=ALU.subtract), r=rk, w=[pk_])
                fw.op("dve", lambda e, curs=curs, es0=es0, m=m, rw=rw: e.scalar_tensor_tensor(
                    out=PT[:, m, C_SMP:C_HALO].rearrange("p (i q) -> p q i", q=16), in0=curs[:, :, 15:19], scalar=rw,
                    in1=es0[:, :, 15:19], op0=ALU.mult, op1=ALU.subtract), r=rk, w=[pk_])
                tmpc = Es[ni][:, :, 0]
                fw.op("dve", lambda e, cur=cur, m=m, tmpc=tmpc: e.tensor_tensor(
                    out=tmpc, in0=cur[:, :, 2], in1=INVC[:, m, :], op=ALU.mult),
                    r=rk + ["INVC"], w=["Es%d" % ni])
                fw.op("dve", lambda e, e0=e0, m=m, tmpc=tmpc: e.tensor_tensor(
                    out=PT[:, m, 0:NOWN].rearrange("p (s k) -> p s k", k=64)[:, :, 0], in0=tmpc, in1=e0[:, :, 2],
                    op=ALU.subtract), r=["Es%d" % ni, E0k], w=[pk_])
        fw.dma("sp", lambda e: e.dma_start(out=p_pool, in_=PPS), "ppool", r=["PPS"])
        self.dbg("PT", PT, ["PT%d" % m for m in range(8)], BF16)

    def phase2(self):
        fw = self.fw
        w_pool = self.din("w_pool", [4, 128, 2, 256])
        pscale = self.din("pscale", [128, 8])
        PSC = self.misc("PSC", [128, 8], F32)
        fw.dma("sp", lambda e: e.dma_start(out=PSC, in_=pscale), "PSC", w=["PSC"])
        ABT = self.ABT = self.sb("ABT", [128, KC, NCOL], BF16, OFF_BT)
        fw.retire(["XA%d" % q for q in range(4)], ["ABT%d" % m for m in range(16)])
        PT = self.PT
        for g in range(4):
            W, wk = self.wload(w_pool[g], 2, 256)
            for mm in range(2):
                m = 2 * g + mm
                banks = [self.bank() for _ in BLOCKS]
                for kc in range(2):
                    for (ps, pk), (c0, c1) in zip(banks, BLOCKS):
                        fw.op("pe", lambda e, ps=ps, kc=kc, c0=c0, c1=c1, W=W, mm=mm, g=g: e.matmul(
                            ps[:, 0:c1 - c0], W[:, kc, mm * 128:(mm + 1) * 128], PT[:, 2 * g + kc, c0:c1],
                            start=(kc == 0), stop=(kc == 1)), r=[wk, "PT%d" % (2 * g + kc)], w=[pk])
                for (ps, pk), (c0, c1) in zip(banks, BLOCKS):
                    fw.op("act", lambda e, ps=ps, c0=c0, c1=c1, m=m: e.activation(
                        out=ABT[:, m, c0:c1], in_=ps[:, 0:c1 - c0], func=AF.Copy, scale=PSC[:, m:m + 1]),
                        r=[pk, "PSC"], w=["ABT%d" % m])
        self.dbg("aT", ABT[:, 0:8, :], ["ABT%d" % m for m in range(8)], BF16)


def _tile_w(w, ncols):
    K, N = w.shape
    return np.ascontiguousarray(w.reshape(K // 128, 128, N // ncols, ncols).transpose(2, 1, 0, 3))


def _fm(v, nch):
    return np.ascontiguousarray(v.reshape(nch, 128).T)


def prep_shared(inp):
    sh = {}
    w_in = inp["w_in"][0]
    sh["w_in_p"] = _tile_w(w_in[:, :DP], 256)
    sh["w_in_s"] = _tile_w(w_in[:, DP:], 256)
    sh["w_pool"] = np.ascontiguousarray(inp["w_pool"][0].reshape(4, 2, 128, 256).transpose(0, 2, 1, 3))
    sh["pscale"] = _fm(inp["pool_scale"][0], 8)
    return sh


def prep_core(inp, c):
    b, half = c // 2, c % 2
    xp = inp["x_prompt"][b]
    own = xp[half * NOWN:(half + 1) * NOWN]
    pre = xp[0:NOWN] if half == 1 else np.zeros_like(own)
    x2 = np.concatenate([pre, own], 0)
    d = {}
    d["xa"] = np.ascontiguousarray(x2.reshape(128, 16, KC, 128).transpose(3, 2, 1, 0))
    xs = inp["x_sample"][16 * c:16 * c + 16]
    d["xs"] = np.ascontiguousarray(xs.transpose(2, 1, 0).reshape(KC, 128, NSMP).transpose(1, 0, 2))
    sp = inp["state_pool"][0, 16 * c:16 * c + 16]
    d["poolT"] = np.ascontiguousarray(sp.transpose(2, 0, 1).reshape(8, 128, 16, 15))
    invc = np.zeros((128, 8, 16), np.float32)
    for m in range(8):
        w = 2 << (m // 2)
        for s in range(16):
            invc[:, m, s] = 1.0 / (min(s + 1, w) if half == 0 else w)
    d["invc"] = invc
    return d


TWO_PI_LO = 6.283185
MAGIC = 12582912.0
K1 = 144


class KernSSM:
    def tt(self, eng, out, a, b, op, r, w):
        self.fw.op(eng, "tensor_tensor", r=r, w=w, out=out, in0=a, in1=b, op=op)

    def ts(self, eng, out, a, s1, op0, r, w, s2=None, op1=None):
        kw = dict(out=out, in0=a, scalar1=s1, scalar2=s2, op0=op0)
        if op1 is not None:
            kw["op1"] = op1
        self.fw.op(eng, "tensor_scalar", r=r, w=w, **kw)

    def phase3_tables(self):
        fw = self.fw
        names = ["lre", "lim", "lstep"]
        d_in = {n: self.din(n, [128, 32]) for n in names}
        for n in ("bre", "bim", "cre", "cim"):
            d_in[n] = self.din(n, [128, 32, CH])
        mv_d = self.din("mvals", [128, 33])
        cst_d = self.din("cst_bf", [128, 128 + 128 + 64])
        dsk_d = self.din("dskT", [128, G])
        fw.retire(["PT%d" % m for m in range(8)] + ["E0", "E1", "E2", "E3"], ["SPR"])
        fw.retire(["ws%d" % i for i in range(NWS)], ["WSR"])
        fw.retire(["XA%d" % q for q in range(4)], ["YT"])
        fw.retire(["XS", "INVC", "PPS", "Es0", "Es1", "Es2", "Es3"], ["X"])
        T = [self.sb("T%d" % i, [128, 2176], F32, OFF_SP + i * 8704) for i in range(4)]
        BTX = OFF_BT + 17664
        sm = {}
        off = BTX
        for n in ("bre", "bim", "cre", "cim", "BBre", "BBim", "ncim"):
            sm[n] = self.sb("sm_" + n, [128, 32, CH], F32, off)
            off += 2048
        for n in ("lre", "lim", "lstep", "dt", "a", "th", "den", "nre", "kre", "kim"):
            sm[n] = self.sb("sm_" + n, [128, 32], F32, off)
            off += 128
        assert off <= OFF_BT + 35328
        fw.retire(["XA%d" % q for q in range(4)], ["sm_" + n for n in sm])
        PWre = self.PWre = self.misc("PWre", [128, 32, 33], F32)
        PWim = self.PWim = self.misc("PWim", [128, 32, 33], F32)
        MV = self.misc("MV", [128, 33], F32)
        CST = self.misc("CST", [128, 320], BF16)
        self.IDB, self.MASK, self.I64R = CST[:, 0:128], CST[:, 128:256], CST[:, 256:320]
        DSK = self.DSK = self.misc("DSK", [128, G], F32)
        COEF = self.COEF = self.misc("COEF", [128, 32, 8], F32)
        L16 = self.L16 = self.misc("L16", [128, 4, 32], F32)
        Xre = self.Xre = self.sb("Xre", [128, 32, 8, CH], BF16, OFF_XT)
        Xim = self.Xim = self.sb("Xim", [128, 32, 8, CH], BF16, OFF_XT + 8192)
        Yre = self.Yre = self.sb("Yre", [128, 32, 17, CH], BF16, OFF_AT)
        Yim = self.Yim = self.sb("YimN", [128, 32, 17, CH], BF16, OFF_AT + 17408)
        for n in ("lre", "lim", "lstep", "bre", "bim", "cre", "cim"):
            fw.dma("sp", None, "ld_" + n, w=["sm_" + n], r=[], out=sm[n], in_=d_in[n])
        fw.dma("sp", None, "ld_mv", w=["MV"], out=MV, in_=mv_d)
        fw.dma("pool", None, "ld_cst", w=["CST"], out=CST, in_=cst_d)
        fw.dma("sp", None, "ld_dsk", w=["DSK"], out=DSK, in_=dsk_d)
        e = "dve"
        S = lambda n: ["sm_" + n]
        fw.op("act", "activation", r=S("lstep"), w=S("dt"), out=sm["dt"], in_=sm["lstep"], func=AF.Exp)
        self.tt(e, sm["a"], sm["lre"], sm["dt"], ALU.mult, S("lre") + S("dt"), S("a"))
        self.tt(e, sm["th"], sm["lim"], sm["dt"], ALU.mult, S("lim") + S("dt"), S("th"))
        ACt = [self.sb("AC%d" % i, [128, 1056], F32, OFF_E + i * 4224) for i in range(4)]
        fw.retire(["E0", "E1", "E2", "E3"], ["AC0", "AC1", "AC2", "AC3"])
        A = [ACt[i].rearrange("p (g i) -> p g i", i=33) for i in range(4)]
        bc_g = lambda v: v.unsqueeze(2).to_broadcast([128, 32, 33])
        bc_i = MV.unsqueeze(1).to_broadcast([128, 32, 33])
        self.tt(e, A[0], bc_g(sm["th"]), bc_i, ALU.mult, S("th") + ["MV"], ["AC0"])
        self.tt(e, A[1], bc_g(sm["a"]), bc_i, ALU.mult, S("a") + ["MV"], ["AC1"])
        fw.op("act", "activation", r=["AC1"], w=["AC1"], out=A[1], in_=A[1], func=AF.Exp)
        self.ts(e, A[2], A[0], 1.0 / (2.0 * np.pi), ALU.mult, ["AC0"], ["AC2"])
        self.ts(e, A[3], A[2], MAGIC, ALU.add, ["AC2"], ["AC3"])
        self.ts(e, A[3], A[3], MAGIC, ALU.subtract, ["AC3"], ["AC3"])
        self.tt(e, A[3], A[2], A[3], ALU.subtract, ["AC2", "AC3"], ["AC3"])
        fw.op("act", "activation", r=["AC3"], w=["AC3"], out=A[3], in_=A[3], func=AF.Sin, scale=TWO_PI_LO)
        self.tt(e, PWim, A[1], A[3], ALU.mult, ["AC1", "AC3"], ["PWim"])
        self.ts(e, A[0], A[2], 0.25, ALU.add, ["AC2"], ["AC0"])
        self.ts(e, A[3], A[0], MAGIC, ALU.add, ["AC0", "PWim"], ["AC3"])
        self.ts(e, A[3], A[3], MAGIC, ALU.subtract, ["AC3"], ["AC3"])
        self.tt(e, A[3], A[0], A[3], ALU.subtract, ["AC0", "AC3"], ["AC3"])
        fw.op("act", "activation", r=["AC3"], w=["AC3"], out=A[3], in_=A[3], func=AF.Sin, scale=TWO_PI_LO)
        self.tt(e, PWre, A[1], A[3], ALU.mult, ["AC1", "AC3"], ["PWre"])
        PW = ["PWre", "PWim"]
        p1re, p1im = PWre[:, :, 17], PWim[:, :, 17]
        self.tt(e, sm["den"], sm["lre"], sm["lre"], ALU.mult, S("lre"), S("den"))
        self.tt(e, sm["nre"], sm["lim"], sm["lim"], ALU.mult, S("lim"), S("nre"))
        self.tt(e, sm["den"], sm["den"], sm["nre"], ALU.add, S("den") + S("nre"), S("den"))
        fw.op(e, "reciprocal", r=S("den"), w=S("den"), out=sm["den"], in_=sm["den"])
        self.ts(e, sm["nre"], p1re, -1.0, ALU.add, PW + S("den"), S("nre"))
        self.tt(e, sm["kre"], sm["nre"], sm["lre"], ALU.mult, S("nre") + S("lre"), S("kre"))
        self.tt(e, sm["dt"], p1im, sm["lim"], ALU.mult, PW + S("lim") + S("a"), S("dt"))
        self.tt(e, sm["kre"], sm["kre"], sm["dt"], ALU.add, S("kre") + S("dt"), S("kre"))
        self.tt(e, sm["kre"], sm["kre"], sm["den"], ALU.mult, S("kre") + S("den"), S("kre"))
        self.tt(e, sm["kim"], p1im, sm["lre"], ALU.mult, PW + S("lre"), S("kim"))
        self.tt(e, sm["dt"], sm["nre"], sm["lim"], ALU.mult, S("nre") + S("lim") + S("kre"), S("dt"))
        self.tt(e, sm["kim"], sm["kim"], sm["dt"], ALU.subtract, S("kim") + S("dt"), S("kim"))
        self.tt(e, sm["kim"], sm["kim"], sm["den"], ALU.mult, S("kim") + S("den"), S("kim"))
        bc_c = lambda v: v.unsqueeze(2).to_broadcast([128, 32, CH])
        t1 = ACt[0][:, 0:512].rearrange("p (g c) -> p g c", c=CH)
        t2 = ACt[1][:, 0:512].rearrange("p (g c) -> p g c", c=CH)
        self.tt(e, t1, bc_c(sm["kre"]), sm["bre"], ALU.mult, S("kre") + S("bre") + PW, ["AC0"])
        self.tt(e, t2, bc_c(sm["kim"]), sm["bim"], ALU.mult, S("kim") + S("bim") + PW, ["AC1"])
        self.tt(e, sm["BBre"], t1, t2, ALU.subtract, ["AC0", "AC1"], S("BBre"))
        self.tt(e, t1, bc_c(sm["kre"]), sm["bim"], ALU.mult, S("kre") + S("bim") + S("BBre"), ["AC0"])
        self.tt(e, t2, bc_c(sm["kim"]), sm["bre"], ALU.mult, S("kim") + S("bre") + S("BBre"), ["AC1"])
        self.tt(e, sm["BBim"], t1, t2, ALU.add, ["AC0", "AC1"], S("BBim"))
        self.ts(e, sm["ncim"], sm["cim"], -1.0, ALU.mult, S("cim"), S("ncim"))
        for h in range(2):
            gs = slice(16 * h, 16 * h + 16)
            v1 = T[2][:, 0:2048].rearrange("p (g s c) -> p g s c", s=8, c=CH)
            v2 = T[3][:, 0:2048].rearrange("p (g s c) -> p g s c", s=8, c=CH)
            npre = PWre[:, gs, 0:8].unsqueeze(3).to_broadcast([128, 16, 8, CH])
            npim = PWim[:, gs, 0:8].unsqueeze(3).to_broadcast([128, 16, 8, CH])
            bbre = sm["BBre"][:, gs, :].unsqueeze(2).to_broadcast([128, 16, 8, CH])
            bbim = sm["BBim"][:, gs, :].unsqueeze(2).to_broadcast([128, 16, 8, CH])
            rr = PW + S("BBre") + S("BBim") + ["SPR", "AC0", "AC1", "AC2", "AC3"]
            for (o_, x1, y1, x2, y2, op) in ((Xre, npre, bbre, npim, bbim, ALU.subtract),
                                             (Xim, npre, bbim, npim, bbre, ALU.add)):
                self.tt("dve", v1, x1, y1, ALU.mult, rr + ["X"], ["T2"])
                self.tt("dve", v2, x2, y2, ALU.mult, rr + ["X"], ["T3"])
                self.tt("dve", o_[:, gs], v1, v2, op, ["T2", "T3"], ["X"])
        for h in range(4):
            gs = slice(8 * h, 8 * h + 8)
            v1 = T[0][:, 0:2176].rearrange("p (g s c) -> p g s c", s=17, c=CH)
            v2 = T[1][:, 0:2176].rearrange("p (g s c) -> p g s c", s=17, c=CH)
            ppre = PWre[:, gs, 16:33].unsqueeze(3).to_broadcast([128, 8, 17, CH])
            ppim = PWim[:, gs, 16:33].unsqueeze(3).to_broadcast([128, 8, 17, CH])
            bc_s = lambda v: v[:, gs, :].unsqueeze(2).to_broadcast([128, 8, 17, CH])
            rr = PW + S("cre") + S("cim") + S("ncim") + ["SPR", "AC0", "AC1", "AC2", "AC3"]
            for (o_, x1, y1, x2, y2) in ((Yre, ppre, bc_s(sm["cre"]), ppim, bc_s(sm["cim"])),
                                         (Yim, ppre, bc_s(sm["ncim"]), ppim, bc_s(sm["cre"]))):
                self.tt(e, v1, x1, y1, ALU.mult, rr + ["YT"], ["T0"])
                self.tt(e, v2, x2, y2, ALU.mult, rr + ["YT"], ["T1"])
                self.tt(e, o_[:, gs], v1, v2, ALU.subtract, ["T0", "T1"], ["YT"])
        for t, mi in ((0, 16 + 15), (1, 16 + 7)):
            cv = COEF[:, :, 4 * t:4 * t + 4]
            self.copy("pool", cv[:, :, 0], PWre[:, :, mi], PW, ["COEF"])
            self.copy("pool", cv[:, :, 1], PWim[:, :, mi], PW, ["COEF"])
            self.ts("pool", cv[:, :, 2], PWim[:, :, mi], -1.0, ALU.mult, PW, ["COEF"])
            self.copy("pool", cv[:, :, 3], PWre[:, :, mi], PW, ["COEF"])
        self.copy("pool", L16[:, 0, :], PWre[:, :, 32], PW, ["L16"])
        self.copy("pool", L16[:, 1, :], PWre[:, :, 32], PW, ["L16"])
        self.ts("pool", L16[:, 2, :], PWim[:, :, 32], -1.0, ALU.mult, PW, ["L16"])
        self.copy("pool", L16[:, 3, :], PWim[:, :, 32], PW, ["L16"])
        self.dbg("PWre", PWre, PW)
        self.dbg("PWim", PWim, PW)
        self.dbg("Xre", Xre, ["X"], BF16)
        self.dbg("Yre", Yre, ["YT"], BF16)
        self.dbg("YimN", Yim, ["YT"], BF16)
        self.dbg("BBre", sm["BBre"], S("BBre"))


def prep_shared_ssm(inp):
    sh = {}
    pl = lambda a: np.ascontiguousarray(a.reshape(32, 2, 64, *a.shape[2:]).transpose(1, 2, 0, *range(3, a.ndim + 1))
                                        .reshape(128, 32, *a.shape[2:]))
    lre, lim = inp["lambda_re"][0], inp["lambda_im"][0]
    sh["lre"], sh["lim"] = pl(lre), pl(lim)
    sh["lstep"] = pl(np.broadcast_to(inp["log_step"][0][:, None], (G, NST)))
    sh["bre"], sh["bim"] = pl(inp["b_re"][0]), pl(inp["b_im"][0])
    sh["cre"] = pl(inp["c_re"][0].transpose(0, 2, 1))
    sh["cim"] = pl(inp["c_im"][0].transpose(0, 2, 1))
    mv = np.concatenate([-np.arange(16), np.arange(17)]).astype(np.float32)
    sh["mvals"] = np.ascontiguousarray(np.broadcast_to(mv[None], (128, 33)))
    ident = np.eye(128, dtype=np.float32)
    sidx = np.arange(128) // 16
    mask = (sidx[None, :] >= sidx[:, None]).astype(np.float32)
    i64 = np.concatenate([np.eye(64, dtype=np.float32)] * 2, 0)
    sh["cst_bf"] = np.ascontiguousarray(np.concatenate([ident, mask, i64], 1))
    dsk = inp["d_skip"][0].reshape(G, CH)
    sh["dskT"] = np.ascontiguousarray(np.tile(dsk.T, (8, 1)))
    return sh


class KernSSM2:
    def phase3_main(self, stop=None):
        fw = self.fw
        UU, UUs = self.UU, self.UUs
        Xre, Xim, Yre, Yim = self.Xre, self.Xim, self.Yre, self.Yim
        IDB, MASK, I64R, DSK, COEF, L16 = self.IDB, self.MASK, self.I64R, self.DSK, self.COEF, self.L16
        h0_d = self.din("h0", [128, 2, 32, 16])
        p_ssm = self.dout("p_ssm", [128, 2, 32])
        s_ssm = self.dout("s_ssm", [128, 2, 32, 16])
        o = OFF_WS
        SBF = self.sb("SBF", [128, 2, 32, K1], BF16, o); o += 18432
        SPS = self.sb("SPS", [128, 2, 32, 16], F32, o); o += 4096
        sets = []
        for i in range(2):
            d = {}
            d["TDO"] = self.sb("TDO%d" % i, [128, 256], BF16, o); o += 512
            d["TD"], d["TO"] = d["TDO"][:, 0:128], d["TDO"][:, 128:256]
            d["TMP"] = self.sb("TMPD%d" % i, [128, 128], F32, o); o += 512
            d["BCT"] = self.sb("BCT%d" % i, [128, 2, 128], BF16, o); o += 512
            d["UT"] = self.sb("UT%d" % i, [128, 2, K1], BF16, o); o += 576
            d["D4"] = self.sb("D4%d" % i, [128, 8, 64], BF16, o); o += 1024
            sets.append(d)
        o = OFF_WS + 29824
        CUR = [self.sb("CUR%d" % i, [128, 4, 32], F32, o + 512 * i) for i in range(2)]; o += 1024
        TT = [self.sb("TT%d" % i, [128, 3, 32], F32, o + 384 * i) for i in range(3)]; o += 1152
        assert o <= OFF_WS + 32768
        SP = self.sb("SP", [128, 2, 32, K1 + 1], F32, OFF_SP)
        psbf = self.psbf
        for i in range(2):
            fw.op("pool", "memset", r=["WSR"], w=["UT%d" % i], ap=sets[i]["UT"], constant=0.0)
        fw.retire(["T0", "T1", "T2", "T3"], ["SP"])
        fw.op("pool", "memset", w=["SP"], ap=SP[:, :, :, 0], constant=0.0)

        def transposes(g, st, si):
            UT = st["UT"]
            for t in range(2):
                fw.op("pe", "transpose", r=["UU%d" % g, "UU", "CST"], w=["psbf"],
                      out=psbf[:, t * 128:(t + 1) * 128],
                      in_=UU[:, g, 8 * t:8 * t + 8, :].rearrange("p s c -> p (s c)"), identity=IDB)
            fw.op("pe", "transpose", r=["UUs", "UUs%d" % g, "CST"], w=["psbf"], out=psbf[64:128, 256:272],
                  in_=UUs[:, g, :, :].rearrange("p s c -> p (s c)"), identity=IDB[0:16, 0:16])
            self.copy(self.evac_eng(), UT[:, :, 0:128], psbf[:, 0:256].rearrange("p (t k) -> p t k", t=2),
                      ["psbf"], ["UT%d" % si])
            self.copy(self.evac_eng(), UT[64:128, 1, 128:K1], psbf[64:128, 256:272], ["psbf"], ["UT%d" % si])

        p1state = {}

        def p1_stageA(g):
            g2, par = g // 2, g % 2
            st = sets[g % 2]
            si = g % 2
            D4 = sets[g2 % 2]["D4"]
            dk = "D4%d" % (g2 % 2)
            if par == 0:
                self.tt("pool", D4, COEF[:, g2, :].unsqueeze(2).to_broadcast([128, 8, 64]),
                        I64R.unsqueeze(1).to_broadcast([128, 8, 64]), ALU.mult, ["COEF", "CST", "WSR"], [dk])
            rows = slice(64 * par, 64 * par + 64)
            pB, kB = self.bank()
            for t in range(2):
                a_ = D4[rows, 4 * t:4 * t + 2, :].rearrange("p a n -> p (a n)")
                b_ = D4[rows, 4 * t + 2:4 * t + 4, :].rearrange("p a n -> p (a n)")
                fw.op("pe", "matmul", r=["X", dk], w=[kB], out=pB[:, t * 128:(t + 1) * 128],
                      lhsT=Xre[rows, g2, :, :].rearrange("p s c -> p (s c)"), rhs=a_, start=True, stop=False)
                fw.op("pe", "matmul", r=["X", dk], w=[kB], out=pB[:, t * 128:(t + 1) * 128],
                      lhsT=Xim[rows, g2, :, :].rearrange("p s c -> p (s c)"), rhs=b_, start=False, stop=True)
            self.copy(self.evac_eng(), st["BCT"], pB[:, 0:256].rearrange("p (t n) -> p t n", t=2),
                      [kB], ["BCT%d" % si])
            transposes(g, st, si)

        def p1_stageB(g):
            g2, par = g // 2, g % 2
            st = sets[g % 2]
            si = g % 2
            rows = slice(64 * par, 64 * par + 64)
            if par == 0:
                p1state["pP"] = self.bank()
            pP, kP = p1state["pP"]
            for ri in range(2):
                for t in range(2):
                    fw.op("pe", "matmul", r=["BCT%d" % si, "UT%d" % si], w=[kP],
                          out=pP[rows, ri * K1:(ri + 1) * K1], lhsT=st["BCT"][:, t, ri * 64:(ri + 1) * 64],
                          rhs=st["UT"][:, t, :], start=(t == 0), stop=(t == 1))
            if par == 1:
                self.copy(self.evac_eng(), SP[:, :, g2, 1:K1 + 1],
                          pP[:, 0:2 * K1].rearrange("p (r k) -> p r k", r=2), [kP], ["SP"])

        p1_stageA(0)
        for g in range(G):
            if g + 1 < G:
                p1_stageA(g + 1)
            p1_stageB(g)
        if stop == "pass1":
            self.dbg("SP", SP, ["SP"])
            return
        e = "dve"
        fw.op(e, "memset", r=["WSR"], w=["CUR0"], ap=CUR[0], constant=0.0)
        fw.op(e, "memset", r=["WSR"], w=["CUR1"], ap=CUR[1], constant=0.0)
        import concourse.bass as _b
        L4 = L16.rearrange("p (w c) g -> p w c g", w=2)
        TWS = [self.sb("TWS%d" % i, [128, 3, 64], F32, OFF_WS + 29824 + 1024 + 768 * i) for i in range(2)]
        for i in range(2):
            fw.op("act", "activation", r=["SP", "WSR"], w=["TWp%d" % i], out=TWS[i][:, 2, :].rearrange("p (c g) -> p c g", c=2),
                  in_=SP[:, :, :, i + 1], func=AF.Copy)
        for k in range(128):
            c0, c1 = CUR[k % 2], CUR[(k + 1) % 2]
            k0, k1 = "CUR%d" % (k % 2), "CUR%d" % ((k + 1) % 2)
            tw = TWS[k % 2]
            ka, kp = "TWa%d" % (k % 2), "TWp%d" % (k % 2)
            win = _b.AP(tensor=c0.tensor, offset=c0.offset, ap=[list(c0.ap[0]), [32, 2], [32, 2], [1, 32]])
            self.tt(e, tw[:, 0:2, :].rearrange("p w (c g) -> p w c g", c=2), L4, win, ALU.mult, ["L16", k0, "WSR"], [ka])
            red_in = _b.AP(tensor=tw.tensor, offset=tw.offset, ap=[list(tw.ap[0]), [0, 2], [1, 64], [64, 3]])
            fw.op(e, "tensor_reduce", r=[ka, kp], w=[k1], out=c1.rearrange("p (r c) g -> p r (c g)", r=2),
                  in_=red_in, axis=AX.X, op=ALU.add)
            self.copy("act", SP[:, :, :, k + 1], c1[:, 0:2, :], [k1], ["SPo"])
            if k + 2 < 128:
                fw.op("act", "activation", r=["SP"], w=[kp], out=tw[:, 2, :].rearrange("p (c g) -> p c g", c=2),
                      in_=SP[:, :, :, k + 3], func=AF.Copy)
        fw.dma("sp", None, "p_ssm", r=["CUR0"], out=p_ssm, in_=CUR[0][:, 0:2, :])
        H0 = self.sb("H0", [128, 2, 32, 16], F32, OFF_SP - 0 + 0) if False else None
        h0t = self.sb("H0t", [128, 2, 32, 16], F32, OFF_BT + 17664)
        tq = [self.sb("TQ%d" % i, [128, 32, 16], F32, OFF_BT + 17664 + 4096 + 2048 * i) for i in range(3)]
        allsm = ["sm_" + n for n in ("bre", "bim", "cre", "cim", "BBre", "BBim", "ncim", "lre", "lim", "lstep",
                                     "dt", "a", "th", "den", "nre", "kre", "kim")]
        fw.retire(allsm, ["H0t", "TQ"])
        fw.dma("sp", None, "ld_h0", w=["H0t"], out=h0t, in_=h0_d)
        PWre, PWim = self.PWre, self.PWim
        bq = lambda v: v.unsqueeze(2).to_broadcast([128, 32, 16])
        n12re, n12im = bq(PWre[:, :, 12]), bq(PWim[:, :, 12])
        l16re, l16im = bq(PWre[:, :, 32]), bq(PWim[:, :, 32])
        PW = ["PWre", "PWim"]

        def cmul(out_re, out_im, are, aim, bre_, bim_, rk, wk):
            self.tt(e, tq[0], are, bre_, ALU.mult, rk + ["TQ"], ["TQ"])
            self.tt(e, tq[1], aim, bim_, ALU.mult, rk + ["TQ"], ["TQ"])
            self.tt(e, tq[2], are, bim_, ALU.mult, rk + ["TQ"], ["TQ"])
            self.tt(e, out_re, tq[0], tq[1], ALU.subtract, ["TQ"], wk)
            self.tt(e, tq[0], aim, bre_, ALU.mult, rk + ["TQ"], ["TQ"])
            self.tt(e, out_im, tq[2], tq[0], ALU.add, ["TQ"], wk)

        cmul(SPS[:, 0], SPS[:, 1], n12re, n12im, h0t[:, 0], h0t[:, 1], PW + ["H0t", "WSR"], ["SPS"])
        cmul(h0t[:, 0], h0t[:, 1], l16re, l16im, SPS[:, 0], SPS[:, 1], PW + ["SPS"], ["H0t"])
        self.tt(e, h0t, h0t, SP[:, :, :, 129:K1 + 1], ALU.add, ["H0t", "SP"], ["H0t"])
        fw.dma("sp", None, "s_ssm", r=["H0t"], out=s_ssm, in_=h0t)
        self.copy("act", SBF[:, :, :, 0:128], SP[:, :, :, 0:128], ["SP", "SPo", "WSR"], ["SBF"])
        self.copy("act", SBF[:, :, :, 128:K1], SPS, ["SPS"], ["SBF"])
        self.dbg("SP", SP, ["SP", "SPo"])
        if stop == "level2":
            return
        o2 = OFF_WS + 22528
        sets2 = []
        for i in range(2):
            d = {}
            d["TDO"] = self.sb("P2TDO%d" % i, [128, 256], BF16, o2); o2 += 512
            d["TD"], d["TO"] = d["TDO"][:, 0:128], d["TDO"][:, 128:256]
            d["TMP"] = self.sb("P2TMP%d" % i, [128, 128], F32, o2); o2 += 512
            d["UT"] = self.sb("P2UT%d" % i, [128, 2, K1], BF16, o2); o2 += 576
            d["CCP"] = self.sb("P2CCP%d" % i, [128, 2, 2, 256], BF16, o2); o2 += 2048
            sets2.append(d)
        assert o2 <= OFF_WS + 29824
        for i in range(2):
            fw.op("pool", "memset", r=["SBF"], w=["UT%d" % i], ap=sets2[i]["UT"], constant=0.0)
            fw.op("pool", "memset", r=["SBF"], w=["CCP%d" % i], ap=sets2[i]["CCP"], constant=0.0)
        Z2 = self.Z2 = self.sb("Z2", [128, 16, G, CH], BF16, OFF_SP)
        Zs2 = self.Zs2 = self.sb("Zs2", [16, 4, G, CH], BF16, OFF_MISC + 8224)
        fw.retire(["SP", "SPo"], ["Z2"])
        fw.retire(["PWre", "PWim"], ["Zs2"])
        def p2_stageA(g):
            g2, par = g // 2, g % 2
            si = g % 2
            st = sets2[si]
            ccp = sets2[g2 % 2]["CCP"]
            ck = "CCP%d" % (g2 % 2)
            if par == 0:
                for pr in range(2):
                    rws = slice(64 * pr, 64 * pr + 64)
                    for ri, Yt in ((0, Yre), (1, Yim)):
                        self.copy("act", ccp[rws, pr, ri, :],
                                  Yt[rws, g2, 1:17, :].rearrange("p j c -> p (j c)"), ["YT"], [ck])
            rows = slice(64 * par, 64 * par + 64)
            pT, kT = self.bank()
            fw.op("pe", "matmul", r=["X", "YT"], w=[kT], out=pT[:, 0:256],
                  lhsT=Xre[rows, g2, :, :].rearrange("p s c -> p (s c)"),
                  rhs=Yre[rows, g2, 0:16, :].rearrange("p j c -> p (j c)"), start=True, stop=False)
            fw.op("pe", "matmul", r=["X", "YT"], w=[kT], out=pT[:, 0:256],
                  lhsT=Xim[rows, g2, :, :].rearrange("p s c -> p (s c)"),
                  rhs=Yim[rows, g2, 0:16, :].rearrange("p j c -> p (j c)"), start=False, stop=True)
            self.tt("dve", st["TMP"], pT[:, 0:128], MASK, ALU.mult, [kT, "CST"], ["TMP%d" % si])
            self.copy("dve", st["TO"], pT[:, 128:256], [kT], ["TDO%d" % si])
            fw.op("dve", "scalar_tensor_tensor", r=["TMP%d" % si, "CST", "DSK"], w=["TDO%d" % si],
                  out=st["TD"], in0=IDB, scalar=DSK[:, g:g + 1], in1=st["TMP"], op0=ALU.mult, op1=ALU.add)
            transposes(g, st, si)

        def p2_stageB(g):
            g2, par = g // 2, g % 2
            si = g % 2
            st = sets2[si]
            ccp = sets2[g2 % 2]["CCP"]
            ck = "CCP%d" % (g2 % 2)
            UT = st["UT"]
            tdk = ["TDO%d" % si, "UT%d" % si, "SBF", ck]
            pY, kY = self.bank()
            fw.op("pe", "matmul", r=tdk, w=[kY], out=pY[:, 0:256], lhsT=UT[:, 0, 0:128], rhs=st["TDO"],
                  start=True, stop=False)
            fw.op("pe", "matmul", r=tdk, w=[kY], out=pY[:, 128:256], lhsT=UT[:, 1, 0:128], rhs=st["TD"],
                  start=False, stop=False)
            for ri in range(2):
                fw.op("pe", "matmul", r=tdk, w=[kY], out=pY[:, 0:256], lhsT=SBF[:, ri, g2, 0:128],
                      rhs=ccp[:, par, ri, :], start=False, stop=(ri == 1))
            fw.op("act", "activation", r=[kY], w=["Z2"], out=Z2[:, :, g, :],
                  in_=pY[:, 0:256].rearrange("p (j c) -> p j c", c=CH), func=AF.Gelu_apprx_tanh)
            pS_, kS_ = self.bank()
            fw.op("pe", "matmul", r=tdk, w=[kS_], out=pS_[0:16, 0:64], lhsT=UT[:, 0, 128:K1], rhs=st["TO"][:, 64:128],
                  start=True, stop=False)
            fw.op("pe", "matmul", r=tdk, w=[kS_], out=pS_[0:16, 0:64], lhsT=UT[:, 1, 128:K1], rhs=st["TD"][:, 64:128],
                  start=False, stop=False)
            for ri in range(2):
                fw.op("pe", "matmul", r=tdk, w=[kS_], out=pS_[0:16, 0:64], lhsT=SBF[:, ri, g2, 128:K1],
                      rhs=ccp[:, par, ri, 192:256], start=False, stop=(ri == 1))
            fw.op("act", "activation", r=[kS_], w=["Zs2"], out=Zs2[:, :, g, :],
                  in_=pS_[0:16, 0:64].rearrange("p (j c) -> p j c", c=CH), func=AF.Gelu_apprx_tanh)

        p2_stageA(0)
        for g in range(G):
            if g + 1 < G:
                p2_stageA(g + 1)
            p2_stageB(g)
        self.dbg("Z", Z2, ["Z2"], BF16)
        self.dbg("Zs", Zs2, ["Zs2"], BF16)


def prep_core_ssm(inp, c):
    d = {}
    pl = lambda a: a.reshape(32, 2, 64, 16).transpose(1, 2, 0, 3).reshape(128, 32, 16)
    hre = inp["state_ssm_re"][0, 16 * c:16 * c + 16].transpose(1, 2, 0)
    him = inp["state_ssm_im"][0, 16 * c:16 * c + 16].transpose(1, 2, 0)
    d["h0"] = np.ascontiguousarray(np.stack([pl(hre), pl(him)], 1))
    return d


class KernRest:
    def linear_fm(self, wd, kcw, nout, in_fn, in_keys, epilogue, wkeys_extra=(), blocks=BLOCKS, kc_tiles=None):
        fw = self.fw
        wst = self.wstream([(wd[b], kcw, 256) for b in range(nout // 256)], ahead=2)
        for blk in range(nout // 256):
            W, wk = next(wst)
            for mm in range(2):
                mo = 2 * blk + mm
                banks = [self.bank() for _ in blocks]
                for bi, ((ps, pk), (c0, c1)) in enumerate(zip(banks, blocks)):
                    for kc in range(kcw):
                        try:
                            ik = list(in_keys(kc, bi))
                        except TypeError:
                            ik = list(in_keys(kc))
                        fw.op("pe", "matmul", r=[wk] + ik, w=[pk], out=ps[:, 0:c1 - c0],
                              lhsT=W[:, kc, mm * 128:(mm + 1) * 128], rhs=in_fn(kc, c0, c1),
                              start=(kc == 0), stop=(kc == kcw - 1))
                epilogue(mo, [(ps, pk, c0, c1) for (ps, pk), (c0, c1) in zip(banks, blocks)])

    def layer_norm(self, tag, gname, bname, last=False):
        fw = self.fw
        RT, AT, BT = self.RT, self.AT, self.BTt
        g_d = self.din(gname, [128, KC])
        b_d = self.din(bname, [128, KC])
        GB = self.sb("GB_" + tag, [128, 2, KC], F32, self.a2_off(256))
        fw.dma("sp", None, "gb_" + tag, w=["GB" + tag], out=GB[:, 0, :], in_=g_d)
        fw.dma("sp", None, "gb2_" + tag, w=["GB" + tag], out=GB[:, 1, :], in_=b_d)
        ONES = self.ONES
        MSQ, VAR = self.LNT[0], self.LNT[1]
        for kc in range(KC):
            fw.retire(["AT%d.%d" % (kc, bi) for bi in range(3)], ["AT%d" % kc])
        for kc in range(KC):
            self.copy("dve", AT[:, kc, :], RT[:, kc, :], ["RT%d" % kc], ["AT%d" % kc])
            fw.op("act", "activation", r=["RT%d" % kc], w=["BT%d" % kc], out=BT[:, kc, :], in_=RT[:, kc, :],
                  func=AF.Square)
        s1, s2 = [], []
        for _ in BLOCKS:
            s1.append(self.bank())
            s2.append(self.bank())
        for kc in range(KC):
            for (ps, pk), (c0, c1) in zip(s1, BLOCKS):
                fw.op("pe", "matmul", r=["AT%d" % kc, "ONES"], w=[pk], out=ps[:, 0:c1 - c0], lhsT=ONES,
                      rhs=AT[:, kc, c0:c1], start=(kc == 0), stop=(kc == KC - 1))
        for kc in range(KC):
            for (ps, pk), (c0, c1) in zip(s2, BLOCKS):
                fw.op("pe", "matmul", r=["BT%d" % kc, "ONES"], w=[pk], out=ps[:, 0:c1 - c0], lhsT=ONES,
                      rhs=BT[:, kc, c0:c1], start=(kc == 0), stop=(kc == KC - 1))
        inv = 1.0 / D
        for ((p1, k1), (p2, k2), (c0, c1)) in zip(s1, s2, BLOCKS):
            n = c1 - c0
            fw.op("act", "activation", r=[k1], w=[k1], out=p1[:, 0:n], in_=p1[:, 0:n], func=AF.Copy, scale=inv)
            fw.op("act", "activation", r=[k1], w=["LNT0"], out=MSQ[:, c0:c1], in_=p1[:, 0:n], func=AF.Square)
            fw.op("dve", "scalar_tensor_tensor", r=[k2, "LNT0"], w=["LNT1"], out=VAR[:, c0:c1], in0=p2[:, 0:n],
                  scalar=inv, in1=MSQ[:, c0:c1], op0=ALU.mult, op1=ALU.subtract)
            self.ts("dve", VAR[:, c0:c1], VAR[:, c0:c1], EPS, ALU.add, ["LNT1"], ["LNT1"])
            fw.op("act", "activation", r=["LNT1"], w=["LNT1"], out=VAR[:, c0:c1], in_=VAR[:, c0:c1], func=AF.Sqrt)
            fw.op("dve", "reciprocal", r=["LNT1", k2], w=[k2], out=p2[:, 0:n], in_=VAR[:, c0:c1])
        fine = ["AT%d.%d" % (kc, bi) for kc in range(KC) for bi in range(3)]
        fw.retire(["AT%d" % kc for kc in range(KC)], fine)
        fw.retire(["LNT0", "LNT2"], ["LNTA", "LNTB"])
        TMPS = [(self.LNT[2], "LNTA"), (self.LNT[0], "LNTB")]
        for bi, ((p1, k1), (p2, k2), (c0, c1)) in enumerate(zip(s1, s2, BLOCKS)):
            n = c1 - c0
            for kc in range(KC):
                TMP, tk = TMPS[kc % 2]
                self.tt("dve", TMP[:, c0:c1], RT[:, kc, c0:c1], p1[:, 0:n], ALU.subtract, ["RT%d" % kc, k1], [tk])
                self.tt("dve", TMP[:, c0:c1], TMP[:, c0:c1], p2[:, 0:n], ALU.mult, [tk, k2], [tk])
                fw.op("act", "activation", r=[tk, "GB" + tag], w=["RT%d" % kc], out=RT[:, kc, c0:c1], in_=TMP[:, c0:c1],
                      func=AF.Identity, scale=GB[:, 0, kc:kc + 1], bias=GB[:, 1, kc:kc + 1])
                if not last:
                    wk_ = ["AT%d.%d" % (kc, bi)] + (["AT%d" % kc] if bi == 2 else [])
                    fw.op("act", "activation", r=[tk, "GB" + tag], w=wk_, out=AT[:, kc, c0:c1], in_=TMP[:, c0:c1],
                          func=AF.Identity, scale=GB[:, 0, kc:kc + 1], bias=GB[:, 1, kc:kc + 1])
        fw.retire(["LNTA", "LNTB"], ["LNT0", "LNT2"])

    def a2_off(self, nbytes):
        off = self.a2
        self.a2 += (nbytes + 31) // 32 * 32
        assert self.a2 <= 16832, self.a2
        return OFF_MISC + off

    def phase3_glu(self):
        fw = self.fw
        UU, UUs, IDB, psbf = self.UU, self.UUs, self.IDB, self.psbf
        Z2, Zs2 = self.Z2, self.Zs2
        w_glu = self.din("w_glu", [4, 128, 8, 256])
        bglu_d = self.din("b_glu", [128, 8])
        allz = ["Z2"]
        allzs = ["Zs2"]
        ZT = self.ZT = self.sb("ZT", [128, 8, NCOL], BF16, OFF_AT)
        fw.retire(["YT"], ["ZT%d" % m for m in range(8)])
        for m in range(8):
            for j in range(16):
                fw.op("pe", "transpose", r=allz + ["CST"], w=["psbf"], out=psbf[:, j * 64:(j + 1) * 64],
                      in_=Z2[64:128, j, 8 * m:8 * m + 8, :].rearrange("p g c -> p (g c)"), identity=IDB[64:128, 64:128])
            self.copy(self.evac_eng(), ZT[:, m, 0:NOWN], psbf[:, 0:1024], ["psbf"], ["ZT%d" % m])
        for m in range(8):
            fw.op("pool", "memset", w=["ZT%d" % m], ap=ZT[:, m, C_HALO:NCOL], constant=0.0)
        for m in range(8):
            for jj in range(2):
                fw.op("pe", "transpose", r=allz + ["CST"], w=["psbf"],
                      out=psbf[:, (2 * m + jj) * 32:(2 * m + jj + 1) * 32],
                      in_=Z2[32:64, 14 + jj, 8 * m:8 * m + 8, :].rearrange("p g c -> p (g c)"),
                      identity=IDB[32:64, 32:64])
        self.copy("dve", ZT[:, :, C_HALO + 14:NCOL],
                  psbf[:, 0:512].rearrange("p (m j k) -> p m j k", m=8, j=2)[:, :, :, 31],
                  ["psbf"], ["ZT%d" % m for m in range(8)])
        for m in range(8):
            for i in range(4):
                fw.op("pe", "transpose", r=allzs + ["CST"], w=["psbf"],
                      out=psbf[:, (4 * m + i) * 16:(4 * m + i + 1) * 16],
                      in_=Zs2[:, i, 8 * m:8 * m + 8, :].rearrange("p g c -> p (g c)"), identity=IDB[0:16, 0:16])
        self.copy("act", ZT[:, :, C_SMP:C_HALO], psbf[:, 0:512].rearrange("p (m c) -> p m c", m=8),
                  ["psbf"], ["ZT%d" % m for m in range(8)])
        self.dbg("ZT", ZT, ["ZT%d" % m for m in range(8)], BF16)
        fw.retire(["SBF", "SPS", "CUR0", "CUR1", "TT0", "TT1", "TT2", "UT0", "UT1", "CCP0", "CCP1", "TDO0", "TDO1",
                   "TMP0", "TMP1", "BCT0", "BCT1", "D40", "D41"], ["ws%d" % i for i in range(NWS)])
        self.a2 = 0
        fw.retire(["UUs"] + ["UUs%d" % g for g in range(G)] + allzs + ["PSC", "PWre", "PWim", "MV"], ["A2"])
        BG = self.sb("BGLU", [128, 8], F32, self.a2_off(32))
        fw.dma("sp", None, "bglu", r=["A2"], w=["BGLU"], out=BG, in_=bglu_d)
        self.sg_off = [self.a2_off(1024) for i in range(2)]
        SG = [self.sb("SG%d" % i, [128, 512], BF16, self.sg_off[i]) for i in range(2)]
        ABT = self.ABT
        fw.retire(["H0t", "TQ"], ["ABT%d" % m for m in range(8, 16)])
        sgi = [0]

        def epi(mo, banks):
            for (ps, pk, c0, c1) in banks:
                i = sgi[0] % 2
                sgi[0] += 1
                fw.op("act", "activation", r=[pk, "BGLU", "A2"], w=["SG%d" % i], out=SG[i][:, 0:c1 - c0],
                      in_=ps[:, 0:c1 - c0], func=AF.Sigmoid, bias=BG[:, mo:mo + 1])
                self.tt("dve", ABT[:, 8 + mo, c0:c1], SG[i][:, 0:c1 - c0], ZT[:, mo, c0:c1], ALU.mult,
                        ["SG%d" % i, "ZT%d" % mo], ["ABT%d" % (8 + mo)])

        self.linear_fm(w_glu, 8, DP, lambda kc, c0, c1: ZT[:, kc, c0:c1], lambda kc: ["ZT%d" % kc], epi)
        self.dbg("bT", ABT[:, 8:16, :], ["ABT%d" % m for m in range(8, 16)], BF16)

    def phase4(self):
        fw = self.fw
        xr = self.din("xr", [128, KC, NCOL])
        w_out = self.din("w_out", [8, 128, KC, 256])
        RT = self.RT = self.sb("RT", [128, KC, NCOL], F32, OFF_RT)
        self.AT = self.sb("AT", [128, KC, NCOL], BF16, OFF_AT)
        self.BTt = self.sb("BTt", [128, KC, NCOL], BF16, OFF_BT)
        allz = ["UU", "Z2"] + ["UU%d" % g for g in range(G)]
        fw.retire(allz + ["SP", "SPo", "T0", "T1", "T2", "T3", "SPR"], ["RT%d" % kc for kc in range(KC)])
        for q in range(4):
            fw.dma("sp", None, "xr%d" % q, w=["RT%d" % kc for kc in range(4 * q, 4 * q + 4)],
                   out=RT[:, 4 * q:4 * q + 4, :], in_=xr[:, 4 * q:4 * q + 4, :])
        ABT = self.ABT

        def epi(mo, banks):
            for (ps, pk, c0, c1) in banks:
                fw.op("dve", "scalar_tensor_tensor", r=[pk, "RT%d" % mo], w=["RT%d" % mo], out=RT[:, mo, c0:c1],
                      in0=RT[:, mo, c0:c1], scalar=ALPHA, in1=ps[:, 0:c1 - c0], op0=ALU.mult, op1=ALU.add)

        self.linear_fm(w_out, KC, D, lambda kc, c0, c1: ABT[:, kc, c0:c1], lambda kc: ["ABT%d" % kc], epi)
        self.lnt_off = self.a2_off(3 * NCOL * 4)
        self.LNT = [self.sb("LNT%d" % i, [128, NCOL], F32, self.lnt_off + i * NCOL * 4) for i in range(3)]
        self.ONES = self.sb("ONES", [128, 128], BF16, OFF_MISC + 16832 - 256 - 0) if False else None
        ones_d = self.din("ones_bf", [128, 128])
        self.ONES = self.sb("ONES", [128, 128], BF16, self.a2_off(256))
        fw.dma("pool", None, "ones", r=["A2"], w=["ONES"], out=self.ONES, in_=ones_d)
        fw.retire(["ZT%d" % m for m in range(8)], ["AT%d" % kc for kc in range(KC)])
        fw.retire(["ABT%d" % m for m in range(16)] + ["H0t", "TQ"], ["BT%d" % kc for kc in range(KC)])
        fw.retire(["SG0", "SG1", "BGLU"], ["LNT0", "LNT1", "LNT2"])
        self.layer_norm("1", "ln1_g", "ln1_b")
        self.dbg("h1", RT, ["RT%d" % kc for kc in range(KC)])


def prep_shared_rest(inp):
    sh = {}
    sh["w_glu"] = _tile_w(inp["w_glu"][0], 256)
    sh["b_glu"] = _fm(inp["b_glu"][0], 8)
    sh["w_out"] = _tile_w(inp["w_out"][0], 256)
    for n in ("ln1_g", "ln1_b", "ln2_g", "ln2_b", "ln3_g", "ln3_b"):
        sh[n] = _fm(inp[n][0], KC)
    sh["ones_bf"] = np.ones((128, 128), np.float32)
    return sh


def _cols(own, smp, halo):
    F_ = own.shape[1]
    a = own.reshape(64, 16, F_).transpose(1, 0, 2).reshape(NOWN, F_)
    b = smp.transpose(1, 0, 2).reshape(NSMP, F_)
    return np.concatenate([a, b, halo], 0).T


def prep_core_rest(inp, c):
    b, half = c // 2, c % 2
    xp = inp["x_prompt"][b]
    own = xp[half * NOWN:(half + 1) * NOWN]
    halo = xp[NOWN - 16:NOWN] if half == 1 else np.zeros((16, D), np.float32)
    xs = inp["x_sample"][16 * c:16 * c + 16]
    xr = _cols(own, xs, halo)
    d = {"xr": np.ascontiguousarray(xr.reshape(KC, 128, NCOL).transpose(1, 0, 2))}
    return d


def uncols(a):
    a = a.T
    own = a[:NOWN].reshape(16, 64, -1).transpose(1, 0, 2).reshape(NOWN, -1)
    smp = a[NOWN:NOWN + NSMP].reshape(4, 16, -1).transpose(1, 0, 2)
    return own, smp, a[NOWN + NSMP:]


class KernAttn:
    def softmax_pv(self, S, sk, np_, OT_out, okeys, vfn, vkeys, ident_rows):
        raise NotImplementedError

    def phase5(self, stop=None):
        fw = self.fw
        RT, AT, BT = self.RT, self.AT, self.BTt
        IDB, psbf = self.IDB, self.psbf
        memT_d = self.din("memT", [128, KC, NMEM])
        w_k = self.din("w_k", [8, 128, KC, 256])
        w_v = self.din("w_v", [8, 128, KC, 256])
        w_q = self.din("w_q", [8, 128, KC, 256])
        w_o = self.din("w_o", [8, 128, KC, 256])
        kTs_d = self.din("kTs", [16, 128, KC, NMEM])
        vs_d = self.din("vs", [16, 128, 2, D])
        p_mem_k = self.dout("p_mem_kT", [KC, 128, NMEM])
        p_mem_v = self.dout("p_mem_v", [2, 128, D])
        KT = self.sb("KT", [128, KC, NMEM], BF16, OFF_XT)
        VV = self.sb("VV", [128, 2, D], BF16, OFF_XT + 8192)
        fw.retire(["X"], ["KT", "VV"])
        MT = self.sb("MT", [128, KC, NMEM], BF16, OFF_BT)
        btk = ["BT%d" % kc for kc in range(KC)]
        fw.dma("pool", None, "memT", r=[], w=btk, out=MT, in_=memT_d, max_dma_last_dim=8192)
        STG = [self.sb("STG%d" % i, [128, 512], F32, OFF_BT + 8192 + 2048 * i) for i in range(2)]
        sti = [0]
        wk_ = self.wstream([(w_k[b], KC, 256) for b in range(8)] + [(w_v[b], KC, 256) for b in range(8)], ahead=2)
        for blk in range(8):
            W, wk = next(wk_)
            for mm in range(2):
                mo = 2 * blk + mm
                ps, pk = self.bank()
                for kc in range(KC):
                    fw.op("pe", "matmul", r=[wk] + btk, w=[pk], out=ps[:, 0:NMEM], lhsT=W[:, kc, mm * 128:(mm + 1) * 128],
                          rhs=MT[:, kc, :], start=(kc == 0), stop=(kc == KC - 1))
                i = sti[0] % 2
                sti[0] += 1
                self.copy("act", KT[:, mo, :], ps[:, 0:NMEM], [pk], ["KT"])
                self.copy("dve", STG[i][:, 0:NMEM], ps[:, 0:NMEM], [pk] + btk, ["STG%d" % i])
                fw.dma("sp", None, "stg%d" % i, r=["STG%d" % i], out=p_mem_k[mo], in_=STG[i][:, 0:NMEM])
        for blk in range(8):
            W, wk = next(wk_)
            for mt in range(2):
                ps, pk = self.bank()
                for kc in range(KC):
                    fw.op("pe", "matmul", r=[wk] + btk, w=[pk], out=ps[:, 0:256], lhsT=MT[:, kc, mt * 128:(mt + 1) * 128],
                          rhs=W[:, kc, :], start=(kc == 0), stop=(kc == KC - 1))
                i = sti[0] % 2
                sti[0] += 1
                self.copy("act", VV[:, mt, blk * 256:(blk + 1) * 256], ps[:, 0:256], [pk], ["VV"])
                self.copy("dve", STG[i][:, 0:256], ps[:, 0:256], [pk] + btk, ["STG%d" % i])
                fw.dma("sp", None, "stg%d" % i, r=["STG%d" % i], out=p_mem_v[mt, :, blk * 256:(blk + 1) * 256],
                       in_=STG[i][:, 0:256])
        if stop == "kv":
            return
        QT = BT
        qk = ["QT%d" % kc for kc in range(KC)]
        fw.retire(btk + ["STG0", "STG1"], qk)
        scale = float(HD) ** -0.5

        def epi_q(mo, banks):
            for (ps, pk, c0, c1) in banks:
                fw.op("act", "activation", r=[pk], w=["QT%d" % mo], out=QT[:, mo, c0:c1], in_=ps[:, 0:c1 - c0],
                      func=AF.Copy, scale=scale)

        self.linear_fm(w_q, KC, D, lambda kc, c0, c1: AT[:, kc, c0:c1],
                       lambda kc, bi=None: ["AT%d" % kc] if bi is None else ["AT%d.%d" % (kc, bi)], epi_q)
        if stop == "q":
            return
        OT = AT
        ok = ["OT%d" % kc for kc in range(KC)]
        fw.retire(["AT%d" % kc for kc in range(KC)] + ["AT%d.%d" % (kc, bi) for kc in range(KC) for bi in range(3)], ok)
        R_ = 4
        lo = self.lnt_off
        PB = [self.sb("PB%d" % i, [128, 2, 256], BF16, lo + 1024 * i) for i in range(R_)]
        SM = [self.sb("SM%d" % i, [128, 8], F32, lo + 1024 * R_ + 32 * i) for i in range(R_)]
        fw.retire(["LNT0", "LNT1", "LNT2"], ["PB%d" % i for i in range(R_)] + ["SM%d" % i for i in range(R_)])
        psb = [(self.psbf, "psbf"), (self.psbf2, "psbf2")]

        def stageA(it, i):
            np_ = it["np"]
            P = PB[i % R_][:, 0, :]
            pkey, sm = "PB%d" % (i % R_), "SM%d" % (i % R_)
            S_ = SM[i % R_]
            pS, kS = self.bank()
            for dc in range(4):
                fw.op("pe", "matmul", r=qk + it["kkeys"], w=[kS], out=pS[0:np_, 0:NMEM], lhsT=it["q"](dc),
                      rhs=it["k"](dc), start=(dc == 0), stop=(dc == 3))
            fw.op("dve", "reduce_max", r=[kS], w=[sm], out=S_[0:np_, 0:1], in_=pS[0:np_, 0:NMEM], axis=AX.X)
            self.ts("dve", S_[0:np_, 1:2], S_[0:np_, 0:1], -1.0, ALU.mult, [sm], [sm])
            fw.op("act", "activation", r=[kS, sm], w=[pkey, sm], out=P[0:np_, :], in_=pS[0:np_, 0:NMEM],
                  func=AF.Exp, bias=S_[0:np_, 1:2], accum_out=S_[0:np_, 2:3])
            fw.op("dve", "reciprocal", r=[sm], w=[sm], out=S_[0:np_, 3:4], in_=S_[0:np_, 2:3])
            self.ts("dve", P[0:np_, :], P[0:np_, :], S_[0:np_, 3:4], ALU.mult, [pkey, sm], [pkey])

        def stageB(it, i):
            np_ = it["np"]
            P, PTr = PB[i % R_][:, 0, :], PB[i % R_][:, 1, :]
            pkey = "PB%d" % (i % R_)
            pb_, pbk = psb[i % 2]
            for mt in range(2):
                fw.op("pe", "transpose", r=[pkey, "CST"], w=[pbk], out=pb_[:, mt * 128:mt * 128 + np_],
                      in_=P[0:np_, mt * 128:(mt + 1) * 128], identity=IDB[0:np_, 0:np_])
            self.copy("dve", PTr.rearrange("p (t k) -> p t k", t=2)[:, :, 0:np_],
                      pb_[:, 0:256].rearrange("p (t k) -> p t k", t=2)[:, :, 0:np_], [pbk], [pkey])

        def stageC(it, i):
            np_ = it["np"]
            PTr = PB[i % R_][:, 1, :]
            pkey = "PB%d" % (i % R_)
            pO, kO = self.bank()
            for dc in range(4):
                for mt in range(2):
                    fw.op("pe", "matmul", r=[pkey] + it["vkeys"], w=[kO], out=pO[:, dc * 128:dc * 128 + np_],
                          lhsT=it["v"](mt, dc), rhs=PTr[:, mt * 128:mt * 128 + np_], start=(mt == 0), stop=(mt == 1))
            dst, dkeys = it["o"]()
            self.copy("act", dst, pO[:, 0:512].rearrange("p (d k) -> p d k", d=4)[:, :, 0:np_], [kO], dkeys)

        pmakers, smakers = [], []
        tiles = [(128, 128 * t) for t in range(8)] + [(NHALO, C_HALO)]
        if stop == "att1":
            tiles = tiles[:1]
        if stop == "atth":
            tiles = tiles[-1:]
        for (np_, c0) in tiles:
            for h in range(NH):
                pmakers.append(lambda h=h, c0=c0, np_=np_: dict(
                    np=np_, q=lambda dc: QT[:, 4 * h + dc, c0:c0 + np_], k=lambda dc: KT[:, 4 * h + dc, :],
                    kkeys=["KT"], v=lambda mt, dc: VV[:, mt, (4 * h + dc) * 128:(4 * h + dc + 1) * 128], vkeys=["VV"],
                    o=lambda: (OT[:, 4 * h:4 * h + 4, c0:c0 + np_], ["OT%d" % (4 * h + d_) for d_ in range(4)])))
        kvq = {}

        def smp_maker(q, h):
            if q not in kvq:
                kvq[q] = (self.wload(kTs_d[q], KC, NMEM), self.wload(vs_d[q], 2, D))
            (Kq, kq), (Vq, vq) = kvq[q]
            return dict(
                np=4, q=lambda dc: QT[:, 4 * h + dc, C_SMP:C_HALO].rearrange("p (i s) -> p i s", s=16)[:, :, q],
                k=lambda dc: Kq[:, 4 * h + dc, :], kkeys=[kq],
                v=lambda mt, dc: Vq[:, mt, (4 * h + dc) * 128:(4 * h + dc + 1) * 128], vkeys=[vq],
                o=lambda: (OT[:, 4 * h:4 * h + 4, C_SMP:C_HALO].rearrange("p d (i s) -> p d i s", s=16)[:, :, :, q],
                           ["OT%d" % (4 * h + d_) for d_ in range(4)]))

        if stop not in ("att1", "atth", "attp"):
            for q in range(16 if stop != "atts" else 1):
                for h in range(NH):
                    smakers.append(lambda q=q, h=h: smp_maker(q, h))
        makers = []
        np_i = 0
        nq = len(smakers) // NH
        for q in range(nq):
            makers += smakers[NH * q:NH * q + NH]
            tgt = (len(pmakers) * (q + 1)) // nq
            makers += pmakers[np_i:tgt]
            np_i = tgt
        makers += pmakers[np_i:]
        n_it = len(makers)
        its = {}
        for i in range(n_it + 2):
            if i < n_it:
                its[i] = makers[i]()
                stageA(its[i], i)
            if 0 <= i - 1 < n_it:
                stageB(its[i - 1], i - 1)
            if 0 <= i - 2 < n_it:
                stageC(its[i - 2], i - 2)
                del its[i - 2]
        if stop in ("att1", "atth", "attp"):
            return
        if stop == "atts":
            return
        def epi_o(mo, banks):
            for (ps, pk, c0, c1) in banks:
                fw.op("dve", "scalar_tensor_tensor", r=[pk, "RT%d" % mo], w=["RT%d" % mo], out=RT[:, mo, c0:c1],
                      in0=RT[:, mo, c0:c1], scalar=ALPHA, in1=ps[:, 0:c1 - c0], op0=ALU.mult, op1=ALU.add)

        self.linear_fm(w_o, KC, D, lambda kc, c0, c1: OT[:, kc, c0:c1], lambda kc: ["OT%d" % kc], epi_o)
        fw.retire(ok, ["AT%d" % kc for kc in range(KC)])
        fw.retire(qk, btk)
        fw.retire(["PB%d" % i for i in range(4)] + ["SM%d" % i for i in range(4)], ["LNT0", "LNT1", "LNT2"])
        self.layer_norm("2", "ln2_g", "ln2_b")
        self.dbg("h2", RT, ["RT%d" % kc for kc in range(KC)])


def prep_shared_attn(inp):
    sh = {}
    for n in ("w_k", "w_v", "w_q", "w_o"):
        sh[n] = _tile_w(inp[n][0], 256)
    return sh


def prep_core_attn(inp, c):
    b = c // 2
    d = {}
    mem = inp["mem_prompt"][b]
    d["memT"] = np.ascontiguousarray(mem.T.reshape(KC, 128, NMEM).transpose(1, 0, 2))
    ck = inp["cache_mem_k"][0, 16 * c:16 * c + 16].reshape(16, NMEM, D)
    d["kTs"] = np.ascontiguousarray(ck.transpose(0, 2, 1).reshape(16, KC, 128, NMEM).transpose(0, 2, 1, 3))
    cv = inp["cache_mem_v"][0, 16 * c:16 * c + 16].reshape(16, 2, 128, D)
    d["vs"] = np.ascontiguousarray(cv.transpose(0, 2, 1, 3))
    return d


NGRP = 6


class KernFFN:
    def phase6(self):
        fw = self.fw
        RT, AT = self.RT, self.AT
        w_gate = self.din("w_gate", [22, 128, KC, 256])
        w_up = self.din("w_up", [22, 128, KC, 256])
        w_down = self.din("w_down", [NGRP, 8, 128, 8, 256])
        cw_d = self.din("convw", [128, FC, 4])
        convT = self.din("convT", [FC, 128, 16, 2])
        flag_d = self.din("flag", [128, 1])
        p_conv = self.dout("p_conv", [128, FC, 2])
        s_conv = self.dout("s_conv", [128, FC, 16, 2])
        yT = self.dout("yT", [128, KC, NCOL])
        fw.retire(["KT", "VV"], ["GE0", "GE1", "CA", "GS", "AS"])
        o = OFF_XT
        GE = [self.sb("GE%d" % i, [128, 16, 65], F32, o + 4160 * i) for i in range(2)]; o += 8320
        CA = self.sb("CA", [128, 16, 65], F32, o); o += 4160
        GS = self.sb("GS", [128, 16, 6], F32, o); o += 384
        AS = self.sb("AS", [128, 16, 4], F32, o); o += 256
        CW = self.sb("CW", [128, FC, 4], F32, o); o += 704
        FL = self.sb("FL", [128, 1], F32, o); o += 32
        assert o <= OFF_XT + 16384
        lnt0_off = self.lnt_off
        SCV = self.sb("SCV", [128, FC, 16, 2], F32, lnt0_off)
        PCV = self.sb("PCV", [128, FC, 2], F32, lnt0_off + 5632)
        SL = self.sb("SL", [128, NCOL], BF16, lnt0_off + 5632 + 352)
        UE = [self.sb("UE%d" % i, [128, NCOL], BF16, lnt0_off + 2 * 4416 + 2208 * i) for i in range(2)]
        fw.retire(["LNT0", "LNT1", "LNT2"], ["SCV", "PCV", "SL", "UE0", "UE1"])
        fw.dma("sp", None, "cw", r=["GE0"], w=["CW"], out=CW, in_=cw_d)
        fw.dma("sp", None, "fl", r=["GE0"], w=["FL"], out=FL, in_=flag_d)
        ACTG = [self.sb("ACTG%d" % i, [128, 8, NCOL], BF16, OFF_BT + 17664 * i) for i in range(2)]
        btk = ["BT%d" % kc for kc in range(KC)]
        fw.retire(btk, ["ACTG0", "ACTG1"])
        for i in range(2):
            fw.op("pool", "memset", w=["ACTG%d" % i], ap=ACTG[i][:, :, C_HALO:NCOL], constant=0.0)
        atk = ["AT%d" % kc for kc in range(KC)]

        gu_specs = []
        for blk in range(22):
            gu_specs += [(w_gate[blk], KC, 256), (w_up[blk], KC, 256)]
        fw.retire(["ws3"], ["ws3a", "ws3b"])
        gus = self.wstream(gu_specs, ahead=1, ring="gu")

        def gate_up(gi):
            nf = 8 if gi < NGRP - 1 else FC - 8 * (NGRP - 1)
            for b2 in range(nf // 2):
                blk = 4 * gi + b2
                Wg, wgk = next(gus)
                Wu, wuk = next(gus)
                for mm in range(2):
                    f = 2 * blk + mm
                    fl = f - 8 * gi
                    ge = GE[f % 2]
                    gek = "GE%d" % (f % 2)
                    ue = UE[f % 2]
                    uek = "UE%d" % (f % 2)
                    gb = [self.bank() for _ in BLOCKS]
                    for bi, ((ps, pk), (c0, c1)) in enumerate(zip(gb, BLOCKS)):
                        for kc in range(KC):
                            fw.op("pe", "matmul", r=[wgk, "AT%d.%d" % (kc, bi)], w=[pk], out=ps[:, 0:c1 - c0],
                                  lhsT=Wg[:, kc, mm * 128:(mm + 1) * 128], rhs=AT[:, kc, c0:c1],
                                  start=(kc == 0), stop=(kc == KC - 1))
                    self.copy("act", ge[:, 0:8, 1:65], gb[0][0].rearrange("p (s k) -> p s k", k=64), [gb[0][1]], [gek])
                    self.copy("act", ge[:, 8:16, 1:65], gb[1][0].rearrange("p (s k) -> p s k", k=64), [gb[1][1]], [gek])
                    fw.op("act", "activation", r=[gb[2][1], "FL"], w=[gek], out=ge[:, :, 0], in_=gb[2][0][:, 64:80],
                          func=AF.Copy, scale=FL[:, 0:1])
                    self.copy("dve", GS[:, :, 2:6], gb[2][0][:, 0:64].rearrange("p (i q) -> p q i", q=16),
                              [gb[2][1]], ["GS"])
                    fw.dma("sp", None, "convh", w=["GS"], out=GS[:, :, 0:2], in_=convT[f])
                    ub = [self.bank() for _ in BLOCKS]
                    for bi, ((ps, pk), (c0, c1)) in enumerate(zip(ub, BLOCKS)):
                        for kc in range(KC):
                            fw.op("pe", "matmul", r=[wuk, "AT%d.%d" % (kc, bi)], w=[pk], out=ps[:, 0:c1 - c0],
                                  lhsT=Wu[:, kc, mm * 128:(mm + 1) * 128], rhs=AT[:, kc, c0:c1],
                                  start=(kc == 0), stop=(kc == KC - 1))
                    for (ps, pk), (c0, c1) in zip(ub, BLOCKS):
                        self.copy("act" if c0 == 0 else "dve", ue[:, c0:c1], ps[:, 0:c1 - c0], [pk], [uek])
                    self.copy("act", PCV[:, f, :], ge[:, 14:16, 64], [gek], ["PCV"])
                    self.copy("act", SCV[:, f, :, :], GS[:, :, 4:6], ["GS"], ["SCV"])
                    w0, w1, w2, bb = (CW[:, f, j:j + 1] for j in range(4))
                    fw.op("act", "activation", r=[gek, "CW"], w=["CA"], out=CA, in_=ge, func=AF.Identity, scale=w2, bias=bb)
                    stt = lambda out, in0, sc, in1, r, w: fw.op("dve", "scalar_tensor_tensor", r=r, w=w, out=out, in0=in0,
                                                               scalar=sc, in1=in1, op0=ALU.mult, op1=ALU.add)
                    stt(CA[:, 1:16, :], ge[:, 0:15, :], w1, CA[:, 1:16, :], [gek, "CA", "CW"], ["CA"])
                    stt(CA[:, 0, 1:65], ge[:, 15, 0:64], w1, CA[:, 0, 1:65], [gek, "CA", "CW"], ["CA"])
                    stt(CA[:, 2:16, :], ge[:, 0:14, :], w0, CA[:, 2:16, :], [gek, "CA", "CW"], ["CA"])
                    stt(CA[:, 0:2, 1:65], ge[:, 14:16, 0:64], w0, CA[:, 0:2, 1:65], [gek, "CA", "CW"], ["CA"])
                    fw.op("act", "activation", r=["GS", "CW"], w=["AS"], out=AS, in_=GS[:, :, 2:6], func=AF.Identity,
                          scale=w2, bias=bb)
                    stt(AS, GS[:, :, 1:5], w1, AS, ["GS", "AS", "CW"], ["AS"])
                    stt(AS, GS[:, :, 0:4], w0, AS, ["GS", "AS", "CW"], ["AS"])
                    fw.op("act", "activation", r=["CA"], w=["SL"], out=SL[:, 0:NOWN].rearrange("p (s k) -> p s k", k=64),
                          in_=CA[:, :, 1:65], func=AF.Silu)
                    fw.op("act", "activation", r=["AS"], w=["SL"],
                          out=SL[:, C_SMP:C_HALO].rearrange("p (i q) -> p q i", q=16), in_=AS, func=AF.Silu)
                    self.tt("dve", ACTG[gi % 2][:, fl, 0:C_HALO], SL[:, 0:C_HALO], ue[:, 0:C_HALO], ALU.mult,
                            ["SL", uek], ["ACTG%d" % (gi % 2)])

        def down(gi):
            nf = 8 if gi < NGRP - 1 else FC - 8 * (NGRP - 1)
            A = ACTG[gi % 2]
            ak = "ACTG%d" % (gi % 2)
            wds = self.wstream([(w_down[gi, nb][:, 0:nf, :], nf, 256) for nb in range(8)], ahead=1, ring="dn")
            for nb in range(8):
                W, wk = next(wds)
                for mm in range(2):
                    mo = 2 * nb + mm
                    banks = [self.bank() for _ in BLOCKS]
                    for fl in range(nf):
                        for (ps, pk), (c0, c1) in zip(banks, BLOCKS):
                            fw.op("pe", "matmul", r=[wk, ak], w=[pk], out=ps[:, 0:c1 - c0],
                                  lhsT=W[:, fl, mm * 128:(mm + 1) * 128], rhs=A[:, fl, c0:c1],
                                  start=(fl == 0), stop=(fl == nf - 1))
                    for (ps, pk), (c0, c1) in zip(banks, BLOCKS):
                        if gi == 0:
                            fw.op("dve", "scalar_tensor_tensor", r=[pk, "RT%d" % mo], w=["RT%d" % mo],
                                  out=RT[:, mo, c0:c1], in0=RT[:, mo, c0:c1], scalar=ALPHA, in1=ps[:, 0:c1 - c0],
                                  op0=ALU.mult, op1=ALU.add)
                        else:
                            self.tt("dve", RT[:, mo, c0:c1], RT[:, mo, c0:c1], ps[:, 0:c1 - c0], ALU.add,
                                    [pk, "RT%d" % mo], ["RT%d" % mo])

        gate_up(0)
        for gi in range(NGRP):
            if gi + 1 < NGRP:
                gate_up(gi + 1)
            down(gi)
        fw.dma("sp", None, "pcv", r=["PCV"], out=p_conv, in_=PCV)
        fw.dma("sp", None, "scv", r=["SCV"], out=s_conv, in_=SCV)
        fw.retire(["ACTG0", "ACTG1"], btk)
        fw.retire(["SCV", "PCV", "SL", "UE0", "UE1"], ["LNT0", "LNT1", "LNT2"])
        self.layer_norm("3", "ln3_g", "ln3_b", last=True)
        self.dbg("yT", RT, ["RT%d" % kc for kc in range(KC)])
        for q in range(4):
            fw.dma("sp", None, "yT%d" % q, r=["RT%d" % kc for kc in range(4 * q, 4 * q + 4)],
                   out=yT[:, 4 * q:4 * q + 4, :], in_=RT[:, 4 * q:4 * q + 4, :])


def prep_shared_ffn(inp):
    sh = {}
    sh["w_gate"] = _tile_w(inp["w_gate"][0], 256)
    sh["w_up"] = _tile_w(inp["w_up"][0], 256)
    wd = inp["w_down"][0].reshape(FC, 128, 8, 256)
    wdp = np.zeros((NGRP * 8, 128, 8, 256), np.float32)
    wdp[:FC] = wd
    sh["w_down"] = np.ascontiguousarray(wdp.reshape(NGRP, 8, 128, 8, 256).transpose(0, 3, 2, 1, 4))
    cw = np.concatenate([inp["conv_w"][0], inp["conv_b"][0][None]], 0)
    sh["convw"] = np.ascontiguousarray(cw.reshape(4, FC, 128).transpose(2, 1, 0))
    return sh


def prep_core_ffn(inp, c):
    d = {}
    sc = inp["state_conv"][0, 16 * c:16 * c + 16]
    d["convT"] = np.ascontiguousarray(sc.transpose(2, 0, 1).reshape(FC, 128, 16, 2))
    d["flag"] = np.full((128, 1), float(c % 2), np.float32)
    return d


class Kern(KernFFN, KernAttn, KernRest, KernSSM2, KernSSM, Kern0):
    pass


_BUILT = {}


def build():
    if "kb" not in _BUILT:
        kb = Kern()
        kb.setup()
        kb.phase1()
        kb.phase2()
        kb.phase3_tables()
        kb.phase3_main()
        kb.phase3_glu()
        kb.phase4()
        kb.phase5()
        kb.phase6()
        kb.fw.emit()
        _BUILT["kb"] = kb
    return _BUILT["kb"]


def kernel(**inputs):
    inp = {k: np.asarray(v) for k, v in inputs.items()}
    kb = build()
    sh = {}
    for f in (prep_shared, prep_shared_ssm, prep_shared_rest, prep_shared_attn, prep_shared_ffn):
        sh.update(f(inp))
    in_maps = []
    for c in range(NCORES):
        d = dict(sh)
        for f in (prep_core, prep_core_ssm, prep_core_rest, prep_core_attn, prep_core_ffn):
            d.update(f(inp, c))
        in_maps.append({k: np.ascontiguousarray(v, dtype=np.float32) for k, v in d.items() if k in kb.ins})
    res = run_bass_kernel_spmd(kb.nc, in_maps, core_ids=list(range(NCORES))).results
    B, S = 4, 2048
    y_p = np.zeros((B, S, D), np.float32)
    y_s = np.zeros((128, 4, D), np.float32)
    p_pool = np.zeros((1, B, 15, DP), np.float32)
    p_re = np.zeros((1, B, G, NST), np.float32)
    p_im = np.zeros((1, B, G, NST), np.float32)
    p_conv = np.zeros((1, B, 2, DFF), np.float32)
    p_mk = np.zeros((1, B, NMEM, NH, HD), np.float32)
    p_mv = np.zeros((1, B, NMEM, NH, HD), np.float32)
    s_pool = np.zeros((1, 128, 15, DP), np.float32)
    s_re = np.zeros((1, 128, G, NST), np.float32)
    s_im = np.zeros((1, 128, G, NST), np.float32)
    s_conv = np.zeros((1, 128, 2, DFF), np.float32)
    for c in range(NCORES):
        o = {k: np.asarray(v) for k, v in res[c].items()}
        b, half = c // 2, c % 2
        yT = o["yT"].transpose(1, 0, 2).reshape(D, NCOL)
        own, smp, _ = uncols(yT)
        y_p[b, half * NOWN:(half + 1) * NOWN] = own
        y_s[16 * c:16 * c + 16] = smp
        qs = slice(16 * c, 16 * c + 16)
        s_pool[0, qs] = o["s_pool"].transpose(2, 3, 0, 1).reshape(16, 15, DP)
        ss = o["s_ssm"].reshape(2, 64, 2, 32, 16)
        s_re[0, qs] = ss[:, :, 0].transpose(3, 2, 0, 1).reshape(16, G, NST)
        s_im[0, qs] = ss[:, :, 1].transpose(3, 2, 0, 1).reshape(16, G, NST)
        s_conv[0, qs] = o["s_conv"].transpose(2, 3, 1, 0).reshape(16, 2, DFF)
        if half == 1:
            p_pool[0, b] = o["p_pool"].transpose(2, 1, 0).reshape(15, DP)
            ps_ = o["p_ssm"].reshape(2, 64, 2, 32)
            p_re[0, b] = ps_[:, :, 0].transpose(2, 0, 1).reshape(G, NST)
            p_im[0, b] = ps_[:, :, 1].transpose(2, 0, 1).reshape(G, NST)
            p_conv[0, b] = o["p_conv"].transpose(2, 1, 0).reshape(2, DFF)
            p_mk[0, b] = o["p_mem_kT"].reshape(D, NMEM).T.reshape(NMEM, NH, HD)
            p_mv[0, b] = o["p_mem_v"].reshape(NMEM, D).reshape(NMEM, NH, HD)
    return (y_p, y_s, p_pool, p_re, p_im, p_conv, p_mk, p_mv, s_pool, s_re, s_im, s_conv)
```

```python
import numpy as np
import contextlib
import concourse.bass as bass
import concourse.mybir as mybir
from concourse.bass_utils import run_bass_kernel_spmd

F32 = mybir.dt.float32
BF16 = mybir.dt.bfloat16
AF = mybir.ActivationFunctionType
ALU = mybir.AluOpType
AX = mybir.AxisListType

NCORES = 8
D = 2048
KC = 16
DP = 1024
G = 64
NST = 64
CH = 16
DFF = 5632
FC = 44
NMEM = 256
NH = 4
HD = 512
NOWN = 1024
NSMP = 64
NHALO = 16
NCOL = NOWN + NSMP + NHALO
C_SMP = NOWN
C_HALO = NOWN + NSMP
ALPHA = 2.0 ** 0.25
EPS = 1e-5
BLOCKS = ((0, 512), (512, 1024), (1024, NCOL))
ENGS = ("pe", "act", "dve", "pool", "sp")
SB_BASE = 16640


class _Op:
    __slots__ = ("eng", "fn", "deps", "is_dma", "semkey", "sig", "idx", "has_dep")

    def __init__(self, eng, fn, is_dma, semkey):
        self.eng, self.fn, self.is_dma, self.semkey = eng, fn, is_dma, semkey
        self.deps = set()
        self.sig = None
        self.has_dep = False


class FW:
    def __init__(self, nc):
        self.nc = nc
        self.ops = []
        self.lastw = {}
        self.readers = {}

    def _add(self, op, r, w):
        op.idx = len(self.ops)
        r = list(r)
        w = list(w) + [k for k in r if k.startswith("ps")]
        r = [k for k in r if not k.startswith("ps")]
        for k in r:
            lw = self.lastw.get(k)
            if lw is not None:
                op.deps.update(lw)
        for k in w:
            lw = self.lastw.get(k)
            if lw is not None:
                op.deps.update(lw)
            op.deps.update(self.readers.get(k, ()))
        op.deps.discard(op.idx)
        if op.eng == "pe" and not op.is_dma:
            op.deps = {d for d in op.deps if not (self.ops[d].eng == "pe" and not self.ops[d].is_dma)}
        for k in r:
            lst = self.readers.setdefault(k, [])
            if not op.is_dma:
                lst[:] = [i for i in lst if self.ops[i].is_dma or self.ops[i].eng != op.eng]
            lst.append(op.idx)
        for k in w:
            self.lastw[k] = [op.idx]
            self.readers[k] = []
        self.ops.append(op)
        return op

    def op(self, eng, fn, r=(), w=(), **kw):
        if isinstance(fn, str):
            name = fn
            fn = lambda e, name=name, kw=kw: getattr(e, name)(**kw)
        return self._add(_Op(eng, fn, False, None), r, w)

    def dma(self, eng, fn, semkey, r=(), w=(), **kw):
        if fn is None:
            fn = lambda e, kw=kw: e.dma_start(**kw)
        return self._add(_Op(eng, fn, True, semkey), r, w)

    def retire(self, old, new):
        pend = []
        for k in old:
            pend += self.lastw.get(k, []) + self.readers.get(k, [])
        for k in new:
            self.lastw[k] = sorted(set(self.lastw.get(k, []) + pend))

    def emit(self, final_wait_eng="sp"):
        nc, ops = self.nc, self.ops
        for o in ops:
            for d in o.deps:
                ops[d].has_dep = True
        dma_keys = []
        for o in ops:
            if o.is_dma and o.semkey not in dma_keys:
                dma_keys.append(o.semkey)
        with contextlib.ExitStack() as st:
            esem = {e: st.enter_context(nc.semaphore("s_" + e)) for e in ENGS}
            dsem = {k: st.enter_context(nc.semaphore("d_%d" % i)) for i, k in enumerate(dma_keys)}
            ecnt = {e: 0 for e in ENGS}
            dcnt = {k: 0 for k in dma_keys}
            for o in ops:
                if o.is_dma:
                    dcnt[o.semkey] += 16
                    o.sig = (dsem[o.semkey], dcnt[o.semkey])
                elif o.has_dep:
                    ecnt[o.eng] += 1
                    o.sig = (esem[o.eng], ecnt[o.eng])
            block = st.enter_context(nc.Block())
            handles = {"pe": block.tensor, "act": block.scalar, "dve": block.vector,
                       "pool": block.gpsimd, "sp": block.sync}
            for ename in ENGS:
                mine = [o for o in ops if o.eng == ename]

                def body(e, mine=mine, ename=ename):
                    waited = {}
                    for o in mine:
                        need = {}
                        for d in o.deps:
                            sem, val = ops[d].sig
                            key = id(sem)
                            if waited.get(key, 0) >= val:
                                continue
                            if key not in need or need[key][1] < val:
                                need[key] = (sem, val)
                        for key, (sem, val) in need.items():
                            e.wait_ge(sem, val)
                            waited[key] = val
                        ins = o.fn(e)
                        if o.sig is not None:
                            ins.then_inc(o.sig[0], 16 if o.is_dma else 1)
                    if ename == final_wait_eng:
                        for k in dma_keys:
                            e.wait_ge(dsem[k], dcnt[k])
                handles[ename](body)
        self.counts = (ecnt, dcnt)


class Builder:
    def __init__(self, debug=()):
        self.nc = nc = bass.Bass("TRN2", target_bir_lowering=False)
        self.fw = FW(nc)
        self.debug = set(debug)
        self.ins = {}
        self.outs = {}
        self.ev_i = 0
        self.ps_i = 0
        self.psum = [nc.alloc_psum_tensor("psb%d" % i, [128, 512], F32).ap() for i in range(6)]
        self.psbf = nc.alloc_psum_tensor("psbf", [128, 1024], BF16).ap()
        self.psbf2 = nc.alloc_psum_tensor("psbf2", [128, 1024], BF16).ap()

    def din(self, name, shape, dt=F32):
        self.ins[name] = self.nc.dram_tensor(name, list(shape), dt, kind="ExternalInput").ap()
        return self.ins[name]

    def dout(self, name, shape, dt=F32):
        self.outs[name] = self.nc.dram_tensor(name, list(shape), dt, kind="ExternalOutput").ap()
        return self.outs[name]

    def sb(self, name, shape, dt, off):
        assert off % 32 == 0, (name, off)
        nbytes = int(np.prod(shape[1:])) * (2 if dt == BF16 else 4)
        assert off + nbytes <= 212736, (name, off, nbytes)
        return self.nc.alloc_sbuf_tensor_at(name, list(shape), dt, offset=SB_BASE + off).ap()

    def bank(self):
        i = self.ps_i % 6
        self.ps_i += 1
        return self.psum[i], "ps%d" % i

    def evac_eng(self):
        self.ev_i += 1
        return "act" if self.ev_i % 2 else "dve"

    def copy(self, eng, out, in_, r, w):
        if eng == "act":
            self.fw.op("act", lambda e: e.activation(out=out, in_=in_, func=AF.Copy), r=r, w=w)
        else:
            self.fw.op(eng, lambda e: e.tensor_copy(out=out, in_=in_), r=r, w=w)

    def dbg(self, name, ap, keys, dt=F32):
        if name not in self.debug:
            return
        o = self.dout("dbg_" + name, list(ap.shape), dt)
        self.fw.dma("sp", lambda e: e.dma_start(out=o, in_=ap), "dbg_" + name, r=keys)


OFF_RT = 0
OFF_UU = 0
OFF_SP = 32768
OFF_PT = 32768
OFF_E = 32768 + 17664
OFF_AT = 70656
OFF_BT = 70656 + 35328
OFF_WS = 141312
OFF_XT = 141312 + 32768
OFF_MISC = 190464
WS_SLOT = 8192
NWS = 4


class Kern0(Builder):
    def setup(self):
        nc = self.nc
        self.ws_i = 0
        self.misc_off = OFF_MISC
        self.ws = [self.sb("ws%d" % i, [128, WS_SLOT // 2], BF16, OFF_WS + i * WS_SLOT) for i in range(NWS)]

    def misc(self, name, shape, dt):
        nbytes = int(np.prod(shape[1:])) * (2 if dt == BF16 else 4)
        off = self.misc_off
        self.misc_off += (nbytes + 31) // 32 * 32
        return self.sb(name, shape, dt, off)

    def wload(self, src, kcw, ncols, ring="main"):
        if not hasattr(self, "rings"):
            h = WS_SLOT // 4
            self.rings = {"main": [(self.ws[i], "ws%d" % i) for i in range(NWS)],
                          "gu": [(self.ws[i], "ws%d" % i) for i in range(3)],
                          "dn": [(self.ws[3][:, 0:h], "ws3a"), (self.ws[3][:, h:2 * h], "ws3b")]}
            self.ring_i = {k: 0 for k in self.rings}
        slots = self.rings[ring]
        buf, key = slots[self.ring_i[ring] % len(slots)]
        self.ring_i[ring] += 1
        dst = buf[:, 0:kcw * ncols].rearrange("p (k n) -> p k n", n=ncols)
        self.fw.dma("pool", lambda e: e.dma_start(out=dst, in_=src, max_dma_last_dim=8192), key, w=[key])
        return dst, key

    def wstream(self, specs, ahead=1, ring="main"):
        specs = list(specs)
        q = []
        nxt = 0
        for i in range(len(specs)):
            while nxt < len(specs) and nxt <= i + ahead:
                q.append(self.wload(*specs[nxt], ring=ring))
                nxt += 1
            yield q.pop(0)

    def phase1(self):
        fw = self.fw
        xa = self.din("xa", [128, KC, 16, 128])
        xs = self.din("xs", [128, KC, NSMP])
        w_in_s = self.din("w_in_s", [4, 128, KC, 256])
        w_in_p = self.din("w_in_p", [4, 128, KC, 256])
        poolT = self.din("poolT", [8, 128, 16, 15])
        invc = self.din("invc", [128, 8, 16])
        XA = self.XA = self.sb("XA", [128, KC, 16, 128], BF16, OFF_AT)
        XS = self.XS = self.sb("XS", [128, KC, NSMP], BF16, OFF_XT)
        UU = self.UU = self.sb("UU", [128, G, 16, CH], BF16, OFF_UU)
        UUs = self.UUs = self.misc("UUs", [16, G, 4, CH], BF16)
        PT = self.PT = self.sb("PT", [128, 8, NCOL], BF16, OFF_PT)
        INVC = self.sb("INVC", [128, 8, 16], F32, OFF_XT + 2048)
        for q in range(4):
            fw.dma("pool", lambda e, q=q: e.dma_start(out=XA[:, 4 * q:4 * q + 4], in_=xa[:, 4 * q:4 * q + 4],
                                                       max_dma_last_dim=8192), "XA%d" % q, w=["XA%d" % q])
        fw.dma("pool", lambda e: e.dma_start(out=XS, in_=xs, max_dma_last_dim=8192), "XS", w=["XS"])
        fw.dma("sp", lambda e: e.dma_start(out=INVC, in_=invc), "INVC", w=["INVC"])
        xak = ["XA%d" % q for q in range(4)]
        ws_ = self.wstream([(w_in_s[b], KC, 256) for b in range(4)])
        for blk in range(4):
            W, wk = next(ws_)
            for s in range(16):
                ps, pk = self.bank()
                for kc in range(KC):
                    fw.op("pe", lambda e, ps=ps, kc=kc, s=s, W=W: e.matmul(
                        ps[:, 0:256], XA[:, kc, s, :], W[:, kc, :], start=(kc == 0), stop=(kc == KC - 1)),
                        r=[wk] + xak, w=[pk])
                self.copy(self.evac_eng(), UU[:, blk * 16:(blk + 1) * 16, s, :],
                          ps[:, 0:256].rearrange("p (g c) -> p g c", c=CH), r=[pk], w=["UU"])
            for i in range(4):
                ps, pk = self.bank()
                for kc in range(KC):
                    fw.op("pe", lambda e, ps=ps, kc=kc, i=i, W=W: e.matmul(
                        ps[0:16, 0:256], XS[:, kc, i * 16:(i + 1) * 16], W[:, kc, :], start=(kc == 0),
                        stop=(kc == KC - 1)), r=[wk, "XS"], w=[pk])
                self.copy(self.evac_eng(), UUs[:, blk * 16:(blk + 1) * 16, i, :],
                          ps[0:16, 0:256].rearrange("p (g c) -> p g c", c=CH), r=[pk], w=["UUs"])
        self.dbg("UU", UU, ["UU"], BF16)
        self.dbg("UUs", UUs, ["UUs"], BF16)
        E = [self.sb("E%d" % i, [128, 16, 66], F32, OFF_E + i * 4224) for i in range(4)]
        Es = [self.sb("Es%d" % i, [128, 16, 19], F32, OFF_XT + 3072 + i * 1216) for i in range(4)]
        p_pool = self.dout("p_pool", [128, 8, 15])
        PPS = self.sb("PPS", [128, 8, 15], F32, OFF_XT + 2560)
        s_pool = self.dout("s_pool", [8, 128, 16, 15])
        for i in range(4):
            fw.op("dve", lambda e, i=i: e.memset(E[i], 0.0), w=["E%d" % i])
            fw.op("dve", lambda e, i=i: e.memset(Es[i], 0.0), w=["Es%d" % i])
        wp_ = self.wstream([(w_in_p[b], KC, 256) for b in range(4)])
        for blk in range(4):
            W, wk = next(wp_)
            for mm in range(2):
                m = 2 * blk + mm
                grp = m // 2
                nst = grp + 1
                pa, ka = self.bank()
                pb, kb = self.bank()
                pc, kc_ = self.bank()
                lw = lambda kc, W=W, mm=mm: W[:, kc, mm * 128:(mm + 1) * 128]
                for kc in range(KC):
                    st, sp_ = (kc == 0), (kc == KC - 1)
                    fw.op("pe", lambda e, kc=kc, st=st, sp_=sp_, lw=lw, pa=pa: e.matmul(
                        pa, lw(kc), XA[:, kc, 0:8, 64:128], start=st, stop=sp_), r=[wk] + xak, w=[ka])
                    fw.op("pe", lambda e, kc=kc, st=st, sp_=sp_, lw=lw, pb=pb: e.matmul(
                        pb, lw(kc), XA[:, kc, 8:16, 64:128], start=st, stop=sp_), r=[wk] + xak, w=[kb])
                for kc in range(KC):
                    fw.op("pe", lambda e, kc=kc, lw=lw, pc=pc: e.matmul(
                        pc[:, 0:32], lw(kc), XA[:, kc, :, 62:64], start=(kc == 0), stop=(kc == KC - 1)),
                        r=[wk] + xak, w=[kc_])
                for kc in range(KC):
                    fw.op("pe", lambda e, kc=kc, lw=lw, pc=pc: e.matmul(
                        pc[:, 32:96], lw(kc), XS[:, kc, :], start=(kc == 0), stop=(kc == KC - 1)),
                        r=[wk, "XS"], w=[kc_])
                b0 = m % 2
                e0, es0 = E[b0], Es[b0]
                E0k, Es0k = "E%d" % b0, "Es%d" % b0
                self.copy("act", e0[:, 0:8, 2:66], pa.rearrange("p (s k) -> p s k", k=64), r=[ka], w=[E0k])
                self.copy("act", e0[:, 8:16, 2:66], pb.rearrange("p (s k) -> p s k", k=64), r=[kb], w=[E0k])
                self.copy("dve", e0[:, :, 0:2], pc[:, 0:32].rearrange("p (s k) -> p s k", k=2), r=[kc_], w=[E0k])
                self.copy("dve", es0[:, :, 15:19], pc[:, 32:96].rearrange("p (i q) -> p q i", q=16),
                          r=[kc_], w=[Es0k])
                fw.dma("sp", lambda e, m=m, es0=es0: e.dma_start(out=es0[:, :, 0:15], in_=poolT[m]),
                       "Es0h%d" % b0, w=[Es0k])
                if m == 7:
                    self.dbg("E0", e0, [E0k])
                    self.dbg("Es0", es0, [Es0k])
                self.copy("dve", PPS[:, m, :], e0[:, 1:16, 65], r=[E0k], w=["PPS"])
                fw.dma("sp", lambda e, m=m, es0=es0: e.dma_start(out=s_pool[m], in_=es0[:, :, 4:19]),
                       "spool%d" % b0, r=[Es0k])
                cur, curs, ci = e0, es0, b0
                for sti in range(nst):
                    d = 1 << sti
                    ni = 2 if ci != 2 else 3
                    nx, nxs = E[ni], Es[ni]
                    eng = "dve" if (m % 2 == 0) else "pool"
                    fw.op(eng, lambda e, nx=nx, cur=cur, d=d: e.tensor_tensor(
                        out=nx[:, d:16, :], in0=cur[:, d:16, :], in1=cur[:, 0:16 - d, :], op=ALU.add),
                        r=["E%d" % ci], w=["E%d" % ni])
                    fw.op(eng, lambda e, nx=nx, cur=cur, d=d: e.tensor_tensor(
                        out=nx[:, 0:d, 1:66], in0=cur[:, 0:d, 1:66], in1=cur[:, 16 - d:16, 0:65], op=ALU.add),
                        r=["E%d" % ci], w=["E%d" % ni])
                    fw.op(eng, lambda e, nxs=nxs, curs=curs, d=d: e.tensor_tensor(
                        out=nxs[:, :, d:19], in0=curs[:, :, d:19], in1=curs[:, :, 0:19 - d], op=ALU.add),
                        r=["Es%d" % ci], w=["Es%d" % ni])
                    cur, curs, ci = nx, nxs, ni
                rw = 1.0 / float(1 << nst)
                rk = ["E%d" % ci, E0k, "Es%d" % ci, Es0k]
                pk_ = "PT%d" % m
                fw.op("dve", lambda e, cur=cur, e0=e0, m=m, rw=rw: e.scalar_tensor_tensor(
                    out=PT[:, m, 0:NOWN].rearrange("p (s k) -> p s k", k=64), in0=cur[:, :, 2:66], scalar=rw,
                    in1=e0[:, :, 2:66], op0=ALU.mult, op1=ALU.subtract), r=rk, w=[pk_])
                fw.op("dve", lambda e, cur=cur, e0=e0, m=m, rw=rw: e.scalar_tensor_tensor(
                    out=PT[:, m, C_HALO:NCOL], in0=cur[:, :, 1], scalar=rw,
                    in1=e0[:, :, 1], op0=ALU.mult, op1=ALU.subtract), r=rk, w=[pk_])
                fw.op("dve", lambda e, curs=curs, es0=es0, m=m, rw=rw: e.scalar_tensor_tensor(
                    out=PT[:, m, C_SMP:C_HALO].rearrange("p (i q) -> p q i", q=16), in0=curs[:, :, 15:19], scalar=rw,
                    in1=es0[:, :, 15:19], op0=ALU.mult, op1=ALU.subtract), r=rk, w=[pk_])
                tmpc = Es[ni][:, :, 0]
                fw.op("dve", lambda e, cur=cur, m=m, tmpc=tmpc: e.tensor_tensor(
                    out=tmpc, in0=cur[:, :, 2], in1=INVC[:, m, :], op=ALU.mult),
                    r=rk + ["INVC"], w=["Es%d" % ni])
                fw.op("dve", lambda e, e0=e0, m=m, tmpc=tmpc: e.tensor_tensor(
                    out=PT[:, m, 0:NOWN].rearrange("p (s k) -> p s k", k=64)[:, :, 0], in0=tmpc, in1=e0[:, :, 2],
                    op=ALU.subtract), r=["Es%d" % ni, E0k], w=[pk_])
        fw.dma("sp", lambda e: e.dma_start(out=p_pool, in_=PPS), "ppool", r=["PPS"])
        self.dbg("PT", PT, ["PT%d" % m for m in range(8)], BF16)

    def phase2(self):
        fw = self.fw
        w_pool = self.din("w_pool", [4, 128, 2, 256])
        pscale = self.din("pscale", [128, 8])
        PSC = self.misc("PSC", [128, 8], F32)
        fw.dma("sp", lambda e: e.dma_start(out=PSC, in_=pscale), "PSC", w=["PSC"])
        ABT = self.ABT = self.sb("ABT", [128, KC, NCOL], BF16, OFF_BT)
        fw.retire(["XA%d" % q for q in range(4)], ["ABT%d" % m for m in range(16)])
        PT = self.PT
        for g in range(4):
            W, wk = self.wload(w_pool[g], 2, 256)
            for mm in range(2):
                m = 2 * g + mm
                banks = [self.bank() for _ in BLOCKS]
                for kc in range(2):
                    for (ps, pk), (c0, c1) in zip(banks, BLOCKS):
                        fw.op("pe", lambda e, ps=ps, kc=kc, c0=c0, c1=c1, W=W, mm=mm, g=g: e.matmul(
                            ps[:, 0:c1 - c0], W[:, kc, mm * 128:(mm + 1) * 128], PT[:, 2 * g + kc, c0:c1],
                            start=(kc == 0), stop=(kc == 1)), r=[wk, "PT%d" % (2 * g + kc)], w=[pk])
                for (ps, pk), (c0, c1) in zip(banks, BLOCKS):
                    fw.op("act", lambda e, ps=ps, c0=c0, c1=c1, m=m: e.activation(
                        out=ABT[:, m, c0:c1], in_=ps[:, 0:c1 - c0], func=AF.Copy, scale=PSC[:, m:m + 1]),
                        r=[pk, "PSC"], w=["ABT%d" % m])
        self.dbg("aT", ABT[:, 0:8, :], ["ABT%d" % m for m in range(8)], BF16)


def _tile_w(w, ncols):
    K, N = w.shape
    return np.ascontiguousarray(w.reshape(K // 128, 128, N // ncols, ncols).transpose(2, 1, 0, 3))


def _fm(v, nch):
    return np.ascontiguousarray(v.reshape(nch, 128).T)


def prep_shared(inp):
    sh = {}
    w_in = inp["w_in"][0]
    sh["w_in_p"] = _tile_w(w_in[:, :DP], 256)
    sh["w_in_s"] = _tile_w(w_in[:, DP:], 256)
    sh["w_pool"] = np.ascontiguousarray(inp["w_pool"][0].reshape(4, 2, 128, 256).transpose(0, 2, 1, 3))
    sh["pscale"] = _fm(inp["pool_scale"][0], 8)
    return sh


def prep_core(inp, c):
    b, half = c // 2, c % 2
    xp = inp["x_prompt"][b]
    own = xp[half * NOWN:(half + 1) * NOWN]
    pre = xp[0:NOWN] if half == 1 else np.zeros_like(own)
    x2 = np.concatenate([pre, own], 0)
    d = {}
    d["xa"] = np.ascontiguousarray(x2.reshape(128, 16, KC, 128).transpose(3, 2, 1, 0))
    xs = inp["x_sample"][16 * c:16 * c + 16]
    d["xs"] = np.ascontiguousarray(xs.transpose(2, 1, 0).reshape(KC, 128, NSMP).transpose(1, 0, 2))
    sp = inp["state_pool"][0, 16 * c:16 * c + 16]
    d["poolT"] = np.ascontiguousarray(sp.transpose(2, 0, 1).reshape(8, 128, 16, 15))
    invc = np.zeros((128, 8, 16), np.float32)
    for m in range(8):
        w = 2 << (m // 2)
        for s in range(16):
            invc[:, m, s] = 1.0 / (min(s + 1, w) if half == 0 else w)
    d["invc"] = invc
    return d


TWO_PI_LO = 6.283185
MAGIC = 12582912.0
K1 = 144


class KernSSM:
    def tt(self, eng, out, a, b, op, r, w):
        self.fw.op(eng, "tensor_tensor", r=r, w=w, out=out, in0=a, in1=b, op=op)

    def ts(self, eng, out, a, s1, op0, r, w, s2=None, op1=None):
        kw = dict(out=out, in0=a, scalar1=s1, scalar2=s2, op0=op0)
        if op1 is not None:
            kw["op1"] = op1
        self.fw.op(eng, "tensor_scalar", r=r, w=w, **kw)

    def phase3_tables(self):
        fw = self.fw
        names = ["lre", "lim", "lstep"]
        d_in = {n: self.din(n, [128, 32]) for n in names}
        for n in ("bre", "bim", "cre", "cim"):
            d_in[n] = self.din(n, [128, 32, CH])
        mv_d = self.din("mvals", [128, 33])
        cst_d = self.din("cst_bf", [128, 128 + 128 + 64])
        dsk_d = self.din("dskT", [128, G])
        fw.retire(["PT%d" % m for m in range(8)] + ["E0", "E1", "E2", "E3"], ["SPR"])
        fw.retire(["ws%d" % i for i in range(NWS)], ["WSR"])
        fw.retire(["XA%d" % q for q in range(4)], ["YT"])
        fw.retire(["XS", "INVC", "PPS", "Es0", "Es1", "Es2", "Es3"], ["X"])
        T = [self.sb("T%d" % i, [128, 2176], F32, OFF_SP + i * 8704) for i in range(4)]
        BTX = OFF_BT + 17664
        sm = {}
        off = BTX
        for n in ("bre", "bim", "cre", "cim", "BBre", "BBim", "ncim"):
            sm[n] = self.sb("sm_" + n, [128, 32, CH], F32, off)
            off += 2048
        for n in ("lre", "lim", "lstep", "dt", "a", "th", "den", "nre", "kre", "kim"):
            sm[n] = self.sb("sm_" + n, [128, 32], F32, off)
            off += 128
        assert off <= OFF_BT + 35328
        fw.retire(["XA%d" % q for q in range(4)], ["sm_" + n for n in sm])
        PWre = self.PWre = self.misc("PWre", [128, 32, 33], F32)
        PWim = self.PWim = self.misc("PWim", [128, 32, 33], F32)
        MV = self.misc("MV", [128, 33], F32)
        CST = self.misc("CST", [128, 320], BF16)
        self.IDB, self.MASK, self.I64R = CST[:, 0:128], CST[:, 128:256], CST[:, 256:320]
        DSK = self.DSK = self.misc("DSK", [128, G], F32)
        COEF = self.COEF = self.misc("COEF", [128, 32, 8], F32)
        L16 = self.L16 = self.misc("L16", [128, 4, 32], F32)
        Xre = self.Xre = self.sb("Xre", [128, 32, 8, CH], BF16, OFF_XT)
        Xim = self.Xim = self.sb("Xim", [128, 32, 8, CH], BF16, OFF_XT + 8192)
        Yre = self.Yre = self.sb("Yre", [128, 32, 17, CH], BF16, OFF_AT)
        Yim = self.Yim = self.sb("YimN", [128, 32, 17, CH], BF16, OFF_AT + 17408)
        for n in ("lre", "lim", "lstep", "bre", "bim", "cre", "cim"):
            fw.dma("sp", None, "ld_" + n, w=["sm_" + n], r=[], out=sm[n], in_=d_in[n])
        fw.dma("sp", None, "ld_mv", w=["MV"], out=MV, in_=mv_d)
        fw.dma("pool", None, "ld_cst", w=["CST"], out=CST, in_=cst_d)
        fw.dma("sp", None, "ld_dsk", w=["DSK"], out=DSK, in_=dsk_d)
        e = "dve"
        S = lambda n: ["sm_" + n]
        fw.op("act", "activation", r=S("lstep"), w=S("dt"), out=sm["dt"], in_=sm["lstep"], func=AF.Exp)
        self.tt(e, sm["a"], sm["lre"], sm["dt"], ALU.mult, S("lre") + S("dt"), S("a"))
        self.tt(e, sm["th"], sm["lim"], sm["dt"], ALU.mult, S("lim") + S("dt"), S("th"))
        ACt = [self.sb("AC%d" % i, [128, 1056], F32, OFF_E + i * 4224) for i in range(4)]
        fw.retire(["E0", "E1", "E2", "E3"], ["AC0", "AC1", "AC2", "AC3"])
        A = [ACt[i].rearrange("p (g i) -> p g i", i=33) for i in range(4)]
        bc_g = lambda v: v.unsqueeze(2).to_broadcast([128, 32, 33])
        bc_i = MV.unsqueeze(1).to_broadcast([128, 32, 33])
        self.tt(e, A[0], bc_g(sm["th"]), bc_i, ALU.mult, S("th") + ["MV"], ["AC0"])
        self.tt(e, A[1], bc_g(sm["a"]), bc_i, ALU.mult, S("a") + ["MV"], ["AC1"])
        fw.op("act", "activation", r=["AC1"], w=["AC1"], out=A[1], in_=A[1], func=AF.Exp)
        self.ts(e, A[2], A[0], 1.0 / (2.0 * np.pi), ALU.mult, ["AC0"], ["AC2"])
        self.ts(e, A[3], A[2], MAGIC, ALU.add, ["AC2"], ["AC3"])
        self.ts(e, A[3], A[3], MAGIC, ALU.subtract, ["AC3"], ["AC3"])
        self.tt(e, A[3], A[2], A[3], ALU.subtract, ["AC2", "AC3"], ["AC3"])
        fw.op("act", "activation", r=["AC3"], w=["AC3"], out=A[3], in_=A[3], func=AF.Sin, scale=TWO_PI_LO)
        self.tt(e, PWim, A[1], A[3], ALU.mult, ["AC1", "AC3"], ["PWim"])
        self.ts(e, A[0], A[2], 0.25, ALU.add, ["AC2"], ["AC0"])
        self.ts(e, A[3], A[0], MAGIC, ALU.add, ["AC0", "PWim"], ["AC3"])
        self.ts(e, A[3], A[3], MAGIC, ALU.subtract, ["AC3"], ["AC3"])
        self.tt(e, A[3], A[0], A[3], ALU.subtract, ["AC0", "AC3"], ["AC3"])
        fw.op("act", "activation", r=["AC3"], w=["AC3"], out=A[3], in_=A[3], func=AF.Sin, scale=TWO_PI_LO)
        self.tt(e, PWre, A[1], A[3], ALU.mult, ["AC1", "AC3"], ["PWre"])
        PW = ["PWre", "PWim"]
        p1re, p1im = PWre[:, :, 17], PWim[:, :, 17]
        self.tt(e, sm["den"], sm["lre"], sm["lre"], ALU.mult, S("lre"), S("den"))
        self.tt(e, sm["nre"], sm["lim"], sm["lim"], ALU.mult, S("lim"), S("nre"))
        self.tt(e, sm["den"], sm["den"], sm["nre"], ALU.add, S("den") + S("nre"), S("den"))
        fw.op(e, "reciprocal", r=S("den"), w=S("den"), out=sm["den"], in_=sm["den"])
        self.ts(e, sm["nre"], p1re, -1.0, ALU.add, PW + S("den"), S("nre"))
        self.tt(e, sm["kre"], sm["nre"], sm["lre"], ALU.mult, S("nre") + S("lre"), S("kre"))
        self.tt(e, sm["dt"], p1im, sm["lim"], ALU.mult, PW + S("lim") + S("a"), S("dt"))
        self.tt(e, sm["kre"], sm["kre"], sm["dt"], ALU.add, S("kre") + S("dt"), S("kre"))
        self.tt(e, sm["kre"], sm["kre"], sm["den"], ALU.mult, S("kre") + S("den"), S("kre"))
        self.tt(e, sm["kim"], p1im, sm["lre"], ALU.mult, PW + S("lre"), S("kim"))
        self.tt(e, sm["dt"], sm["nre"], sm["lim"], ALU.mult, S("nre") + S("lim") + S("kre"), S("dt"))
        self.tt(e, sm["kim"], sm["kim"], sm["dt"], ALU.subtract, S("kim") + S("dt"), S("kim"))
        self.tt(e, sm["kim"], sm["kim"], sm["den"], ALU.mult, S("kim") + S("den"), S("kim"))
        bc_c = lambda v: v.unsqueeze(2).to_broadcast([128, 32, CH])
        t1 = ACt[0][:, 0:512].rearrange("p (g c) -> p g c", c=CH)
        t2 = ACt[1][:, 0:512].rearrange("p (g c) -> p g c", c=CH)
        self.tt(e, t1, bc_c(sm["kre"]), sm["bre"], ALU.mult, S("kre") + S("bre") + PW, ["AC0"])
        self.tt(e, t2, bc_c(sm["kim"]), sm["bim"], ALU.mult, S("kim") + S("bim") + PW, ["AC1"])
        self.tt(e, sm["BBre"], t1, t2, ALU.subtract, ["AC0", "AC1"], S("BBre"))
        self.tt(e, t1, bc_c(sm["kre"]), sm["bim"], ALU.mult, S("kre") + S("bim") + S("BBre"), ["AC0"])
        self.tt(e, t2, bc_c(sm["kim"]), sm["bre"], ALU.mult, S("kim") + S("bre") + S("BBre"), ["AC1"])
        self.tt(e, sm["BBim"], t1, t2, ALU.add, ["AC0", "AC1"], S("BBim"))
        self.ts(e, sm["ncim"], sm["cim"], -1.0, ALU.mult, S("cim"), S("ncim"))
        for h in range(2):
            gs = slice(16 * h, 16 * h + 16)
            v1 = T[2][:, 0:2048].rearrange("p (g s c) -> p g s c", s=8, c=CH)
            v2 = T[3][:, 0:2048].rearrange("p (g s c) -> p g s c", s=8, c=CH)
            npre = PWre[:, gs, 0:8].unsqueeze(3).to_broadcast([128, 16, 8, CH])
            npim = PWim[:, gs, 0:8].unsqueeze(3).to_broadcast([128, 16, 8, CH])
            bbre = sm["BBre"][:, gs, :].unsqueeze(2).to_broadcast([128, 16, 8, CH])
            bbim = sm["BBim"][:, gs, :].unsqueeze(2).to_broadcast([128, 16, 8, CH])
            rr = PW + S("BBre") + S("BBim") + ["SPR", "AC0", "AC1", "AC2", "AC3"]
            for (o_, x1, y1, x2, y2, op) in ((Xre, npre, bbre, npim, bbim, ALU.subtract),
                                             (Xim, npre, bbim, npim, bbre, ALU.add)):
                self.tt("dve", v1, x1, y1, ALU.mult, rr + ["X"], ["T2"])
                self.tt("dve", v2, x2, y2, ALU.mult, rr + ["X"], ["T3"])
                self.tt("dve", o_[:, gs], v1, v2, op, ["T2", "T3"], ["X"])
        for h in range(4):
            gs = slice(8 * h, 8 * h + 8)
            v1 = T[0][:, 0:2176].rearrange("p (g s c) -> p g s c", s=17, c=CH)
            v2 = T[1][:, 0:2176].rearrange("p (g s c) -> p g s c", s=17, c=CH)
            ppre = PWre[:, gs, 16:33].unsqueeze(3).to_broadcast([128, 8, 17, CH])
            ppim = PWim[:, gs, 16:33].unsqueeze(3).to_broadcast([128, 8, 17, CH])
            bc_s = lambda v: v[:, gs, :].unsqueeze(2).to_broadcast([128, 8, 17, CH])
            rr = PW + S("cre") + S("cim") + S("ncim") + ["SPR", "AC0", "AC1", "AC2", "AC3"]
            for (o_, x1, y1, x2, y2) in ((Yre, ppre, bc_s(sm["cre"]), ppim, bc_s(sm["cim"])),
                                         (Yim, ppre, bc_s(sm["ncim"]), ppim, bc_s(sm["cre"]))):
                self.tt(e, v1, x1, y1, ALU.mult, rr + ["YT"], ["T0"])
                self.tt(e, v2, x2, y2, ALU.mult, rr + ["YT"], ["T1"])
                self.tt(e, o_[:, gs], v1, v2, ALU.subtract, ["T0", "T1"], ["YT"])
        for t, mi in ((0, 16 + 15), (1, 16 + 7)):
            cv = COEF[:, :, 4 * t:4 * t + 4]
            self.copy("pool", cv[:, :, 0], PWre[:, :, mi], PW, ["COEF"])
            self.copy("pool", cv[:, :, 1], PWim[:, :, mi], PW, ["COEF"])
            self.ts("pool", cv[:, :, 2], PWim[:, :, mi], -1.0, ALU.mult, PW, ["COEF"])
            self.copy("pool", cv[:, :, 3], PWre[:, :, mi], PW, ["COEF"])
        self.copy("pool", L16[:, 0, :], PWre[:, :, 32], PW, ["L16"])
        self.copy("pool", L16[:, 1, :], PWre[:, :, 32], PW, ["L16"])
        self.ts("pool", L16[:, 2, :], PWim[:, :, 32], -1.0, ALU.mult, PW, ["L16"])
        self.copy("pool", L16[:, 3, :], PWim[:, :, 32], PW, ["L16"])
        self.dbg("PWre", PWre, PW)
        self.dbg("PWim", PWim, PW)
        self.dbg("Xre", Xre, ["X"], BF16)
        self.dbg("Yre", Yre, ["YT"], BF16)
        self.dbg("YimN", Yim, ["YT"], BF16)
        self.dbg("BBre", sm["BBre"], S("BBre"))


def prep_shared_ssm(inp):
    sh = {}
    pl = lambda a: np.ascontiguousarray(a.reshape(32, 2, 64, *a.shape[2:]).transpose(1, 2, 0, *range(3, a.ndim + 1))
                                        .reshape(128, 32, *a.shape[2:]))
    lre, lim = inp["lambda_re"][0], inp["lambda_im"][0]
    sh["lre"], sh["lim"] = pl(lre), pl(lim)
    sh["lstep"] = pl(np.broadcast_to(inp["log_step"][0][:, None], (G, NST)))
    sh["bre"], sh["bim"] = pl(inp["b_re"][0]), pl(inp["b_im"][0])
    sh["cre"] = pl(inp["c_re"][0].transpose(0, 2, 1))
    sh["cim"] = pl(inp["c_im"][0].transpose(0, 2, 1))
    mv = np.concatenate([-np.arange(16), np.arange(17)]).astype(np.float32)
    sh["mvals"] = np.ascontiguousarray(np.broadcast_to(mv[None], (128, 33)))
    ident = np.eye(128, dtype=np.float32)
    sidx = np.arange(128) // 16
    mask = (sidx[None, :] >= sidx[:, None]).astype(np.float32)
    i64 = np.concatenate([np.eye(64, dtype=np.float32)] * 2, 0)
    sh["cst_bf"] = np.ascontiguousarray(np.concatenate([ident, mask, i64], 1))
    dsk = inp["d_skip"][0].reshape(G, CH)
    sh["dskT"] = np.ascontiguousarray(np.tile(dsk.T, (8, 1)))
    return sh


class KernSSM2:
    def phase3_main(self, stop=None):
        fw = self.fw
        UU, UUs = self.UU, self.UUs
        Xre, Xim, Yre, Yim = self.Xre, self.Xim, self.Yre, self.Yim
        IDB, MASK, I64R, DSK, COEF, L16 = self.IDB, self.MASK, self.I64R, self.DSK, self.COEF, self.L16
        h0_d = self.din("h0", [128, 2, 32, 16])
        p_ssm = self.dout("p_ssm", [128, 2, 32])
        s_ssm = self.dout("s_ssm", [128, 2, 32, 16])
        o = OFF_WS
        SBF = self.sb("SBF", [128, 2, 32, K1], BF16, o); o += 18432
        SPS = self.sb("SPS", [128, 2, 32, 16], F32, o); o += 4096
        sets = []
        for i in range(2):
            d = {}
            d["TDO"] = self.sb("TDO%d" % i, [128, 256], BF16, o); o += 512
            d["TD"], d["TO"] = d["TDO"][:, 0:128], d["TDO"][:, 128:256]
            d["TMP"] = self.sb("TMPD%d" % i, [128, 128], F32, o); o += 512
            d["BCT"] = self.sb("BCT%d" % i, [128, 2, 128], BF16, o); o += 512
            d["UT"] = self.sb("UT%d" % i, [128, 2, K1], BF16, o); o += 576
            d["D4"] = self.sb("D4%d" % i, [128, 8, 64], BF16, o); o += 1024
            sets.append(d)
        o = OFF_WS + 29824
        CUR = [self.sb("CUR%d" % i, [128, 4, 32], F32, o + 512 * i) for i in range(2)]; o += 1024
        TT = [self.sb("TT%d" % i, [128, 3, 32], F32, o + 384 * i) for i in range(3)]; o += 1152
        assert o <= OFF_WS + 32768
        SP = self.sb("SP", [128, 2, 32, K1 + 1], F32, OFF_SP)
        psbf = self.psbf
        for i in range(2):
            fw.op("pool", "memset", r=["WSR"], w=["UT%d" % i], ap=sets[i]["UT"], constant=0.0)
        fw.retire(["T0", "T1", "T2", "T3"], ["SP"])
        fw.op("pool", "memset", w=["SP"], ap=SP[:, :, :, 0], constant=0.0)

        def transposes(g, st, si):
            UT = st["UT"]
            for t in range(2):
                fw.op("pe", "transpose", r=["UU%d" % g, "UU", "CST"], w=["psbf"],
                      out=psbf[:, t * 128:(t + 1) * 128],
                      in_=UU[:, g, 8 * t:8 * t + 8, :].rearrange("p s c -> p (s c)"), identity=IDB)
            fw.op("pe", "transpose", r=["UUs", "UUs%d" % g, "CST"], w=["psbf"], out=psbf[64:128, 256:272],
                  in_=UUs[:, g, :, :].rearrange("p s c -> p (s c)"), identity=IDB[0:16, 0:16])
            self.copy(self.evac_eng(), UT[:, :, 0:128], psbf[:, 0:256].rearrange("p (t k) -> p t k", t=2),
                      ["psbf"], ["UT%d" % si])
            self.copy(self.evac_eng(), UT[64:128, 1, 128:K1], psbf[64:128, 256:272], ["psbf"], ["UT%d" % si])

        p1state = {}

        def p1_stageA(g):
            g2, par = g // 2, g % 2
            st = sets[g % 2]
            si = g % 2
            D4 = sets[g2 % 2]["D4"]
            dk = "D4%d" % (g2 % 2)
            if par == 0:
                self.tt("pool", D4, COEF[:, g2, :].unsqueeze(2).to_broadcast([128, 8, 64]),
                        I64R.unsqueeze(1).to_broadcast([128, 8, 64]), ALU.mult, ["COEF", "CST", "WSR"], [dk])
            rows = slice(64 * par, 64 * par + 64)
            pB, kB = self.bank()
            for t in range(2):
                a_ = D4[rows, 4 * t:4 * t + 2, :].rearrange("p a n -> p (a n)")
                b_ = D4[rows, 4 * t + 2:4 * t + 4, :].rearrange("p a n -> p (a n)")
                fw.op("pe", "matmul", r=["X", dk], w=[kB], out=pB[:, t * 128:(t + 1) * 128],
                      lhsT=Xre[rows, g2, :, :].rearrange("p s c -> p (s c)"), rhs=a_, start=True, stop=False)
                fw.op("pe", "matmul", r=["X", dk], w=[kB], out=pB[:, t * 128:(t + 1) * 128],
                      lhsT=Xim[rows, g2, :, :].rearrange("p s c -> p (s c)"), rhs=b_, start=False, stop=True)
            self.copy(self.evac_eng(), st["BCT"], pB[:, 0:256].rearrange("p (t n) -> p t n", t=2),
                      [kB], ["BCT%d" % si])
            transposes(g, st, si)

        def p1_stageB(g):
            g2, par = g // 2, g % 2
            st = sets[g % 2]
            si = g % 2
            rows = slice(64 * par, 64 * par + 64)
            if par == 0:
                p1state["pP"] = self.bank()
            pP, kP = p1state["pP"]
            for ri in range(2):
                for t in range(2):
                    fw.op("pe", "matmul", r=["BCT%d" % si, "UT%d" % si], w=[kP],
                          out=pP[rows, ri * K1:(ri + 1) * K1], lhsT=st["BCT"][:, t, ri * 64:(ri + 1) * 64],
                          rhs=st["UT"][:, t, :], start=(t == 0), stop=(t == 1))
            if par == 1:
                self.copy(self.evac_eng(), SP[:, :, g2, 1:K1 + 1],
                          pP[:, 0:2 * K1].rearrange("p (r k) -> p r k", r=2), [kP], ["SP"])

        p1_stageA(0)
        for g in range(G):
            if g + 1 < G:
                p1_stageA(g + 1)
            p1_stageB(g)
        if stop == "pass1":
            self.dbg("SP", SP, ["SP"])
            return
        e = "dve"
        fw.op(e, "memset", r=["WSR"], w=["CUR0"], ap=CUR[0], constant=0.0)
        fw.op(e, "memset", r=["WSR"], w=["CUR1"], ap=CUR[1], constant=0.0)
        import concourse.bass as _b
        L4 = L16.rearrange("p (w c) g -> p w c g", w=2)
        TWS = [self.sb("TWS%d" % i, [128, 3, 64], F32, OFF_WS + 29824 + 1024 + 768 * i) for i in range(2)]
        for i in range(2):
            fw.op("act", "activation", r=["SP", "WSR"], w=["TWp%d" % i], out=TWS[i][:, 2, :].rearrange("p (c g) -> p c g", c=2),
                  in_=SP[:, :, :, i + 1], func=AF.Copy)
        for k in range(128):
            c0, c1 = CUR[k % 2], CUR[(k + 1) % 2]
            k0, k1 = "CUR%d" % (k % 2), "CUR%d" % ((k + 1) % 2)
            tw = TWS[k % 2]
            ka, kp = "TWa%d" % (k % 2), "TWp%d" % (k % 2)
            win = _b.AP(tensor=c0.tensor, offset=c0.offset, ap=[list(c0.ap[0]), [32, 2], [32, 2], [1, 32]])
            self.tt(e, tw[:, 0:2, :].rearrange("p w (c g) -> p w c g", c=2), L4, win, ALU.mult, ["L16", k0, "WSR"], [ka])
            red_in = _b.AP(tensor=tw.tensor, offset=tw.offset, ap=[list(tw.ap[0]), [0, 2], [1, 64], [64, 3]])
            fw.op(e, "tensor_reduce", r=[ka, kp], w=[k1], out=c1.rearrange("p (r c) g -> p r (c g)", r=2),
                  in_=red_in, axis=AX.X, op=ALU.add)
            self.copy("act", SP[:, :, :, k + 1], c1[:, 0:2, :], [k1], ["SPo"])
            if k + 2 < 128:
                fw.op("act", "activation", r=["SP"], w=[kp], out=tw[:, 2, :].rearrange("p (c g) -> p c g", c=2),
                      in_=SP[:, :, :, k + 3], func=AF.Copy)
        fw.dma("sp", None, "p_ssm", r=["CUR0"], out=p_ssm, in_=CUR[0][:, 0:2, :])
        H0 = self.sb("H0", [128, 2, 32, 16], F32, OFF_SP - 0 + 0) if False else None
        h0t = self.sb("H0t", [128, 2, 32, 16], F32, OFF_BT + 17664)
        tq = [self.sb("TQ%d" % i, [128, 32, 16], F32, OFF_BT + 17664 + 4096 + 2048 * i) for i in range(3)]
        allsm = ["sm_" + n for n in ("bre", "bim", "cre", "cim", "BBre", "BBim", "ncim", "lre", "lim", "lstep",
                                     "dt", "a", "th", "den", "nre", "kre", "kim")]
        fw.retire(allsm, ["H0t", "TQ"])
        fw.dma("sp", None, "ld_h0", w=["H0t"], out=h0t, in_=h0_d)
        PWre, PWim = self.PWre, self.PWim
        bq = lambda v: v.unsqueeze(2).to_broadcast([128, 32, 16])
        n12re, n12im = bq(PWre[:, :, 12]), bq(PWim[:, :, 12])
        l16re, l16im = bq(PWre[:, :, 32]), bq(PWim[:, :, 32])
        PW = ["PWre", "PWim"]

        def cmul(out_re, out_im, are, aim, bre_, bim_, rk, wk):
            self.tt(e, tq[0], are, bre_, ALU.mult, rk + ["TQ"], ["TQ"])
            self.tt(e, tq[1], aim, bim_, ALU.mult, rk + ["TQ"], ["TQ"])
            self.tt(e, tq[2], are, bim_, ALU.mult, rk + ["TQ"], ["TQ"])
            self.tt(e, out_re, tq[0], tq[1], ALU.subtract, ["TQ"], wk)
            self.tt(e, tq[0], aim, bre_, ALU.mult, rk + ["TQ"], ["TQ"])
            self.tt(e, out_im, tq[2], tq[0], ALU.add, ["TQ"], wk)

        cmul(SPS[:, 0], SPS[:, 1], n12re, n12im, h0t[:, 0], h0t[:, 1], PW + ["H0t", "WSR"], ["SPS"])
        cmul(h0t[:, 0], h0t[:, 1], l16re, l16im, SPS[:, 0], SPS[:, 1], PW + ["SPS"], ["H0t"])
        self.tt(e, h0t, h0t, SP[:, :, :, 129:K1 + 1], ALU.add, ["H0t", "SP"], ["H0t"])
        fw.dma("sp", None, "s_ssm", r=["H0t"], out=s_ssm, in_=h0t)
        self.copy("act", SBF[:, :, :, 0:128], SP[:, :, :, 0:128], ["SP", "SPo", "WSR"], ["SBF"])
        self.copy("act", SBF[:, :, :, 128:K1], SPS, ["SPS"], ["SBF"])
        self.dbg("SP", SP, ["SP", "SPo"])
        if stop == "level2":
            return
        o2 = OFF_WS + 22528
        sets2 = []
        for i in range(2):
            d = {}
            d["TDO"] = self.sb("P2TDO%d" % i, [128, 256], BF16, o2); o2 += 512
            d["TD"], d["TO"] = d["TDO"][:, 0:128], d["TDO"][:, 128:256]
            d["TMP"] = self.sb("P2TMP%d" % i, [128, 128], F32, o2); o2 += 512
            d["UT"] = self.sb("P2UT%d" % i, [128, 2, K1], BF16, o2); o2 += 576
            d["CCP"] = self.sb("P2CCP%d" % i, [128, 2, 2, 256], BF16, o2); o2 += 2048
            sets2.append(d)
        assert o2 <= OFF_WS + 29824
        for i in range(2):
            fw.op("pool", "memset", r=["SBF"], w=["UT%d" % i], ap=sets2[i]["UT"], constant=0.0)
            fw.op("pool", "memset", r=["SBF"], w=["CCP%d" % i], ap=sets2[i]["CCP"], constant=0.0)
        Z2 = self.Z2 = self.sb("Z2", [128, 16, G, CH], BF16, OFF_SP)
        Zs2 = self.Zs2 = self.sb("Zs2", [16, 4, G, CH], BF16, OFF_MISC + 8224)
        fw.retire(["SP", "SPo"], ["Z2"])
        fw.retire(["PWre", "PWim"], ["Zs2"])
        def p2_stageA(g):
            g2, par = g // 2, g % 2
            si = g % 2
            st = sets2[si]
            ccp = sets2[g2 % 2]["CCP"]
            ck = "CCP%d" % (g2 % 2)
            if par == 0:
                for pr in range(2):
                    rws = slice(64 * pr, 64 * pr + 64)
                    for ri, Yt in ((0, Yre), (1, Yim)):
                        self.copy("act", ccp[rws, pr, ri, :],
                                  Yt[rws, g2, 1:17, :].rearrange("p j c -> p (j c)"), ["YT"], [ck])
            rows = slice(64 * par, 64 * par + 64)
            pT, kT = self.bank()
            fw.op("pe", "matmul", r=["X", "YT"], w=[kT], out=pT[:, 0:256],
                  lhsT=Xre[rows, g2, :, :].rearrange("p s c -> p (s c)"),
                  rhs=Yre[rows, g2, 0:16, :].rearrange("p j c -> p (j c)"), start=True, stop=False)
            fw.op("pe", "matmul", r=["X", "YT"], w=[kT], out=pT[:, 0:256],
                  lhsT=Xim[rows, g2, :, :].rearrange("p s c -> p (s c)"),
                  rhs=Yim[rows, g2, 0:16, :].rearrange("p j c -> p (j c)"), start=False, stop=True)
            self.tt("dve", st["TMP"], pT[:, 0:128], MASK, ALU.mult, [kT, "CST"], ["TMP%d" % si])
            self.copy("dve", st["TO"], pT[:, 128:256], [kT], ["TDO%d" % si])
            fw.op("dve", "scalar_tensor_tensor", r=["TMP%d" % si, "CST", "DSK"], w=["TDO%d" % si],
                  out=st["TD"], in0=IDB, scalar=DSK[:, g:g + 1], in1=st["TMP"], op0=ALU.mult, op1=ALU.add)
            transposes(g, st, si)

        def p2_stageB(g):
            g2, par = g // 2, g % 2
            si = g % 2
            st = sets2[si]
            ccp = sets2[g2 % 2]["CCP"]
            ck = "CCP%d" % (g2 % 2)
            UT = st["UT"]
            tdk = ["TDO%d" % si, "UT%d" % si, "SBF", ck]
            pY, kY = self.bank()
            fw.op("pe", "matmul", r=tdk, w=[kY], out=pY[:, 0:256], lhsT=UT[:, 0, 0:128], rhs=st["TDO"],
                  start=True, stop=False)
            fw.op("pe", "matmul", r=tdk, w=[kY], out=pY[:, 128:256], lhsT=UT[:, 1, 0:128], rhs=st["TD"],
                  start=False, stop=False)
            for ri in range(2):
                fw.op("pe", "matmul", r=tdk, w=[kY], out=pY[:, 0:256], lhsT=SBF[:, ri, g2, 0:128],
                      rhs=ccp[:, par, ri, :], start=False, stop=(ri == 1))
            fw.op("act", "activation", r=[kY], w=["Z2"], out=Z2[:, :, g, :],
                  in_=pY[:, 0:256].rearrange("p (j c) -> p j c", c=CH), func=AF.Gelu_apprx_tanh)
            pS_, kS_ = self.bank()
            fw.op("pe", "matmul", r=tdk, w=[kS_], out=pS_[0:16, 0:64], lhsT=UT[:, 0, 128:K1], rhs=st["TO"][:, 64:128],
                  start=True, stop=False)
            fw.op("pe", "matmul", r=tdk, w=[kS_], out=pS_[0:16, 0:64], lhsT=UT[:, 1, 128:K1], rhs=st["TD"][:, 64:128],
                  start=False, stop=False)
            for ri in range(2):
                fw.op("pe", "matmul", r=tdk, w=[kS_], out=pS_[0:16, 0:64], lhsT=SBF[:, ri, g2, 128:K1],
                      rhs=ccp[:, par, ri, 192:256], start=False, stop=(ri == 1))
            fw.op("act", "activation", r=[kS_], w=["Zs2"], out=Zs2[:, :, g, :],
                  in_=pS_[0:16, 0:64].rearrange("p (j c) -> p j c", c=CH), func=AF.Gelu_apprx_tanh)

        p2_stageA(0)
        for g in range(G):
            if g + 1 < G:
                p2_stageA(g + 1)
            p2_stageB(g)
        self.dbg("Z", Z2, ["Z2"], BF16)
        self.dbg("Zs", Zs2, ["Zs2"], BF16)


def prep_core_ssm(inp, c):
    d = {}
    pl = lambda a: a.reshape(32, 2, 64, 16).transpose(1, 2, 0, 3).reshape(128, 32, 16)
    hre = inp["state_ssm_re"][0, 16 * c:16 * c + 16].transpose(1, 2, 0)
    him = inp["state_ssm_im"][0, 16 * c:16 * c + 16].transpose(1, 2, 0)
    d["h0"] = np.ascontiguousarray(np.stack([pl(hre), pl(him)], 1))
    return d


class KernRest:
    def linear_fm(self, wd, kcw, nout, in_fn, in_keys, epilogue, wkeys_extra=(), blocks=BLOCKS, kc_tiles=None):
        fw = self.fw
        wst = self.wstream([(wd[b], kcw, 256) for b in range(nout // 256)], ahead=2)
        for blk in range(nout // 256):
            W, wk = next(wst)
            for mm in range(2):
                mo = 2 * blk + mm
                banks = [self.bank() for _ in blocks]
                for bi, ((ps, pk), (c0, c1)) in enumerate(zip(banks, blocks)):
                    for kc in range(kcw):
                        try:
                            ik = list(in_keys(kc, bi))
                        except TypeError:
                            ik = list(in_keys(kc))
                        fw.op("pe", "matmul", r=[wk] + ik, w=[pk], out=ps[:, 0:c1 - c0],
                              lhsT=W[:, kc, mm * 128:(mm + 1) * 128], rhs=in_fn(kc, c0, c1),
                              start=(kc == 0), stop=(kc == kcw - 1))
                epilogue(mo, [(ps, pk, c0, c1) for (ps, pk), (c0, c1) in zip(banks, blocks)])

    def layer_norm(self, tag, gname, bname, last=False):
        fw = self.fw
        RT, AT, BT = self.RT, self.AT, self.BTt
        g_d = self.din(gname, [128, KC])
        b_d = self.din(bname, [128, KC])
        GB = self.sb("GB_" + tag, [128, 2, KC], F32, self.a2_off(256))
        fw.dma("sp", None, "gb_" + tag, w=["GB" + tag], out=GB[:, 0, :], in_=g_d)
        fw.dma("sp", None, "gb2_" + tag, w=["GB" + tag], out=GB[:, 1, :], in_=b_d)
        ONES = self.ONES
        MSQ, VAR = self.LNT[0], self.LNT[1]
        for kc in range(KC):
            fw.retire(["AT%d.%d" % (kc, bi) for bi in range(3)], ["AT%d" % kc])
        for kc in range(KC):
            self.copy("dve", AT[:, kc, :], RT[:, kc, :], ["RT%d" % kc], ["AT%d" % kc])
            fw.op("act", "activation", r=["RT%d" % kc], w=["BT%d" % kc], out=BT[:, kc, :], in_=RT[:, kc, :],
                  func=AF.Square)
        s1, s2 = [], []
        for _ in BLOCKS:
            s1.append(self.bank())
            s2.append(self.bank())
        inv = 1.0 / D
        fw.retire(["LNT0", "LNT2"], ["LNTA", "LNTB"])
        fw.retire(["LNT1"], ["MSQ0", "MSQ1", "MSQ2"])
        TMPS = [(self.LNT[2], "LNTA"), (self.LNT[0], "LNTB")]
        for bi, ((p1, k1), (p2, k2), (c0, c1)) in enumerate(zip(s1, s2, BLOCKS)):
            n = c1 - c0
            for kc in range(KC):
                fw.op("pe", "matmul", r=["AT%d" % kc, "ONES"], w=[k1], out=p1[:, 0:n], lhsT=ONES,
                      rhs=AT[:, kc, c0:c1], start=(kc == 0), stop=(kc == KC - 1))
            for kc in range(KC):
                fw.op("pe", "matmul", r=["BT%d" % kc, "ONES"], w=[k2], out=p2[:, 0:n], lhsT=ONES,
                      rhs=BT[:, kc, c0:c1], start=(kc == 0), stop=(kc == KC - 1))
            fw.retire(["AT%d" % kc for kc in range(KC)], ["AT%d.%d" % (kc, bi) for kc in range(KC)])
            fw.op("act", "activation", r=[k1], w=[k1], out=p1[:, 0:n], in_=p1[:, 0:n], func=AF.Copy, scale=inv)
            fw.op("act", "activation", r=[k1], w=["MSQ%d" % bi], out=VAR[:, c0:c1], in_=p1[:, 0:n], func=AF.Square)
            fw.op("dve", "scalar_tensor_tensor", r=[k2, "MSQ%d" % bi], w=["MSQ%d" % bi], out=VAR[:, c0:c1],
                  in0=p2[:, 0:n], scalar=inv, in1=VAR[:, c0:c1], op0=ALU.mult, op1=ALU.subtract)
            self.ts("dve", VAR[:, c0:c1], VAR[:, c0:c1], EPS, ALU.add, ["MSQ%d" % bi], ["MSQ%d" % bi])
            fw.op("act", "activation", r=["MSQ%d" % bi], w=["MSQ%d" % bi], out=VAR[:, c0:c1], in_=VAR[:, c0:c1],
                  func=AF.Sqrt)
            fw.op("dve", "reciprocal", r=["MSQ%d" % bi, k2], w=[k2], out=p2[:, 0:n], in_=VAR[:, c0:c1])
            for kc in range(KC):
                TMP, tk = TMPS[kc % 2]
                self.tt("dve", TMP[:, c0:c1], RT[:, kc, c0:c1], p1[:, 0:n], ALU.subtract, ["RT%d" % kc, k1], [tk])
                self.tt("dve", TMP[:, c0:c1], TMP[:, c0:c1], p2[:, 0:n], ALU.mult, [tk, k2], [tk])
                fw.op("act", "activation", r=[tk, "GB" + tag], w=["RT%d" % kc], out=RT[:, kc, c0:c1], in_=TMP[:, c0:c1],
                      func=AF.Identity, scale=GB[:, 0, kc:kc + 1], bias=GB[:, 1, kc:kc + 1])
                if not last:
                    wk_ = ["AT%d.%d" % (kc, bi)] + (["AT%d" % kc] if bi == 2 else [])
                    fw.op("act", "activation", r=[tk, "GB" + tag], w=wk_, out=AT[:, kc, c0:c1], in_=TMP[:, c0:c1],
                          func=AF.Identity, scale=GB[:, 0, kc:kc + 1], bias=GB[:, 1, kc:kc + 1])
        fw.retire(["MSQ0", "MSQ1", "MSQ2"], ["LNT1"])
        fw.retire(["LNTA", "LNTB"], ["LNT0", "LNT2"])

    def a2_off(self, nbytes):
        off = self.a2
        self.a2 += (nbytes + 31) // 32 * 32
        assert self.a2 <= 16832, self.a2
        return OFF_MISC + off

    def phase3_glu(self):
        fw = self.fw
        UU, UUs, IDB, psbf = self.UU, self.UUs, self.IDB, self.psbf
        Z2, Zs2 = self.Z2, self.Zs2
        w_glu = self.din("w_glu", [4, 128, 8, 256])
        bglu_d = self.din("b_glu", [128, 8])
        allz = ["Z2"]
        allzs = ["Zs2"]
        ZT = self.ZT = self.sb("ZT", [128, 8, NCOL], BF16, OFF_AT)
        fw.retire(["YT"], ["ZT%d" % m for m in range(8)])
        for m in range(8):
            for j in range(16):
                fw.op("pe", "transpose", r=allz + ["CST"], w=["psbf"], out=psbf[:, j * 64:(j + 1) * 64],
                      in_=Z2[64:128, j, 8 * m:8 * m + 8, :].rearrange("p g c -> p (g c)"), identity=IDB[64:128, 64:128])
            self.copy(self.evac_eng(), ZT[:, m, 0:NOWN], psbf[:, 0:1024], ["psbf"], ["ZT%d" % m])
        for m in range(8):
            fw.op("pool", "memset", w=["ZT%d" % m], ap=ZT[:, m, C_HALO:NCOL], constant=0.0)
        for m in range(8):
            for jj in range(2):
                fw.op("pe", "transpose", r=allz + ["CST"], w=["psbf"],
                      out=psbf[:, (2 * m + jj) * 32:(2 * m + jj + 1) * 32],
                      in_=Z2[32:64, 14 + jj, 8 * m:8 * m + 8, :].rearrange("p g c -> p (g c)"),
                      identity=IDB[32:64, 32:64])
        self.copy("dve", ZT[:, :, C_HALO + 14:NCOL],
                  psbf[:, 0:512].rearrange("p (m j k) -> p m j k", m=8, j=2)[:, :, :, 31],
                  ["psbf"], ["ZT%d" % m for m in range(8)])
        for m in range(8):
            for i in range(4):
                fw.op("pe", "transpose", r=allzs + ["CST"], w=["psbf"],
                      out=psbf[:, (4 * m + i) * 16:(4 * m + i + 1) * 16],
                      in_=Zs2[:, i, 8 * m:8 * m + 8, :].rearrange("p g c -> p (g c)"), identity=IDB[0:16, 0:16])
        self.copy("act", ZT[:, :, C_SMP:C_HALO], psbf[:, 0:512].rearrange("p (m c) -> p m c", m=8),
                  ["psbf"], ["ZT%d" % m for m in range(8)])
        self.dbg("ZT", ZT, ["ZT%d" % m for m in range(8)], BF16)
        fw.retire(["SBF", "SPS", "CUR0", "CUR1", "TT0", "TT1", "TT2", "UT0", "UT1", "CCP0", "CCP1", "TDO0", "TDO1",
                   "TMP0", "TMP1", "BCT0", "BCT1", "D40", "D41"], ["ws%d" % i for i in range(NWS)])
        self.a2 = 0
        fw.retire(["UUs"] + ["UUs%d" % g for g in range(G)] + allzs + ["PSC", "PWre", "PWim", "MV"], ["A2"])
        BG = self.sb("BGLU", [128, 8], F32, self.a2_off(32))
        fw.dma("sp", None, "bglu", r=["A2"], w=["BGLU"], out=BG, in_=bglu_d)
        self.sg_off = [self.a2_off(1024) for i in range(2)]
        SG = [self.sb("SG%d" % i, [128, 512], BF16, self.sg_off[i]) for i in range(2)]
        ABT = self.ABT
        fw.retire(["H0t", "TQ"], ["ABT%d" % m for m in range(8, 16)])
        sgi = [0]

        def epi(mo, banks):
            for (ps, pk, c0, c1) in banks:
                i = sgi[0] % 2
                sgi[0] += 1
                fw.op("act", "activation", r=[pk, "BGLU", "A2"], w=["SG%d" % i], out=SG[i][:, 0:c1 - c0],
                      in_=ps[:, 0:c1 - c0], func=AF.Sigmoid, bias=BG[:, mo:mo + 1])
                self.tt("dve", ABT[:, 8 + mo, c0:c1], SG[i][:, 0:c1 - c0], ZT[:, mo, c0:c1], ALU.mult,
                        ["SG%d" % i, "ZT%d" % mo], ["ABT%d" % (8 + mo)])

        self.linear_fm(w_glu, 8, DP, lambda kc, c0, c1: ZT[:, kc, c0:c1], lambda kc: ["ZT%d" % kc], epi)
        self.dbg("bT", ABT[:, 8:16, :], ["ABT%d" % m for m in range(8, 16)], BF16)

    def phase4(self):
        fw = self.fw
        xr = self.din("xr", [128, KC, NCOL])
        w_out = self.din("w_out", [8, 128, KC, 256])
        RT = self.RT = self.sb("RT", [128, KC, NCOL], F32, OFF_RT)
        self.AT = self.sb("AT", [128, KC, NCOL], BF16, OFF_AT)
        self.BTt = self.sb("BTt", [128, KC, NCOL], BF16, OFF_BT)
        allz = ["UU", "Z2"] + ["UU%d" % g for g in range(G)]
        fw.retire(allz + ["SP", "SPo", "T0", "T1", "T2", "T3", "SPR"], ["RT%d" % kc for kc in range(KC)])
        for q in range(4):
            fw.dma("sp", None, "xr%d" % q, w=["RT%d" % kc for kc in range(4 * q, 4 * q + 4)],
                   out=RT[:, 4 * q:4 * q + 4, :], in_=xr[:, 4 * q:4 * q + 4, :])
        ABT = self.ABT

        def epi(mo, banks):
            for (ps, pk, c0, c1) in banks:
                fw.op("dve", "scalar_tensor_tensor", r=[pk, "RT%d" % mo], w=["RT%d" % mo], out=RT[:, mo, c0:c1],
                      in0=RT[:, mo, c0:c1], scalar=ALPHA, in1=ps[:, 0:c1 - c0], op0=ALU.mult, op1=ALU.add)

        self.linear_fm(w_out, KC, D, lambda kc, c0, c1: ABT[:, kc, c0:c1], lambda kc: ["ABT%d" % kc], epi)
        self.lnt_off = self.a2_off(3 * NCOL * 4)
        self.LNT = [self.sb("LNT%d" % i, [128, NCOL], F32, self.lnt_off + i * NCOL * 4) for i in range(3)]
        self.ONES = self.sb("ONES", [128, 128], BF16, OFF_MISC + 16832 - 256 - 0) if False else None
        ones_d = self.din("ones_bf", [128, 128])
        self.ONES = self.sb("ONES", [128, 128], BF16, self.a2_off(256))
        fw.dma("pool", None, "ones", r=["A2"], w=["ONES"], out=self.ONES, in_=ones_d)
        fw.retire(["ZT%d" % m for m in range(8)], ["AT%d" % kc for kc in range(KC)])
        fw.retire(["ABT%d" % m for m in range(16)] + ["H0t", "TQ"], ["BT%d" % kc for kc in range(KC)])
        fw.retire(["SG0", "SG1", "BGLU"], ["LNT0", "LNT1", "LNT2"])
        self.layer_norm("1", "ln1_g", "ln1_b")
        self.dbg("h1", RT, ["RT%d" % kc for kc in range(KC)])


def prep_shared_rest(inp):
    sh = {}
    sh["w_glu"] = _tile_w(inp["w_glu"][0], 256)
    sh["b_glu"] = _fm(inp["b_glu"][0], 8)
    sh["w_out"] = _tile_w(inp["w_out"][0], 256)
    for n in ("ln1_g", "ln1_b", "ln2_g", "ln2_b", "ln3_g", "ln3_b"):
        sh[n] = _fm(inp[n][0], KC)
    sh["ones_bf"] = np.ones((128, 128), np.float32)
    return sh


def _cols(own, smp, halo):
    F_ = own.shape[1]
    a = own.reshape(64, 16, F_).transpose(1, 0, 2).reshape(NOWN, F_)
    b = smp.transpose(1, 0, 2).reshape(NSMP, F_)
    return np.concatenate([a, b, halo], 0).T


def prep_core_rest(inp, c):
    b, half = c // 2, c % 2
    xp = inp["x_prompt"][b]
    own = xp[half * NOWN:(half + 1) * NOWN]
    halo = xp[NOWN - 16:NOWN] if half == 1 else np.zeros((16, D), np.float32)
    xs = inp["x_sample"][16 * c:16 * c + 16]
    xr = _cols(own, xs, halo)
    d = {"xr": np.ascontiguousarray(xr.reshape(KC, 128, NCOL).transpose(1, 0, 2))}
    return d


def uncols(a):
    a = a.T
    own = a[:NOWN].reshape(16, 64, -1).transpose(1, 0, 2).reshape(NOWN, -1)
    smp = a[NOWN:NOWN + NSMP].reshape(4, 16, -1).transpose(1, 0, 2)
    return own, smp, a[NOWN + NSMP:]


class KernAttn:
    def softmax_pv(self, S, sk, np_, OT_out, okeys, vfn, vkeys, ident_rows):
        raise NotImplementedError

    def phase5(self, stop=None):
        fw = self.fw
        RT, AT, BT = self.RT, self.AT, self.BTt
        IDB, psbf = self.IDB, self.psbf
        memT_d = self.din("memT", [128, KC, NMEM])
        w_k = self.din("w_k", [8, 128, KC, 256])
        w_v = self.din("w_v", [8, 128, KC, 256])
        w_q = self.din("w_q", [8, 128, KC, 256])
        w_o = self.din("w_o", [8, 128, KC, 256])
        kTs_d = self.din("kTs", [16, 128, KC, NMEM])
        vs_d = self.din("vs", [16, 128, 2, D])
        p_mem_k = self.dout("p_mem_kT", [KC, 128, NMEM])
        p_mem_v = self.dout("p_mem_v", [2, 128, D])
        KT = self.sb("KT", [128, KC, NMEM], BF16, OFF_XT)
        VV = self.sb("VV", [128, 2, D], BF16, OFF_XT + 8192)
        fw.retire(["X"], ["KT", "VV"])
        MT = self.sb("MT", [128, KC, NMEM], BF16, OFF_BT)
        btk = ["BT%d" % kc for kc in range(KC)]
        fw.dma("pool", None, "memT", r=[], w=btk, out=MT, in_=memT_d, max_dma_last_dim=8192)
        STG = [self.sb("STG%d" % i, [128, 512], F32, OFF_BT + 8192 + 2048 * i) for i in range(2)]
        sti = [0]
        wk_ = self.wstream([(w_k[b], KC, 256) for b in range(8)] + [(w_v[b], KC, 256) for b in range(8)], ahead=2)
        for blk in range(8):
            W, wk = next(wk_)
            for mm in range(2):
                mo = 2 * blk + mm
                ps, pk = self.bank()
                for kc in range(KC):
                    fw.op("pe", "matmul", r=[wk] + btk, w=[pk], out=ps[:, 0:NMEM], lhsT=W[:, kc, mm * 128:(mm + 1) * 128],
                          rhs=MT[:, kc, :], start=(kc == 0), stop=(kc == KC - 1))
                i = sti[0] % 2
                sti[0] += 1
                self.copy("act", KT[:, mo, :], ps[:, 0:NMEM], [pk], ["KT"])
                self.copy("dve", STG[i][:, 0:NMEM], ps[:, 0:NMEM], [pk] + btk, ["STG%d" % i])
                fw.dma("sp", None, "stg%d" % i, r=["STG%d" % i], out=p_mem_k[mo], in_=STG[i][:, 0:NMEM])
        for blk in range(8):
            W, wk = next(wk_)
            for mt in range(2):
                ps, pk = self.bank()
                for kc in range(KC):
                    fw.op("pe", "matmul", r=[wk] + btk, w=[pk], out=ps[:, 0:256], lhsT=MT[:, kc, mt * 128:(mt + 1) * 128],
                          rhs=W[:, kc, :], start=(kc == 0), stop=(kc == KC - 1))
                i = sti[0] % 2
                sti[0] += 1
                self.copy("act", VV[:, mt, blk * 256:(blk + 1) * 256], ps[:, 0:256], [pk], ["VV"])
                self.copy("dve", STG[i][:, 0:256], ps[:, 0:256], [pk] + btk, ["STG%d" % i])
                fw.dma("sp", None, "stg%d" % i, r=["STG%d" % i], out=p_mem_v[mt, :, blk * 256:(blk + 1) * 256],
                       in_=STG[i][:, 0:256])
        if stop == "kv":
            return
        QT = BT
        qk = ["QT%d" % kc for kc in range(KC)]
        fw.retire(btk + ["STG0", "STG1"], qk)
        scale = float(HD) ** -0.5

        def epi_q(mo, banks):
            for (ps, pk, c0, c1) in banks:
                fw.op("act", "activation", r=[pk], w=["QT%d" % mo], out=QT[:, mo, c0:c1], in_=ps[:, 0:c1 - c0],
                      func=AF.Copy, scale=scale)

        self.linear_fm(w_q, KC, D, lambda kc, c0, c1: AT[:, kc, c0:c1],
                       lambda kc, bi=None: ["AT%d" % kc] if bi is None else ["AT%d.%d" % (kc, bi)], epi_q)
        if stop == "q":
            return
        OT = AT
        ok = ["OT%d" % kc for kc in range(KC)]
        fw.retire(["AT%d" % kc for kc in range(KC)] + ["AT%d.%d" % (kc, bi) for kc in range(KC) for bi in range(3)], ok)
        R_ = 4
        lo = self.lnt_off
        PB = [self.sb("PB%d" % i, [128, 2, 256], BF16, lo + 1024 * i) for i in range(R_)]
        SM = [self.sb("SM%d" % i, [128, 8], F32, lo + 1024 * R_ + 32 * i) for i in range(R_)]
        fw.retire(["LNT0", "LNT1", "LNT2"], ["PB%d" % i for i in range(R_)] + ["SM%d" % i for i in range(R_)])
        psb = [(self.psbf, "psbf"), (self.psbf2, "psbf2")]

        def stageA(it, i):
            np_ = it["np"]
            P = PB[i % R_][:, 0, :]
            pkey, sm = "PB%d" % (i % R_), "SM%d" % (i % R_)
            S_ = SM[i % R_]
            pS, kS = self.bank()
            for dc in range(4):
                fw.op("pe", "matmul", r=qk + it["kkeys"], w=[kS], out=pS[0:np_, 0:NMEM], lhsT=it["q"](dc),
                      rhs=it["k"](dc), start=(dc == 0), stop=(dc == 3))
            fw.op("dve", "reduce_max", r=[kS], w=[sm], out=S_[0:np_, 0:1], in_=pS[0:np_, 0:NMEM], axis=AX.X)
            self.ts("dve", S_[0:np_, 1:2], S_[0:np_, 0:1], -1.0, ALU.mult, [sm], [sm])
            fw.op("act", "activation", r=[kS, sm], w=[pkey, sm], out=P[0:np_, :], in_=pS[0:np_, 0:NMEM],
                  func=AF.Exp, bias=S_[0:np_, 1:2], accum_out=S_[0:np_, 2:3])
            fw.op("dve", "reciprocal", r=[sm], w=[sm], out=S_[0:np_, 3:4], in_=S_[0:np_, 2:3])
            self.ts("dve", P[0:np_, :], P[0:np_, :], S_[0:np_, 3:4], ALU.mult, [pkey, sm], [pkey])

        def stageB(it, i):
            np_ = it["np"]
            P, PTr = PB[i % R_][:, 0, :], PB[i % R_][:, 1, :]
            pkey = "PB%d" % (i % R_)
            pb_, pbk = psb[i % 2]
            for mt in range(2):
                fw.op("pe", "transpose", r=[pkey, "CST"], w=[pbk], out=pb_[:, mt * 128:mt * 128 + np_],
                      in_=P[0:np_, mt * 128:(mt + 1) * 128], identity=IDB[0:np_, 0:np_])
            self.copy("dve", PTr.rearrange("p (t k) -> p t k", t=2)[:, :, 0:np_],
                      pb_[:, 0:256].rearrange("p (t k) -> p t k", t=2)[:, :, 0:np_], [pbk], [pkey])

        def stageC(it, i):
            np_ = it["np"]
            PTr = PB[i % R_][:, 1, :]
            pkey = "PB%d" % (i % R_)
            pO, kO = self.bank()
            for dc in range(4):
                for mt in range(2):
                    fw.op("pe", "matmul", r=[pkey] + it["vkeys"], w=[kO], out=pO[:, dc * 128:dc * 128 + np_],
                          lhsT=it["v"](mt, dc), rhs=PTr[:, mt * 128:mt * 128 + np_], start=(mt == 0), stop=(mt == 1))
            dst, dkeys = it["o"]()
            self.copy("act", dst, pO[:, 0:512].rearrange("p (d k) -> p d k", d=4)[:, :, 0:np_], [kO], dkeys)

        pmakers, smakers = [], []
        tiles = [(128, 128 * t) for t in range(8)] + [(NHALO, C_HALO)]
        if stop == "att1":
            tiles = tiles[:1]
        if stop == "atth":
            tiles = tiles[-1:]
        for (np_, c0) in tiles:
            for h in range(NH):
                pmakers.append(lambda h=h, c0=c0, np_=np_: dict(
                    np=np_, q=lambda dc: QT[:, 4 * h + dc, c0:c0 + np_], k=lambda dc: KT[:, 4 * h + dc, :],
                    kkeys=["KT"], v=lambda mt, dc: VV[:, mt, (4 * h + dc) * 128:(4 * h + dc + 1) * 128], vkeys=["VV"],
                    o=lambda: (OT[:, 4 * h:4 * h + 4, c0:c0 + np_], ["OT%d" % (4 * h + d_) for d_ in range(4)])))
        kvq = {}

        def smp_maker(q, h):
            if q not in kvq:
                kvq[q] = (self.wload(kTs_d[q], KC, NMEM), self.wload(vs_d[q], 2, D))
            (Kq, kq), (Vq, vq) = kvq[q]
            return dict(
                np=4, q=lambda dc: QT[:, 4 * h + dc, C_SMP:C_HALO].rearrange("p (i s) -> p i s", s=16)[:, :, q],
                k=lambda dc: Kq[:, 4 * h + dc, :], kkeys=[kq],
                v=lambda mt, dc: Vq[:, mt, (4 * h + dc) * 128:(4 * h + dc + 1) * 128], vkeys=[vq],
                o=lambda: (OT[:, 4 * h:4 * h + 4, C_SMP:C_HALO].rearrange("p d (i s) -> p d i s", s=16)[:, :, :, q],
                           ["OT%d" % (4 * h + d_) for d_ in range(4)]))

        if stop not in ("att1", "atth", "attp"):
            for q in range(16 if stop != "atts" else 1):
                for h in range(NH):
                    smakers.append(lambda q=q, h=h: smp_maker(q, h))
        makers = []
        np_i = 0
        nq = len(smakers) // NH
        for q in range(nq):
            makers += smakers[NH * q:NH * q + NH]
            tgt = (len(pmakers) * (q + 1)) // nq
            makers += pmakers[np_i:tgt]
            np_i = tgt
        makers += pmakers[np_i:]
        n_it = len(makers)
        its = {}
        for i in range(n_it + 2):
            if i < n_it:
                its[i] = makers[i]()
                stageA(its[i], i)
            if 0 <= i - 1 < n_it:
                stageB(its[i - 1], i - 1)
            if 0 <= i - 2 < n_it:
                stageC(its[i - 2], i - 2)
                del its[i - 2]
        if stop in ("att1", "atth", "attp"):
            return
        if stop == "atts":
            return
        def epi_o(mo, banks):
            for (ps, pk, c0, c1) in banks:
                fw.op("dve", "scalar_tensor_tensor", r=[pk, "RT%d" % mo], w=["RT%d" % mo], out=RT[:, mo, c0:c1],
                      in0=RT[:, mo, c0:c1], scalar=ALPHA, in1=ps[:, 0:c1 - c0], op0=ALU.mult, op1=ALU.add)

        self.linear_fm(w_o, KC, D, lambda kc, c0, c1: OT[:, kc, c0:c1], lambda kc: ["OT%d" % kc], epi_o)
        fw.retire(ok, ["AT%d" % kc for kc in range(KC)])
        fw.retire(qk, btk)
        fw.retire(["PB%d" % i for i in range(4)] + ["SM%d" % i for i in range(4)], ["LNT0", "LNT1", "LNT2"])
        self.layer_norm("2", "ln2_g", "ln2_b")
        self.dbg("h2", RT, ["RT%d" % kc for kc in range(KC)])


def prep_shared_attn(inp):
    sh = {}
    for n in ("w_k", "w_v", "w_q", "w_o"):
        sh[n] = _tile_w(inp[n][0], 256)
    return sh


def prep_core_attn(inp, c):
    b = c // 2
    d = {}
    mem = inp["mem_prompt"][b]
    d["memT"] = np.ascontiguousarray(mem.T.reshape(KC, 128, NMEM).transpose(1, 0, 2))
    ck = inp["cache_mem_k"][0, 16 * c:16 * c + 16].reshape(16, NMEM, D)
    d["kTs"] = np.ascontiguousarray(ck.transpose(0, 2, 1).reshape(16, KC, 128, NMEM).transpose(0, 2, 1, 3))
    cv = inp["cache_mem_v"][0, 16 * c:16 * c + 16].reshape(16, 2, 128, D)
    d["vs"] = np.ascontiguousarray(cv.transpose(0, 2, 1, 3))
    return d


NGRP = 6


class KernFFN:
    def phase6(self):
        fw = self.fw
        RT, AT = self.RT, self.AT
        w_gate = self.din("w_gate", [22, 128, KC, 256])
        w_up = self.din("w_up", [22, 128, KC, 256])
        w_down = self.din("w_down", [NGRP, 8, 128, 8, 256])
        cw_d = self.din("convw", [128, FC, 4])
        convT = self.din("convT", [FC, 128, 16, 2])
        flag_d = self.din("flag", [128, 1])
        p_conv = self.dout("p_conv", [128, FC, 2])
        s_conv = self.dout("s_conv", [128, FC, 16, 2])
        yT = self.dout("yT", [128, KC, NCOL])
        fw.retire(["KT", "VV"], ["GE0", "GE1", "CA", "GS", "AS"])
        o = OFF_XT
        GE = [self.sb("GE%d" % i, [128, 16, 65], F32, o + 4160 * i) for i in range(2)]; o += 8320
        CA = self.sb("CA", [128, 16, 65], F32, o); o += 4160
        GS = self.sb("GS", [128, 16, 6], F32, o); o += 384
        AS = self.sb("AS", [128, 16, 4], F32, o); o += 256
        CW = self.sb("CW", [128, FC, 4], F32, o); o += 704
        FL = self.sb("FL", [128, 1], F32, o); o += 32
        assert o <= OFF_XT + 16384
        lnt0_off = self.lnt_off
        SCV = self.sb("SCV", [128, FC, 16, 2], F32, lnt0_off)
        PCV = self.sb("PCV", [128, FC, 2], F32, lnt0_off + 5632)
        SL = self.sb("SL", [128, NCOL], BF16, lnt0_off + 5632 + 352)
        UE = [self.sb("UE%d" % i, [128, NCOL], BF16, lnt0_off + 2 * 4416 + 2208 * i) for i in range(2)]
        fw.retire(["LNT0", "LNT1", "LNT2"], ["SCV", "PCV", "SL", "UE0", "UE1"])
        fw.dma("sp", None, "cw", r=["GE0"], w=["CW"], out=CW, in_=cw_d)
        fw.dma("sp", None, "fl", r=["GE0"], w=["FL"], out=FL, in_=flag_d)
        ACTG = [self.sb("ACTG%d" % i, [128, 8, NCOL], BF16, OFF_BT + 17664 * i) for i in range(2)]
        btk = ["BT%d" % kc for kc in range(KC)]
        fw.retire(btk, ["ACTG0", "ACTG1"])
        for i in range(2):
            fw.op("pool", "memset", w=["ACTG%d" % i], ap=ACTG[i][:, :, C_HALO:NCOL], constant=0.0)
        atk = ["AT%d" % kc for kc in range(KC)]

        gu_specs = []
        for blk in range(22):
            gu_specs += [(w_gate[blk], KC, 256), (w_up[blk], KC, 256)]
        fw.retire(["ws3"], ["ws3a", "ws3b"])
        gus = self.wstream(gu_specs, ahead=1, ring="gu")

        def gate_up(gi):
            nf = 8 if gi < NGRP - 1 else FC - 8 * (NGRP - 1)
            for b2 in range(nf // 2):
                blk = 4 * gi + b2
                Wg, wgk = next(gus)
                Wu, wuk = next(gus)
                for mm in range(2):
                    f = 2 * blk + mm
                    fl = f - 8 * gi
                    ge = GE[f % 2]
                    gek = "GE%d" % (f % 2)
                    ue = UE[f % 2]
                    uek = "UE%d" % (f % 2)
                    gb = [self.bank() for _ in BLOCKS]
                    for bi, ((ps, pk), (c0, c1)) in enumerate(zip(gb, BLOCKS)):
                        for kc in range(KC):
                            fw.op("pe", "matmul", r=[wgk, "AT%d.%d" % (kc, bi)], w=[pk], out=ps[:, 0:c1 - c0],
                                  lhsT=Wg[:, kc, mm * 128:(mm + 1) * 128], rhs=AT[:, kc, c0:c1],
                                  start=(kc == 0), stop=(kc == KC - 1))
                    self.copy("act", ge[:, 0:8, 1:65], gb[0][0].rearrange("p (s k) -> p s k", k=64), [gb[0][1]], [gek])
                    self.copy("act", ge[:, 8:16, 1:65], gb[1][0].rearrange("p (s k) -> p s k", k=64), [gb[1][1]], [gek])
                    fw.op("act", "activation", r=[gb[2][1], "FL"], w=[gek], out=ge[:, :, 0], in_=gb[2][0][:, 64:80],
                          func=AF.Copy, scale=FL[:, 0:1])
                    self.copy("dve", GS[:, :, 2:6], gb[2][0][:, 0:64].rearrange("p (i q) -> p q i", q=16),
                              [gb[2][1]], ["GS"])
                    fw.dma("sp", None, "convh", w=["GS"], out=GS[:, :, 0:2], in_=convT[f])
                    ub = [self.bank() for _ in BLOCKS]
                    for bi, ((ps, pk), (c0, c1)) in enumerate(zip(ub, BLOCKS)):
                        for kc in range(KC):
                            fw.op("pe", "matmul", r=[wuk, "AT%d.%d" % (kc, bi)], w=[pk], out=ps[:, 0:c1 - c0],
                                  lhsT=Wu[:, kc, mm * 128:(mm + 1) * 128], rhs=AT[:, kc, c0:c1],
                                  start=(kc == 0), stop=(kc == KC - 1))
                    for (ps, pk), (c0, c1) in zip(ub, BLOCKS):
                        self.copy("act" if c0 == 0 else "dve", ue[:, c0:c1], ps[:, 0:c1 - c0], [pk], [uek])
                    self.copy("act", PCV[:, f, :], ge[:, 14:16, 64], [gek], ["PCV"])
                    self.copy("act", SCV[:, f, :, :], GS[:, :, 4:6], ["GS"], ["SCV"])
                    w0, w1, w2, bb = (CW[:, f, j:j + 1] for j in range(4))
                    fw.op("act", "activation", r=[gek, "CW"], w=["CA"], out=CA, in_=ge, func=AF.Identity, scale=w2, bias=bb)
                    stt = lambda out, in0, sc, in1, r, w: fw.op("dve", "scalar_tensor_tensor", r=r, w=w, out=out, in0=in0,
                                                               scalar=sc, in1=in1, op0=ALU.mult, op1=ALU.add)
                    stt(CA[:, 1:16, :], ge[:, 0:15, :], w1, CA[:, 1:16, :], [gek, "CA", "CW"], ["CA"])
                    stt(CA[:, 0, 1:65], ge[:, 15, 0:64], w1, CA[:, 0, 1:65], [gek, "CA", "CW"], ["CA"])
                    stt(CA[:, 2:16, :], ge[:, 0:14, :], w0, CA[:, 2:16, :], [gek, "CA", "CW"], ["CA"])
                    stt(CA[:, 0:2, 1:65], ge[:, 14:16, 0:64], w0, CA[:, 0:2, 1:65], [gek, "CA", "CW"], ["CA"])
                    fw.op("act", "activation", r=["GS", "CW"], w=["AS"], out=AS, in_=GS[:, :, 2:6], func=AF.Identity,
                          scale=w2, bias=bb)
                    stt(AS, GS[:, :, 1:5], w1, AS, ["GS", "AS", "CW"], ["AS"])
                    stt(AS, GS[:, :, 0:4], w0, AS, ["GS", "AS", "CW"], ["AS"])
                    fw.op("act", "activation", r=["CA"], w=["SL"], out=SL[:, 0:NOWN].rearrange("p (s k) -> p s k", k=64),
                          in_=CA[:, :, 1:65], func=AF.Silu)
                    fw.op("act", "activation", r=["AS"], w=["SL"],
                          out=SL[:, C_SMP:C_HALO].rearrange("p (i q) -> p q i", q=16), in_=AS, func=AF.Silu)
                    self.tt("dve", ACTG[gi % 2][:, fl, 0:C_HALO], SL[:, 0:C_HALO], ue[:, 0:C_HALO], ALU.mult,
                            ["SL", uek], ["ACTG%d" % (gi % 2)])

        def down(gi):
            nf = 8 if gi < NGRP - 1 else FC - 8 * (NGRP - 1)
            A = ACTG[gi % 2]
            ak = "ACTG%d" % (gi % 2)
            wds = self.wstream([(w_down[gi, nb][:, 0:nf, :], nf, 256) for nb in range(8)], ahead=1, ring="dn")
            for nb in range(8):
                W, wk = next(wds)
                for mm in range(2):
                    mo = 2 * nb + mm
                    banks = [self.bank() for _ in BLOCKS]
                    for fl in range(nf):
                        for (ps, pk), (c0, c1) in zip(banks, BLOCKS):
                            fw.op("pe", "matmul", r=[wk, ak], w=[pk], out=ps[:, 0:c1 - c0],
                                  lhsT=W[:, fl, mm * 128:(mm + 1) * 128], rhs=A[:, fl, c0:c1],
                                  start=(fl == 0), stop=(fl == nf - 1))
                    for (ps, pk), (c0, c1) in zip(banks, BLOCKS):
                        if gi == 0:
                            fw.op("dve", "scalar_tensor_tensor", r=[pk, "RT%d" % mo], w=["RT%d" % mo],
                                  out=RT[:, mo, c0:c1], in0=RT[:, mo, c0:c1], scalar=ALPHA, in1=ps[:, 0:c1 - c0],
                                  op0=ALU.mult, op1=ALU.add)
                        else:
                            self.tt("dve", RT[:, mo, c0:c1], RT[:, mo, c0:c1], ps[:, 0:c1 - c0], ALU.add,
                                    [pk, "RT%d" % mo], ["RT%d" % mo])

        gate_up(0)
        for gi in range(NGRP):
            if gi + 1 < NGRP:
                gate_up(gi + 1)
            down(gi)
        fw.dma("sp", None, "pcv", r=["PCV"], out=p_conv, in_=PCV)
        fw.dma("sp", None, "scv", r=["SCV"], out=s_conv, in_=SCV)
        fw.retire(["ACTG0", "ACTG1"], btk)
        fw.retire(["SCV", "PCV", "SL", "UE0", "UE1"], ["LNT0", "LNT1", "LNT2"])
        self.layer_norm("3", "ln3_g", "ln3_b", last=True)
        self.dbg("yT", RT, ["RT%d" % kc for kc in range(KC)])
        for q in range(4):
            fw.dma("sp", None, "yT%d" % q, r=["RT%d" % kc for kc in range(4 * q, 4 * q + 4)],
                   out=yT[:, 4 * q:4 * q + 4, :], in_=RT[:, 4 * q:4 * q + 4, :])


def prep_shared_ffn(inp):
    sh = {}
    sh["w_gate"] = _tile_w(inp["w_gate"][0], 256)
    sh["w_up"] = _tile_w(inp["w_up"][0], 256)
    wd = inp["w_down"][0].reshape(FC, 128, 8, 256)
    wdp = np.zeros((NGRP * 8, 128, 8, 256), np.float32)
    wdp[:FC] = wd
    sh["w_down"] = np.ascontiguousarray(wdp.reshape(NGRP, 8, 128, 8, 256).transpose(0, 3, 2, 1, 4))
    cw = np.concatenate([inp["conv_w"][0], inp["conv_b"][0][None]], 0)
    sh["convw"] = np.ascontiguousarray(cw.reshape(4, FC, 128).transpose(2, 1, 0))
    return sh


def prep_core_ffn(inp, c):
    d = {}
    sc = inp["state_conv"][0, 16 * c:16 * c + 16]
    d["convT"] = np.ascontiguousarray(sc.transpose(2, 0, 1).reshape(FC, 128, 16, 2))
    d["flag"] = np.full((128, 1), float(c % 2), np.float32)
    return d


class Kern(KernFFN, KernAttn, KernRest, KernSSM2, KernSSM, Kern0):
    pass


_BUILT = {}


def build():
    if "kb" not in _BUILT:
        kb = Kern()
        kb.setup()
        kb.phase1()
        kb.phase2()
        kb.phase3_tables()
        kb.phase3_main()
        kb.phase3_glu()
        kb.phase4()
        kb.phase5()
        kb.phase6()
        kb.fw.emit()
        _BUILT["kb"] = kb
    return _BUILT["kb"]


def kernel(**inputs):
    inp = {k: np.asarray(v) for k, v in inputs.items()}
    kb = build()
    sh = {}
    for f in (prep_shared, prep_shared_ssm, prep_shared_rest, prep_shared_attn, prep_shared_ffn):
        sh.update(f(inp))
    in_maps = []
    for c in range(NCORES):
        d = dict(sh)
        for f in (prep_core, prep_core_ssm, prep_core_rest, prep_core_attn, prep_core_ffn):
            d.update(f(inp, c))
        in_maps.append({k: np.ascontiguousarray(v, dtype=np.float32) for k, v in d.items() if k in kb.ins})
    res = run_bass_kernel_spmd(kb.nc, in_maps, core_ids=list(range(NCORES))).results
    B, S = 4, 2048
    y_p = np.zeros((B, S, D), np.float32)
    y_s = np.zeros((128, 4, D), np.float32)
    p_pool = np.zeros((1, B, 15, DP), np.float32)
    p_re = np.zeros((1, B, G, NST), np.float32)
    p_im = np.zeros((1, B, G, NST), np.float32)
    p_conv = np.zeros((1, B, 2, DFF), np.float32)
    p_mk = np.zeros((1, B, NMEM, NH, HD), np.float32)
    p_mv = np.zeros((1, B, NMEM, NH, HD), np.float32)
    s_pool = np.zeros((1, 128, 15, DP), np.float32)
    s_re = np.zeros((1, 128, G, NST), np.float32)
    s_im = np.zeros((1, 128, G, NST), np.float32)
    s_conv = np.zeros((1, 128, 2, DFF), np.float32)
    for c in range(NCORES):
        o = {k: np.asarray(v) for k, v in res[c].items()}
        b, half = c // 2, c % 2
        yT = o["yT"].transpose(1, 0, 2).reshape(D, NCOL)
        own, smp, _ = uncols(yT)
        y_p[b, half * NOWN:(half + 1) * NOWN] = own
        y_s[16 * c:16 * c + 16] = smp
        qs = slice(16 * c, 16 * c + 16)
        s_pool[0, qs] = o["s_pool"].transpose(2, 3, 0, 1).reshape(16, 15, DP)
        ss = o["s_ssm"].reshape(2, 64, 2, 32, 16)
        s_re[0, qs] = ss[:, :, 0].transpose(3, 2, 0, 1).reshape(16, G, NST)
        s_im[0, qs] = ss[:, :, 1].transpose(3, 2, 0, 1).reshape(16, G, NST)
        s_conv[0, qs] = o["s_conv"].transpose(2, 3, 1, 0).reshape(16, 2, DFF)
        if half == 1:
            p_pool[0, b] = o["p_pool"].transpose(2, 1, 0).reshape(15, DP)
            ps_ = o["p_ssm"].reshape(2, 64, 2, 32)
            p_re[0, b] = ps_[:, :, 0].transpose(2, 0, 1).reshape(G, NST)
            p_im[0, b] = ps_[:, :, 1].transpose(2, 0, 1).reshape(G, NST)
            p_conv[0, b] = o["p_conv"].transpose(2, 1, 0).reshape(2, DFF)
            p_mk[0, b] = o["p_mem_kT"].reshape(D, NMEM).T.reshape(NMEM, NH, HD)
            p_mv[0, b] = o["p_mem_v"].reshape(NMEM, D).reshape(NMEM, NH, HD)
    return (y_p, y_s, p_pool, p_re, p_im, p_conv, p_mk, p_mv, s_pool, s_re, s_im, s_conv)
```

```python
import numpy as np
import contextlib
import concourse.bass as bass
import concourse.mybir as mybir
from concourse.bass_utils import run_bass_kernel_spmd

F32 = mybir.dt.float32
BF16 = mybir.dt.bfloat16
AF = mybir.ActivationFunctionType
ALU = mybir.AluOpType
AX = mybir.AxisListType

NCORES = 8
D = 2048
KC = 16
DP = 1024
G = 64
NST = 64
CH = 16
DFF = 5632
FC = 44
NMEM = 256
NH = 4
HD = 512
NOWN = 1024
NSMP = 64
NHALO = 16
NCOL = NOWN + NSMP + NHALO
C_SMP = NOWN
C_HALO = NOWN + NSMP
ALPHA = 2.0 ** 0.25
EPS = 1e-5
BLOCKS = ((0, 512), (512, 1024), (1024, NCOL))
ENGS = ("pe", "act", "dve", "pool", "sp")
SB_BASE = 16640


class _Op:
    __slots__ = ("eng", "fn", "deps", "is_dma", "semkey", "sig", "idx", "has_dep")

    def __init__(self, eng, fn, is_dma, semkey):
        self.eng, self.fn, self.is_dma, self.semkey = eng, fn, is_dma, semkey
        self.deps = set()
        self.sig = None
        self.has_dep = False


class FW:
    def __init__(self, nc):
        self.nc = nc
        self.ops = []
        self.lastw = {}
        self.readers = {}

    def _add(self, op, r, w):
        op.idx = len(self.ops)
        r = list(r)
        w = list(w) + [k for k in r if k.startswith("ps")]
        r = [k for k in r if not k.startswith("ps")]
        for k in r:
            lw = self.lastw.get(k)
            if lw is not None:
                op.deps.update(lw)
        for k in w:
            lw = self.lastw.get(k)
            if lw is not None:
                op.deps.update(lw)
            op.deps.update(self.readers.get(k, ()))
        op.deps.discard(op.idx)
        if op.eng == "pe" and not op.is_dma:
            op.deps = {d for d in op.deps if not (self.ops[d].eng == "pe" and not self.ops[d].is_dma)}
        for k in r:
            lst = self.readers.setdefault(k, [])
            if not op.is_dma:
                lst[:] = [i for i in lst if self.ops[i].is_dma or self.ops[i].eng != op.eng]
            lst.append(op.idx)
        for k in w:
            self.lastw[k] = [op.idx]
            self.readers[k] = []
        self.ops.append(op)
        return op

    def op(self, eng, fn, r=(), w=(), **kw):
        if isinstance(fn, str):
            name = fn
            fn = lambda e, name=name, kw=kw: getattr(e, name)(**kw)
        return self._add(_Op(eng, fn, False, None), r, w)

    def dma(self, eng, fn, semkey, r=(), w=(), **kw):
        if fn is None:
            fn = lambda e, kw=kw: e.dma_start(**kw)
        return self._add(_Op(eng, fn, True, semkey), r, w)

    def retire(self, old, new):
        pend = []
        for k in old:
            pend += self.lastw.get(k, []) + self.readers.get(k, [])
        for k in new:
            self.lastw[k] = sorted(set(self.lastw.get(k, []) + pend))

    def emit(self, final_wait_eng="sp"):
        nc, ops = self.nc, self.ops
        for o in ops:
            for d in o.deps:
                ops[d].has_dep = True
        dma_keys = []
        for o in ops:
            if o.is_dma and o.semkey not in dma_keys:
                dma_keys.append(o.semkey)
        with contextlib.ExitStack() as st:
            esem = {e: st.enter_context(nc.semaphore("s_" + e)) for e in ENGS}
            dsem = {k: st.enter_context(nc.semaphore("d_%d" % i)) for i, k in enumerate(dma_keys)}
            ecnt = {e: 0 for e in ENGS}
            dcnt = {k: 0 for k in dma_keys}
            for o in ops:
                if o.is_dma:
                    dcnt[o.semkey] += 16
                    o.sig = (dsem[o.semkey], dcnt[o.semkey])
                elif o.has_dep:
                    ecnt[o.eng] += 1
                    o.sig = (esem[o.eng], ecnt[o.eng])
            block = st.enter_context(nc.Block())
            handles = {"pe": block.tensor, "act": block.scalar, "dve": block.vector,
                       "pool": block.gpsimd, "sp": block.sync}
            for ename in ENGS:
                mine = [o for o in ops if o.eng == ename]

                def body(e, mine=mine, ename=ename):
                    waited = {}
                    for o in mine:
                        need = {}
                        for d in o.deps:
                            sem, val = ops[d].sig
                            key = id(sem)
                            if waited.get(key, 0) >= val:
                                continue
                            if key not in need or need[key][1] < val:
                                need[key] = (sem, val)
                        for key, (sem, val) in need.items():
                            e.wait_ge(sem, val)
                            waited[key] = val
                        ins = o.fn(e)
                        if o.sig is not None:
                            ins.then_inc(o.sig[0], 16 if o.is_dma else 1)
                    if ename == final_wait_eng:
                        for k in dma_keys:
                            e.wait_ge(dsem[k], dcnt[k])
                handles[ename](body)
        self.counts = (ecnt, dcnt)


class Builder:
    def __init__(self, debug=()):
        self.nc = nc = bass.Bass("TRN2", target_bir_lowering=False)
        self.fw = FW(nc)
        self.debug = set(debug)
        self.ins = {}
        self.outs = {}
        self.ev_i = 0
        self.ps_i = 0
        self.psum = [nc.alloc_psum_tensor("psb%d" % i, [128, 512], F32).ap() for i in range(6)]
        self.psbf = nc.alloc_psum_tensor("psbf", [128, 1024], BF16).ap()
        self.psbf2 = nc.alloc_psum_tensor("psbf2", [128, 1024], BF16).ap()

    def din(self, name, shape, dt=F32):
        self.ins[name] = self.nc.dram_tensor(name, list(shape), dt, kind="ExternalInput").ap()
        return self.ins[name]

    def dout(self, name, shape, dt=F32):
        self.outs[name] = self.nc.dram_tensor(name, list(shape), dt, kind="ExternalOutput").ap()
        return self.outs[name]

    def sb(self, name, shape, dt, off):
        assert off % 32 == 0, (name, off)
        nbytes = int(np.prod(shape[1:])) * (2 if dt == BF16 else 4)
        assert off + nbytes <= 212736, (name, off, nbytes)
        return self.nc.alloc_sbuf_tensor_at(name, list(shape), dt, offset=SB_BASE + off).ap()

    def bank(self):
        i = self.ps_i % 6
        self.ps_i += 1
        return self.psum[i], "ps%d" % i

    def evac_eng(self):
        self.ev_i += 1
        return "act" if self.ev_i % 2 else "dve"

    def copy(self, eng, out, in_, r, w):
        if eng == "act":
            self.fw.op("act", lambda e: e.activation(out=out, in_=in_, func=AF.Copy), r=r, w=w)
        else:
            self.fw.op(eng, lambda e: e.tensor_copy(out=out, in_=in_), r=r, w=w)

    def dbg(self, name, ap, keys, dt=F32):
        if name not in self.debug:
            return
        o = self.dout("dbg_" + name, list(ap.shape), dt)
        self.fw.dma("sp", lambda e: e.dma_start(out=o, in_=ap), "dbg_" + name, r=keys)


OFF_RT = 0
OFF_UU = 0
OFF_SP = 32768
OFF_PT = 32768
OFF_E = 32768 + 17664
OFF_AT = 70656
OFF_BT = 70656 + 35328
OFF_WS = 141312
OFF_XT = 141312 + 32768
OFF_MISC = 190464
WS_SLOT = 8192
NWS = 4


class Kern0(Builder):
    def setup(self):
        nc = self.nc
        self.ws_i = 0
        self.misc_off = OFF_MISC
        self.ws = [self.sb("ws%d" % i, [128, WS_SLOT // 2], BF16, OFF_WS + i * WS_SLOT) for i in range(NWS)]

    def misc(self, name, shape, dt):
        nbytes = int(np.prod(shape[1:])) * (2 if dt == BF16 else 4)
        off = self.misc_off
        self.misc_off += (nbytes + 31) // 32 * 32
        return self.sb(name, shape, dt, off)

    def wload(self, src, kcw, ncols, ring="main"):
        if not hasattr(self, "rings"):
            h = WS_SLOT // 4
            self.rings = {"main": [(self.ws[i], "ws%d" % i) for i in range(NWS)],
                          "gu": [(self.ws[i], "ws%d" % i) for i in range(3)],
                          "dn": [(self.ws[3][:, 0:h], "ws3a"), (self.ws[3][:, h:2 * h], "ws3b")]}
            self.ring_i = {k: 0 for k in self.rings}
        slots = self.rings[ring]
        buf, key = slots[self.ring_i[ring] % len(slots)]
        self.ring_i[ring] += 1
        dst = buf[:, 0:kcw * ncols].rearrange("p (k n) -> p k n", n=ncols)
        self.fw.dma("pool", lambda e: e.dma_start(out=dst, in_=src, max_dma_last_dim=8192), key, w=[key])
        return dst, key

    def wstream(self, specs, ahead=1, ring="main"):
        specs = list(specs)
        q = []
        nxt = 0
        for i in range(len(specs)):
            while nxt < len(specs) and nxt <= i + ahead:
                q.append(self.wload(*specs[nxt], ring=ring))
                nxt += 1
            yield q.pop(0)

    def phase1(self):
        fw = self.fw
        xa = self.din("xa", [128, KC, 16, 128])
        xs = self.din("xs", [128, KC, NSMP])
        w_in_s = self.din("w_in_s", [4, 128, KC, 256])
        w_in_p = self.din("w_in_p", [4, 128, KC, 256])
        poolT = self.din("poolT", [8, 128, 16, 15])
        invc = self.din("invc", [128, 8, 16])
        XA = self.XA = self.sb("XA", [128, KC, 16, 128], BF16, OFF_AT)
        XS = self.XS = self.sb("XS", [128, KC, NSMP], BF16, OFF_XT)
        UU = self.UU = self.sb("UU", [128, G, 16, CH], BF16, OFF_UU)
        UUs = self.UUs = self.misc("UUs", [16, G, 4, CH], BF16)
        PT = self.PT = self.sb("PT", [128, 8, NCOL], BF16, OFF_PT)
        INVC = self.sb("INVC", [128, 8, 16], F32, OFF_XT + 2048)
        for q in range(4):
            fw.dma("pool", lambda e, q=q: e.dma_start(out=XA[:, 4 * q:4 * q + 4], in_=xa[:, 4 * q:4 * q + 4],
                                                       max_dma_last_dim=8192), "XA%d" % q, w=["XA%d" % q])
        fw.dma("pool", lambda e: e.dma_start(out=XS, in_=xs, max_dma_last_dim=8192), "XS", w=["XS"])
        fw.dma("sp", lambda e: e.dma_start(out=INVC, in_=invc), "INVC", w=["INVC"])
        xak = ["XA%d" % q for q in range(4)]
        ws_ = self.wstream([(w_in_s[b], KC, 256) for b in range(4)])
        for blk in range(4):
            W, wk = next(ws_)
            for s in range(16):
                ps, pk = self.bank()
                for kc in range(KC):
                    fw.op("pe", lambda e, ps=ps, kc=kc, s=s, W=W: e.matmul(
                        ps[:, 0:256], XA[:, kc, s, :], W[:, kc, :], start=(kc == 0), stop=(kc == KC - 1)),
                        r=[wk] + xak, w=[pk])
                self.copy(self.evac_eng(), UU[:, blk * 16:(blk + 1) * 16, s, :],
                          ps[:, 0:256].rearrange("p (g c) -> p g c", c=CH), r=[pk], w=["UU"])
            for i in range(4):
                ps, pk = self.bank()
                for kc in range(KC):
                    fw.op("pe", lambda e, ps=ps, kc=kc, i=i, W=W: e.matmul(
                        ps[0:16, 0:256], XS[:, kc, i * 16:(i + 1) * 16], W[:, kc, :], start=(kc == 0),
                        stop=(kc == KC - 1)), r=[wk, "XS"], w=[pk])
                self.copy(self.evac_eng(), UUs[:, blk * 16:(blk + 1) * 16, i, :],
                          ps[0:16, 0:256].rearrange("p (g c) -> p g c", c=CH), r=[pk], w=["UUs"])
        self.dbg("UU", UU, ["UU"], BF16)
        self.dbg("UUs", UUs, ["UUs"], BF16)
        E = [self.sb("E%d" % i, [128, 16, 66], F32, OFF_E + i * 4224) for i in range(4)]
        Es = [self.sb("Es%d" % i, [128, 16, 19], F32, OFF_XT + 3072 + i * 1216) for i in range(4)]
        p_pool = self.dout("p_pool", [128, 8, 15])
        PPS = self.sb("PPS", [128, 8, 15], F32, OFF_XT + 2560)
        s_pool = self.dout("s_pool", [8, 128, 16, 15])
        for i in range(4):
            fw.op("dve", lambda e, i=i: e.memset(E[i], 0.0), w=["E%d" % i])
            fw.op("dve", lambda e, i=i: e.memset(Es[i], 0.0), w=["Es%d" % i])
        wp_ = self.wstream([(w_in_p[b], KC, 256) for b in range(4)])
        for blk in range(4):
            W, wk = next(wp_)
            for mm in range(2):
                m = 2 * blk + mm
                grp = m // 2
                nst = grp + 1
                pa, ka = self.bank()
                pb, kb = self.bank()
                pc, kc_ = self.bank()
                lw = lambda kc, W=W, mm=mm: W[:, kc, mm * 128:(mm + 1) * 128]
                for kc in range(KC):
                    st, sp_ = (kc == 0), (kc == KC - 1)
                    fw.op("pe", lambda e, kc=kc, st=st, sp_=sp_, lw=lw, pa=pa: e.matmul(
                        pa, lw(kc), XA[:, kc, 0:8, 64:128], start=st, stop=sp_), r=[wk] + xak, w=[ka])
                    fw.op("pe", lambda e, kc=kc, st=st, sp_=sp_, lw=lw, pb=pb: e.matmul(
                        pb, lw(kc), XA[:, kc, 8:16, 64:128], start=st, stop=sp_), r=[wk] + xak, w=[kb])
                for kc in range(KC):
                    fw.op("pe", lambda e, kc=kc, lw=lw, pc=pc: e.matmul(
                        pc[:, 0:32], lw(kc), XA[:, kc, :, 62:64], start=(kc == 0), stop=(kc == KC - 1)),
                        r=[wk] + xak, w=[kc_])
                for kc in range(KC):
                    fw.op("pe", lambda e, kc=kc, lw=lw, pc=pc: e.matmul(
                        pc[:, 32:96], lw(kc), XS[:, kc, :], start=(kc == 0), stop=(kc == KC - 1)),
                        r=[wk, "XS"], w=[kc_])
                b0 = m % 2
                e0, es0 = E[b0], Es[b0]
                E0k, Es0k = "E%d" % b0, "Es%d" % b0
                self.copy("act", e0[:, 0:8, 2:66], pa.rearrange("p (s k) -> p s k", k=64), r=[ka], w=[E0k])
                self.copy("act", e0[:, 8:16, 2:66], pb.rearrange("p (s k) -> p s k", k=64), r=[kb], w=[E0k])
                self.copy("dve", e0[:, :, 0:2], pc[:, 0:32].rearrange("p (s k) -> p s k", k=2), r=[kc_], w=[E0k])
                self.copy("dve", es0[:, :, 15:19], pc[:, 32:96].rearrange("p (i q) -> p q i", q=16),
                          r=[kc_], w=[Es0k])
                fw.dma("sp", lambda e, m=m, es0=es0: e.dma_start(out=es0[:, :, 0:15], in_=poolT[m]),
                       "Es0h%d" % b0, w=[Es0k])
                if m == 7:
                    self.dbg("E0", e0, [E0k])
                    self.dbg("Es0", es0, [Es0k])
                self.copy("dve", PPS[:, m, :], e0[:, 1:16, 65], r=[E0k], w=["PPS"])
                fw.dma("sp", lambda e, m=m, es0=es0: e.dma_start(out=s_pool[m], in_=es0[:, :, 4:19]),
                       "spool%d" % b0, r=[Es0k])
                cur, curs, ci = e0, es0, b0
                for sti in range(nst):
                    d = 1 << sti
                    ni = 2 if ci != 2 else 3
                    nx, nxs = E[ni], Es[ni]
                    eng = "dve" if (m % 2 == 0) else "pool"
                    fw.op(eng, lambda e, nx=nx, cur=cur, d=d: e.tensor_tensor(
                        out=nx[:, d:16, :], in0=cur[:, d:16, :], in1=cur[:, 0:16 - d, :], op=ALU.add),
                        r=["E%d" % ci], w=["E%d" % ni])
                    fw.op(eng, lambda e, nx=nx, cur=cur, d=d: e.tensor_tensor(
                        out=nx[:, 0:d, 1:66], in0=cur[:, 0:d, 1:66], in1=cur[:, 16 - d:16, 0:65], op=ALU.add),
                        r=["E%d" % ci], w=["E%d" % ni])
                    fw.op(eng, lambda e, nxs=nxs, curs=curs, d=d: e.tensor_tensor(
                        out=nxs[:, :, d:19], in0=curs[:, :, d:19], in1=curs[:, :, 0:19 - d], op=ALU.add),
                        r=["Es%d" % ci], w=["Es%d" % ni])
                    cur, curs, ci = nx, nxs, ni
                rw = 1.0 / float(1 << nst)
                rk = ["E%d" % ci, E0k, "Es%d" % ci, Es0k]
                pk_ = "PT%d" % m
                fw.op("dve", lambda e, cur=cur, e0=e0, m=m, rw=rw: e.scalar_tensor_tensor(
                    out=PT[:, m, 0:NOWN].rearrange("p (s k) -> p s k", k=64), in0=cur[:, :, 2:66], scalar=rw,
                    in1=e0[:, :, 2:66], op0=ALU.mult, op1=ALU.subtract), r=rk, w=[pk_])
                fw.op("dve", lambda e, cur=cur, e0=e0, m=m, rw=rw: e.scalar_tensor_tensor(
                    out=PT[:, m, C_HALO:NCOL], in0=cur[:, :, 1], scalar=rw,
                    in1=e0[:, :, 1], op0=ALU.mult, op1=ALU.subtract), r=rk, w=[pk_])
                fw.op("dve", lambda e, curs=curs, es0=es0, m=m, rw=rw: e.scalar_tensor_tensor(
                    out=PT[:, m, C_SMP:C_HALO].rearrange("p (i q) -> p q i", q=16), in0=curs[:, :, 15:19], scalar=rw,
                    in1=es0[:, :, 15:19], op0=ALU.mult, op1=ALU.subtract), r=rk, w=[pk_])
                tmpc = Es[ni][:, :, 0]
                fw.op("dve", lambda e, cur=cur, m=m, tmpc=tmpc: e.tensor_tensor(
                    out=tmpc, in0=cur[:, :, 2], in1=INVC[:, m, :], op=ALU.mult),
                    r=rk + ["INVC"], w=["Es%d" % ni])
                fw.op("dve", lambda e, e0=e0, m=m, tmpc=tmpc: e.tensor_tensor(
                    out=PT[:, m, 0:NOWN].rearrange("p (s k) -> p s k", k=64)[:, :, 0], in0=tmpc, in1=e0[:, :, 2],
                    op=ALU.subtract), r=["Es%d" % ni, E0k], w=[pk_])
        fw.dma("sp", lambda e: e.dma_start(out=p_pool, in_=PPS), "ppool", r=["PPS"])
        self.dbg("PT", PT, ["PT%d" % m for m in range(8)], BF16)

    def phase2(self):
        fw = self.fw
        w_pool = self.din("w_pool", [4, 128, 2, 256])
        pscale = self.din("pscale", [128, 8])
        PSC = self.misc("PSC", [128, 8], F32)
        fw.dma("sp", lambda e: e.dma_start(out=PSC, in_=pscale), "PSC", w=["PSC"])
        ABT = self.ABT = self.sb("ABT", [128, KC, NCOL], BF16, OFF_BT)
        fw.retire(["XA%d" % q for q in range(4)], ["ABT%d" % m for m in range(16)])
        PT = self.PT
        for g in range(4):
            W, wk = self.wload(w_pool[g], 2, 256)
            for mm in range(2):
                m = 2 * g + mm
                banks = [self.bank() for _ in BLOCKS]
                for kc in range(2):
                    for (ps, pk), (c0, c1) in zip(banks, BLOCKS):
                        fw.op("pe", lambda e, ps=ps, kc=kc, c0=c0, c1=c1, W=W, mm=mm, g=g: e.matmul(
                            ps[:, 0:c1 - c0], W[:, kc, mm * 128:(mm + 1) * 128], PT[:, 2 * g + kc, c0:c1],
                            start=(kc == 0), stop=(kc == 1)), r=[wk, "PT%d" % (2 * g + kc)], w=[pk])
                for (ps, pk), (c0, c1) in zip(banks, BLOCKS):
                    fw.op("act", lambda e, ps=ps, c0=c0, c1=c1, m=m: e.activation(
                        out=ABT[:, m, c0:c1], in_=ps[:, 0:c1 - c0], func=AF.Copy, scale=PSC[:, m:m + 1]),
                        r=[pk, "PSC"], w=["ABT%d" % m])
        self.dbg("aT", ABT[:, 0:8, :], ["ABT%d" % m for m in range(8)], BF16)


def _tile_w(w, ncols):
    K, N = w.shape
    return np.ascontiguousarray(w.reshape(K // 128, 128, N // ncols, ncols).transpose(2, 1, 0, 3))


def _fm(v, nch):
    return np.ascontiguousarray(v.reshape(nch, 128).T)


def prep_shared(inp):
    sh = {}
    w_in = inp["w_in"][0]
    sh["w_in_p"] = _tile_w(w_in[:, :DP], 256)
    sh["w_in_s"] = _tile_w(w_in[:, DP:], 256)
    sh["w_pool"] = np.ascontiguousarray(inp["w_pool"][0].reshape(4, 2, 128, 256).transpose(0, 2, 1, 3))
    sh["pscale"] = _fm(inp["pool_scale"][0], 8)
    return sh


def prep_core(inp, c):
    b, half = c // 2, c % 2
    xp = inp["x_prompt"][b]
    own = xp[half * NOWN:(half + 1) * NOWN]
    pre = xp[0:NOWN] if half == 1 else np.zeros_like(own)
    x2 = np.concatenate([pre, own], 0)
    d = {}
    d["xa"] = np.ascontiguousarray(x2.reshape(128, 16, KC, 128).transpose(3, 2, 1, 0))
    xs = inp["x_sample"][16 * c:16 * c + 16]
    d["xs"] = np.ascontiguousarray(xs.transpose(2, 1, 0).reshape(KC, 128, NSMP).transpose(1, 0, 2))
    sp = inp["state_pool"][0, 16 * c:16 * c + 16]
    d["poolT"] = np.ascontiguousarray(sp.transpose(2, 0, 1).reshape(8, 128, 16, 15))
    invc = np.zeros((128, 8, 16), np.float32)
    for m in range(8):
        w = 2 << (m // 2)
        for s in range(16):
            invc[:, m, s] = 1.0 / (min(s + 1, w) if half == 0 else w)
    d["invc"] = invc
    return d


TWO_PI_LO = 6.283185
MAGIC = 12582912.0
K1 = 144


class KernSSM:
    def tt(self, eng, out, a, b, op, r, w):
        self.fw.op(eng, "tensor_tensor", r=r, w=w, out=out, in0=a, in1=b, op=op)

    def ts(self, eng, out, a, s1, op0, r, w, s2=None, op1=None):
        kw = dict(out=out, in0=a, scalar1=s1, scalar2=s2, op0=op0)
        if op1 is not None:
            kw["op1"] = op1
        self.fw.op(eng, "tensor_scalar", r=r, w=w, **kw)

    def phase3_tables(self):
        fw = self.fw
        names = ["lre", "lim", "lstep"]
        d_in = {n: self.din(n, [128, 32]) for n in names}
        for n in ("bre", "bim", "cre", "cim"):
            d_in[n] = self.din(n, [128, 32, CH])
        mv_d = self.din("mvals", [128, 33])
        cst_d = self.din("cst_bf", [128, 128 + 128 + 64])
        dsk_d = self.din("dskT", [128, G])
        fw.retire(["PT%d" % m for m in range(8)] + ["E0", "E1", "E2", "E3"], ["SPR"])
        fw.retire(["ws%d" % i for i in range(NWS)], ["WSR"])
        fw.retire(["XA%d" % q for q in range(4)], ["YT"])
        fw.retire(["XS", "INVC", "PPS", "Es0", "Es1", "Es2", "Es3"], ["X"])
        T = [self.sb("T%d" % i, [128, 2176], F32, OFF_SP + i * 8704) for i in range(4)]
        BTX = OFF_BT + 17664
        sm = {}
        off = BTX
        for n in ("bre", "bim", "cre", "cim", "BBre", "BBim", "ncim"):
            sm[n] = self.sb("sm_" + n, [128, 32, CH], F32, off)
            off += 2048
        for n in ("lre", "lim", "lstep", "dt", "a", "th", "den", "nre", "kre", "kim"):
            sm[n] = self.sb("sm_" + n, [128, 32], F32, off)
            off += 128
        assert off <= OFF_BT + 35328
        fw.retire(["XA%d" % q for q in range(4)], ["sm_" + n for n in sm])
        PWre = self.PWre = self.misc("PWre", [128, 32, 33], F32)
        PWim = self.PWim = self.misc("PWim", [128, 32, 33], F32)
        MV = self.misc("MV", [128, 33], F32)
        CST = self.misc("CST", [128, 320], BF16)
        self.IDB, self.MASK, self.I64R = CST[:, 0:128], CST[:, 128:256], CST[:, 256:320]
        DSK = self.DSK = self.misc("DSK", [128, G], F32)
        COEF = self.COEF = self.misc("COEF", [128, 32, 8], F32)
        L16 = self.L16 = self.misc("L16", [128, 4, 32], F32)
        Xre = self.Xre = self.sb("Xre", [128, 32, 8, CH], BF16, OFF_XT)
        Xim = self.Xim = self.sb("Xim", [128, 32, 8, CH], BF16, OFF_XT + 8192)
        Yre = self.Yre = self.sb("Yre", [128, 32, 17, CH], BF16, OFF_AT)
        Yim = self.Yim = self.sb("YimN", [128, 32, 17, CH], BF16, OFF_AT + 17408)
        for n in ("lre", "lim", "lstep", "bre", "bim", "cre", "cim"):
            fw.dma("sp", None, "ld_" + n, w=["sm_" + n], r=[], out=sm[n], in_=d_in[n])
        fw.dma("sp", None, "ld_mv", w=["MV"], out=MV, in_=mv_d)
        fw.dma("pool", None, "ld_cst", w=["CST"], out=CST, in_=cst_d)
        fw.dma("sp", None, "ld_dsk", w=["DSK"], out=DSK, in_=dsk_d)
        e = "dve"
        S = lambda n: ["sm_" + n]
        fw.op("act", "activation", r=S("lstep"), w=S("dt"), out=sm["dt"], in_=sm["lstep"], func=AF.Exp)
        self.tt(e, sm["a"], sm["lre"], sm["dt"], ALU.mult, S("lre") + S("dt"), S("a"))
        self.tt(e, sm["th"], sm["lim"], sm["dt"], ALU.mult, S("lim") + S("dt"), S("th"))
        ACt = [self.sb("AC%d" % i, [128, 1056], F32, OFF_E + i * 4224) for i in range(4)]
        fw.retire(["E0", "E1", "E2", "E3"], ["AC0", "AC1", "AC2", "AC3"])
        A = [ACt[i].rearrange("p (g i) -> p g i", i=33) for i in range(4)]
        bc_g = lambda v: v.unsqueeze(2).to_broadcast([128, 32, 33])
        bc_i = MV.unsqueeze(1).to_broadcast([128, 32, 33])
        self.tt(e, A[0], bc_g(sm["th"]), bc_i, ALU.mult, S("th") + ["MV"], ["AC0"])
        self.tt(e, A[1], bc_g(sm["a"]), bc_i, ALU.mult, S("a") + ["MV"], ["AC1"])
        fw.op("act", "activation", r=["AC1"], w=["AC1"], out=A[1], in_=A[1], func=AF.Exp)
        self.ts(e, A[2], A[0], 1.0 / (2.0 * np.pi), ALU.mult, ["AC0"], ["AC2"])
        self.ts(e, A[3], A[2], MAGIC, ALU.add, ["AC2"], ["AC3"])
        self.ts(e, A[3], A[3], MAGIC, ALU.subtract, ["AC3"], ["AC3"])
        self.tt(e, A[3], A[2], A[3], ALU.subtract, ["AC2", "AC3"], ["AC3"])
        fw.op("act", "activation", r=["AC3"], w=["AC3"], out=A[3], in_=A[3], func=AF.Sin, scale=TWO_PI_LO)
        self.tt(e, PWim, A[1], A[3], ALU.mult, ["AC1", "AC3"], ["PWim"])
        self.ts(e, A[0], A[2], 0.25, ALU.add, ["AC2"], ["AC0"])
        self.ts(e, A[3], A[0], MAGIC, ALU.add, ["AC0", "PWim"], ["AC3"])
        self.ts(e, A[3], A[3], MAGIC, ALU.subtract, ["AC3"], ["AC3"])
        self.tt(e, A[3], A[0], A[3], ALU.subtract, ["AC0", "AC3"], ["AC3"])
        fw.op("act", "activation", r=["AC3"], w=["AC3"], out=A[3], in_=A[3], func=AF.Sin, scale=TWO_PI_LO)
        self.tt(e, PWre, A[1], A[3], ALU.mult, ["AC1", "AC3"], ["PWre"])
        PW = ["PWre", "PWim"]
        p1re, p1im = PWre[:, :, 17], PWim[:, :, 17]
        self.tt(e, sm["den"], sm["lre"], sm["lre"], ALU.mult, S("lre"), S("den"))
        self.tt(e, sm["nre"], sm["lim"], sm["lim"], ALU.mult, S("lim"), S("nre"))
        self.tt(e, sm["den"], sm["den"], sm["nre"], ALU.add, S("den") + S("nre"), S("den"))
        fw.op(e, "reciprocal", r=S("den"), w=S("den"), out=sm["den"], in_=sm["den"])
        self.ts(e, sm["nre"], p1re, -1.0, ALU.add, PW + S("den"), S("nre"))
        self.tt(e, sm["kre"], sm["nre"], sm["lre"], ALU.mult, S("nre") + S("lre"), S("kre"))
        self.tt(e, sm["dt"], p1im, sm["lim"], ALU.mult, PW + S("lim") + S("a"), S("dt"))
        self.tt(e, sm["kre"], sm["kre"], sm["dt"], ALU.add, S("kre") + S("dt"), S("kre"))
        self.tt(e, sm["kre"], sm["kre"], sm["den"], ALU.mult, S("kre") + S("den"), S("kre"))
        self.tt(e, sm["kim"], p1im, sm["lre"], ALU.mult, PW + S("lre"), S("kim"))
        self.tt(e, sm["dt"], sm["nre"], sm["lim"], ALU.mult, S("nre") + S("lim") + S("kre"), S("dt"))
        self.tt(e, sm["kim"], sm["kim"], sm["dt"], ALU.subtract, S("kim") + S("dt"), S("kim"))
        self.tt(e, sm["kim"], sm["kim"], sm["den"], ALU.mult, S("kim") + S("den"), S("kim"))
        bc_c = lambda v: v.unsqueeze(2).to_broadcast([128, 32, CH])
        t1 = ACt[0][:, 0:512].rearrange("p (g c) -> p g c", c=CH)
        t2 = ACt[1][:, 0:512].rearrange("p (g c) -> p g c", c=CH)
        self.tt(e, t1, bc_c(sm["kre"]), sm["bre"], ALU.mult, S("kre") + S("bre") + PW, ["AC0"])
        self.tt(e, t2, bc_c(sm["kim"]), sm["bim"], ALU.mult, S("kim") + S("bim") + PW, ["AC1"])
        self.tt(e, sm["BBre"], t1, t2, ALU.subtract, ["AC0", "AC1"], S("BBre"))
        self.tt(e, t1, bc_c(sm["kre"]), sm["bim"], ALU.mult, S("kre") + S("bim") + S("BBre"), ["AC0"])
        self.tt(e, t2, bc_c(sm["kim"]), sm["bre"], ALU.mult, S("kim") + S("bre") + S("BBre"), ["AC1"])
        self.tt(e, sm["BBim"], t1, t2, ALU.add, ["AC0", "AC1"], S("BBim"))
        self.ts(e, sm["ncim"], sm["cim"], -1.0, ALU.mult, S("cim"), S("ncim"))
        for h in range(2):
            gs = slice(16 * h, 16 * h + 16)
            v1 = T[2][:, 0:2048].rearrange("p (g s c) -> p g s c", s=8, c=CH)
            v2 = T[3][:, 0:2048].rearrange("p (g s c) -> p g s c", s=8, c=CH)
            npre = PWre[:, gs, 0:8].unsqueeze(3).to_broadcast([128, 16, 8, CH])
            npim = PWim[:, gs, 0:8].unsqueeze(3).to_broadcast([128, 16, 8, CH])
            bbre = sm["BBre"][:, gs, :].unsqueeze(2).to_broadcast([128, 16, 8, CH])
            bbim = sm["BBim"][:, gs, :].unsqueeze(2).to_broadcast([128, 16, 8, CH])
            rr = PW + S("BBre") + S("BBim") + ["SPR", "AC0", "AC1", "AC2", "AC3"]
            for (o_, x1, y1, x2, y2, op) in ((Xre, npre, bbre, npim, bbim, ALU.subtract),
                                             (Xim, npre, bbim, npim, bbre, ALU.add)):
                self.tt("dve", v1, x1, y1, ALU.mult, rr + ["X"], ["T2"])
                self.tt("dve", v2, x2, y2, ALU.mult, rr + ["X"], ["T3"])
                self.tt("dve", o_[:, gs], v1, v2, op, ["T2", "T3"], ["X"])
        for h in range(4):
            gs = slice(8 * h, 8 * h + 8)
            v1 = T[0][:, 0:2176].rearrange("p (g s c) -> p g s c", s=17, c=CH)
            v2 = T[1][:, 0:2176].rearrange("p (g s c) -> p g s c", s=17, c=CH)
            ppre = PWre[:, gs, 16:33].unsqueeze(3).to_broadcast([128, 8, 17, CH])
            ppim = PWim[:, gs, 16:33].unsqueeze(3).to_broadcast([128, 8, 17, CH])
            bc_s = lambda v: v[:, gs, :].unsqueeze(2).to_broadcast([128, 8, 17, CH])
            rr = PW + S("cre") + S("cim") + S("ncim") + ["SPR", "AC0", "AC1", "AC2", "AC3"]
            for (o_, x1, y1, x2, y2) in ((Yre, ppre, bc_s(sm["cre"]), ppim, bc_s(sm["cim"])),
                                         (Yim, ppre, bc_s(sm["ncim"]), ppim, bc_s(sm["cre"]))):
                self.tt(e, v1, x1, y1, ALU.mult, rr + ["YT"], ["T0"])
                self.tt(e, v2, x2, y2, ALU.mult, rr + ["YT"], ["T1"])
                self.tt(e, o_[:, gs], v1, v2, ALU.subtract, ["T0", "T1"], ["YT"])
        for t, mi in ((0, 16 + 15), (1, 16 + 7)):
            cv = COEF[:, :, 4 * t:4 * t + 4]
            self.copy("pool", cv[:, :, 0], PWre[:, :, mi], PW, ["COEF"])
            self.copy("pool", cv[:, :, 1], PWim[:, :, mi], PW, ["COEF"])
            self.ts("pool", cv[:, :, 2], PWim[:, :, mi], -1.0, ALU.mult, PW, ["COEF"])
            self.copy("pool", cv[:, :, 3], PWre[:, :, mi], PW, ["COEF"])
        self.copy("pool", L16[:, 0, :], PWre[:, :, 32], PW, ["L16"])
        self.copy("pool", L16[:, 1, :], PWre[:, :, 32], PW, ["L16"])
        self.ts("pool", L16[:, 2, :], PWim[:, :, 32], -1.0, ALU.mult, PW, ["L16"])
        self.copy("pool", L16[:, 3, :], PWim[:, :, 32], PW, ["L16"])
        self.dbg("PWre", PWre, PW)
        self.dbg("PWim", PWim, PW)
        self.dbg("Xre", Xre, ["X"], BF16)
        self.dbg("Yre", Yre, ["YT"], BF16)
        self.dbg("YimN", Yim, ["YT"], BF16)
        self.dbg("BBre", sm["BBre"], S("BBre"))


def prep_shared_ssm(inp):
    sh = {}
    pl = lambda a: np.ascontiguousarray(a.reshape(32, 2, 64, *a.shape[2:]).transpose(1, 2, 0, *range(3, a.ndim + 1))
                                        .reshape(128, 32, *a.shape[2:]))
    lre, lim = inp["lambda_re"][0], inp["lambda_im"][0]
    sh["lre"], sh["lim"] = pl(lre), pl(lim)
    sh["lstep"] = pl(np.broadcast_to(inp["log_step"][0][:, None], (G, NST)))
    sh["bre"], sh["bim"] = pl(inp["b_re"][0]), pl(inp["b_im"][0])
    sh["cre"] = pl(inp["c_re"][0].transpose(0, 2, 1))
    sh["cim"] = pl(inp["c_im"][0].transpose(0, 2, 1))
    mv = np.concatenate([-np.arange(16), np.arange(17)]).astype(np.float32)
    sh["mvals"] = np.ascontiguousarray(np.broadcast_to(mv[None], (128, 33)))
    ident = np.eye(128, dtype=np.float32)
    sidx = np.arange(128) // 16
    mask = (sidx[None, :] >= sidx[:, None]).astype(np.float32)
    i64 = np.concatenate([np.eye(64, dtype=np.float32)] * 2, 0)
    sh["cst_bf"] = np.ascontiguousarray(np.concatenate([ident, mask, i64], 1))
    dsk = inp["d_skip"][0].reshape(G, CH)
    sh["dskT"] = np.ascontiguousarray(np.tile(dsk.T, (8, 1)))
    return sh


class KernSSM2:
    def phase3_main(self, stop=None):
        fw = self.fw
        UU, UUs = self.UU, self.UUs
        Xre, Xim, Yre, Yim = self.Xre, self.Xim, self.Yre, self.Yim
        IDB, MASK, I64R, DSK, COEF, L16 = self.IDB, self.MASK, self.I64R, self.DSK, self.COEF, self.L16
        h0_d = self.din("h0", [128, 2, 32, 16])
        p_ssm = self.dout("p_ssm", [128, 2, 32])
        s_ssm = self.dout("s_ssm", [128, 2, 32, 16])
        o = OFF_WS
        SBF = self.sb("SBF", [128, 2, 32, K1], BF16, o); o += 18432
        SPS = self.sb("SPS", [128, 2, 32, 16], F32, o); o += 4096
        sets = []
        for i in range(2):
            d = {}
            d["TDO"] = self.sb("TDO%d" % i, [128, 256], BF16, o); o += 512
            d["TD"], d["TO"] = d["TDO"][:, 0:128], d["TDO"][:, 128:256]
            d["TMP"] = self.sb("TMPD%d" % i, [128, 128], F32, o); o += 512
            d["BCT"] = self.sb("BCT%d" % i, [128, 2, 128], BF16, o); o += 512
            d["UT"] = self.sb("UT%d" % i, [128, 2, K1], BF16, o); o += 576
            d["D4"] = self.sb("D4%d" % i, [128, 8, 64], BF16, o); o += 1024
            sets.append(d)
        o = OFF_WS + 29824
        CUR = [self.sb("CUR%d" % i, [128, 4, 32], F32, o + 512 * i) for i in range(2)]; o += 1024
        TT = [self.sb("TT%d" % i, [128, 3, 32], F32, o + 384 * i) for i in range(3)]; o += 1152
        assert o <= OFF_WS + 32768
        SP = self.sb("SP", [128, 2, 32, K1 + 1], F32, OFF_SP)
        psbf = self.psbf
        for i in range(2):
            fw.op("pool", "memset", r=["WSR"], w=["UT%d" % i], ap=sets[i]["UT"], constant=0.0)
        fw.retire(["T0", "T1", "T2", "T3"], ["SP"])
        fw.op("pool", "memset", w=["SP"], ap=SP[:, :, :, 0], constant=0.0)

        def transposes(g, st, si):
            UT = st["UT"]
            for t in range(2):
                fw.op("pe", "transpose", r=["UU%d" % g, "UU", "CST"], w=["psbf"],
                      out=psbf[:, t * 128:(t + 1) * 128],
                      in_=UU[:, g, 8 * t:8 * t + 8, :].rearrange("p s c -> p (s c)"), identity=IDB)
            fw.op("pe", "transpose", r=["UUs", "UUs%d" % g, "CST"], w=["psbf"], out=psbf[64:128, 256:272],
                  in_=UUs[:, g, :, :].rearrange("p s c -> p (s c)"), identity=IDB[0:16, 0:16])
            self.copy(self.evac_eng(), UT[:, :, 0:128], psbf[:, 0:256].rearrange("p (t k) -> p t k", t=2),
                      ["psbf"], ["UT%d" % si])
            self.copy(self.evac_eng(), UT[64:128, 1, 128:K1], psbf[64:128, 256:272], ["psbf"], ["UT%d" % si])

        p1state = {}

        def p1_stageA(g):
            g2, par = g // 2, g % 2
            st = sets[g % 2]
            si = g % 2
            D4 = sets[g2 % 2]["D4"]
            dk = "D4%d" % (g2 % 2)
            if par == 0:
                self.tt("pool", D4, COEF[:, g2, :].unsqueeze(2).to_broadcast([128, 8, 64]),
                        I64R.unsqueeze(1).to_broadcast([128, 8, 64]), ALU.mult, ["COEF", "CST", "WSR"], [dk])
            rows = slice(64 * par, 64 * par + 64)
            pB, kB = self.bank()
            for t in range(2):
                a_ = D4[rows, 4 * t:4 * t + 2, :].rearrange("p a n -> p (a n)")
                b_ = D4[rows, 4 * t + 2:4 * t + 4, :].rearrange("p a n -> p (a n)")
                fw.op("pe", "matmul", r=["X", dk], w=[kB], out=pB[:, t * 128:(t + 1) * 128],
                      lhsT=Xre[rows, g2, :, :].rearrange("p s c -> p (s c)"), rhs=a_, start=True, stop=False)
                fw.op("pe", "matmul", r=["X", dk], w=[kB], out=pB[:, t * 128:(t + 1) * 128],
                      lhsT=Xim[rows, g2, :, :].rearrange("p s c -> p (s c)"), rhs=b_, start=False, stop=True)
            self.copy(self.evac_eng(), st["BCT"], pB[:, 0:256].rearrange("p (t n) -> p t n", t=2),
                      [kB], ["BCT%d" % si])
            transposes(g, st, si)

        def p1_stageB(g):
            g2, par = g // 2, g % 2
            st = sets[g % 2]
            si = g % 2
            rows = slice(64 * par, 64 * par + 64)
            if par == 0:
                p1state["pP"] = self.bank()
            pP, kP = p1state["pP"]
            for ri in range(2):
                for t in range(2):
                    fw.op("pe", "matmul", r=["BCT%d" % si, "UT%d" % si], w=[kP],
                          out=pP[rows, ri * K1:(ri + 1) * K1], lhsT=st["BCT"][:, t, ri * 64:(ri + 1) * 64],
                          rhs=st["UT"][:, t, :], start=(t == 0), stop=(t == 1))
            if par == 1:
                self.copy(self.evac_eng(), SP[:, :, g2, 1:K1 + 1],
                          pP[:, 0:2 * K1].rearrange("p (r k) -> p r k", r=2), [kP], ["SP"])

        p1_stageA(0)
        for g in range(G):
            if g + 1 < G:
                p1_stageA(g + 1)
            p1_stageB(g)
        if stop == "pass1":
            self.dbg("SP", SP, ["SP"])
            return
        e = "dve"
        fw.op(e, "memset", r=["WSR"], w=["CUR0"], ap=CUR[0], constant=0.0)
        fw.op(e, "memset", r=["WSR"], w=["CUR1"], ap=CUR[1], constant=0.0)
        import concourse.bass as _b
        L4 = L16.rearrange("p (w c) g -> p w c g", w=2)
        TWS = [self.sb("TWS%d" % i, [128, 3, 64], F32, OFF_WS + 29824 + 1024 + 768 * i) for i in range(2)]
        for i in range(2):
            fw.op("act", "activation", r=["SP", "WSR"], w=["TWp%d" % i], out=TWS[i][:, 2, :].rearrange("p (c g) -> p c g", c=2),
                  in_=SP[:, :, :, i + 1], func=AF.Copy)
        for k in range(128):
            c0, c1 = CUR[k % 2], CUR[(k + 1) % 2]
            k0, k1 = "CUR%d" % (k % 2), "CUR%d" % ((k + 1) % 2)
            tw = TWS[k % 2]
            ka, kp = "TWa%d" % (k % 2), "TWp%d" % (k % 2)
            win = _b.AP(tensor=c0.tensor, offset=c0.offset, ap=[list(c0.ap[0]), [32, 2], [32, 2], [1, 32]])
            self.tt(e, tw[:, 0:2, :].rearrange("p w (c g) -> p w c g", c=2), L4, win, ALU.mult, ["L16", k0, "WSR"], [ka])
            red_in = _b.AP(tensor=tw.tensor, offset=tw.offset, ap=[list(tw.ap[0]), [0, 2], [1, 64], [64, 3]])
            fw.op(e, "tensor_reduce", r=[ka, kp], w=[k1], out=c1.rearrange("p (r c) g -> p r (c g)", r=2),
                  in_=red_in, axis=AX.X, op=ALU.add)
            self.copy("act", SP[:, :, :, k + 1], c1[:, 0:2, :], [k1], ["SPo"])
            if k + 2 < 128:
                fw.op("act", "activation", r=["SP"], w=[kp], out=tw[:, 2, :].rearrange("p (c g) -> p c g", c=2),
                      in_=SP[:, :, :, k + 3], func=AF.Copy)
        fw.dma("sp", None, "p_ssm", r=["CUR0"], out=p_ssm, in_=CUR[0][:, 0:2, :])
        H0 = self.sb("H0", [128, 2, 32, 16], F32, OFF_SP - 0 + 0) if False else None
        h0t = self.sb("H0t", [128, 2, 32, 16], F32, OFF_BT + 17664)
        tq = [self.sb("TQ%d" % i, [128, 32, 16], F32, OFF_BT + 17664 + 4096 + 2048 * i) for i in range(3)]
        allsm = ["sm_" + n for n in ("bre", "bim", "cre", "cim", "BBre", "BBim", "ncim", "lre", "lim", "lstep",
                                     "dt", "a", "th", "den", "nre", "kre", "kim")]
        fw.retire(allsm, ["H0t", "TQ"])
        fw.dma("sp", None, "ld_h0", w=["H0t"], out=h0t, in_=h0_d)
        PWre, PWim = self.PWre, self.PWim
        bq = lambda v: v.unsqueeze(2).to_broadcast([128, 32, 16])
        n12re, n12im = bq(PWre[:, :, 12]), bq(PWim[:, :, 12])
        l16re, l16im = bq(PWre[:, :, 32]), bq(PWim[:, :, 32])
        PW = ["PWre", "PWim"]

        def cmul(out_re, out_im, are, aim, bre_, bim_, rk, wk):
            self.tt(e, tq[0], are, bre_, ALU.mult, rk + ["TQ"], ["TQ"])
            self.tt(e, tq[1], aim, bim_, ALU.mult, rk + ["TQ"], ["TQ"])
            self.tt(e, tq[2], are, bim_, ALU.mult, rk + ["TQ"], ["TQ"])
            self.tt(e, out_re, tq[0], tq[1], ALU.subtract, ["TQ"], wk)
            self.tt(e, tq[0], aim, bre_, ALU.mult, rk + ["TQ"], ["TQ"])
            self.tt(e, out_im, tq[2], tq[0], ALU.add, ["TQ"], wk)

        cmul(SPS[:, 0], SPS[:, 1], n12re, n12im, h0t[:, 0], h0t[:, 1], PW + ["H0t", "WSR"], ["SPS"])
        cmul(h0t[:, 0], h0t[:, 1], l16re, l16im, SPS[:, 0], SPS[:, 1], PW + ["SPS"], ["H0t"])
        self.tt(e, h0t, h0t, SP[:, :, :, 129:K1 + 1], ALU.add, ["H0t", "SP"], ["H0t"])
        fw.dma("sp", None, "s_ssm", r=["H0t"], out=s_ssm, in_=h0t)
        self.copy("act", SBF[:, :, :, 0:128], SP[:, :, :, 0:128], ["SP", "SPo", "WSR"], ["SBF"])
        self.copy("act", SBF[:, :, :, 128:K1], SPS, ["SPS"], ["SBF"])
        self.dbg("SP", SP, ["SP", "SPo"])
        if stop == "level2":
            return
        o2 = OFF_WS + 22528
        sets2 = []
        for i in range(2):
            d = {}
            d["TDO"] = self.sb("P2TDO%d" % i, [128, 256], BF16, o2); o2 += 512
            d["TD"], d["TO"] = d["TDO"][:, 0:128], d["TDO"][:, 128:256]
            d["TMP"] = self.sb("P2TMP%d" % i, [128, 128], F32, o2); o2 += 512
            d["UT"] = self.sb("P2UT%d" % i, [128, 2, K1], BF16, o2); o2 += 576
            d["CCP"] = self.sb("P2CCP%d" % i, [128, 2, 2, 256], BF16, o2); o2 += 2048
            sets2.append(d)
        assert o2 <= OFF_WS + 29824
        for i in range(2):
            fw.op("pool", "memset", r=["SBF"], w=["UT%d" % i], ap=sets2[i]["UT"], constant=0.0)
            fw.op("pool", "memset", r=["SBF"], w=["CCP%d" % i], ap=sets2[i]["CCP"], constant=0.0)
        Z2 = self.Z2 = self.sb("Z2", [128, 16, G, CH], BF16, OFF_SP)
        Zs2 = self.Zs2 = self.sb("Zs2", [16, 4, G, CH], BF16, OFF_MISC + 8224)
        fw.retire(["SP", "SPo"], ["Z2"])
        fw.retire(["PWre", "PWim"], ["Zs2"])
        def p2_stageA(g):
            g2, par = g // 2, g % 2
            si = g % 2
            st = sets2[si]
            ccp = sets2[g2 % 2]["CCP"]
            ck = "CCP%d" % (g2 % 2)
            if par == 0:
                for pr in range(2):
                    rws = slice(64 * pr, 64 * pr + 64)
                    for ri, Yt in ((0, Yre), (1, Yim)):
                        self.copy("act", ccp[rws, pr, ri, :],
                                  Yt[rws, g2, 1:17, :].rearrange("p j c -> p (j c)"), ["YT"], [ck])
            rows = slice(64 * par, 64 * par + 64)
            pT, kT = self.bank()
            fw.op("pe", "matmul", r=["X", "YT"], w=[kT], out=pT[:, 0:256],
                  lhsT=Xre[rows, g2, :, :].rearrange("p s c -> p (s c)"),
                  rhs=Yre[rows, g2, 0:16, :].rearrange("p j c -> p (j c)"), start=True, stop=False)
            fw.op("pe", "matmul", r=["X", "YT"], w=[kT], out=pT[:, 0:256],
                  lhsT=Xim[rows, g2, :, :].rearrange("p s c -> p (s c)"),
                  rhs=Yim[rows, g2, 0:16, :].rearrange("p j c -> p (j c)"), start=False, stop=True)
            self.tt("dve", st["TMP"], pT[:, 0:128], MASK, ALU.mult, [kT, "CST"], ["TMP%d" % si])
            self.copy("dve", st["TO"], pT[:, 128:256], [kT], ["TDO%d" % si])
            fw.op("dve", "scalar_tensor_tensor", r=["TMP%d" % si, "CST", "DSK"], w=["TDO%d" % si],
                  out=st["TD"], in0=IDB, scalar=DSK[:, g:g + 1], in1=st["TMP"], op0=ALU.mult, op1=ALU.add)
            transposes(g, st, si)

        def p2_stageB(g):
            g2, par = g // 2, g % 2
            si = g % 2
            st = sets2[si]
            ccp = sets2[g2 % 2]["CCP"]
            ck = "CCP%d" % (g2 % 2)
            UT = st["UT"]
            tdk = ["TDO%d" % si, "UT%d" % si, "SBF", ck]
            pY, kY = self.bank()
            fw.op("pe", "matmul", r=tdk, w=[kY], out=pY[:, 0:256], lhsT=UT[:, 0, 0:128], rhs=st["TDO"],
                  start=True, stop=False)
            fw.op("pe", "matmul", r=tdk, w=[kY], out=pY[:, 128:256], lhsT=UT[:, 1, 0:128], rhs=st["TD"],
                  start=False, stop=False)
            for ri in range(2):
                fw.op("pe", "matmul", r=tdk, w=[kY], out=pY[:, 0:256], lhsT=SBF[:, ri, g2, 0:128],
                      rhs=ccp[:, par, ri, :], start=False, stop=(ri == 1))
            fw.op("act", "activation", r=[kY], w=["Z2"], out=Z2[:, :, g, :],
                  in_=pY[:, 0:256].rearrange("p (j c) -> p j c", c=CH), func=AF.Gelu_apprx_tanh)
            pS_, kS_ = self.bank()
            fw.op("pe", "matmul", r=tdk, w=[kS_], out=pS_[0:16, 0:64], lhsT=UT[:, 0, 128:K1], rhs=st["TO"][:, 64:128],
                  start=True, stop=False)
            fw.op("pe", "matmul", r=tdk, w=[kS_], out=pS_[0:16, 0:64], lhsT=UT[:, 1, 128:K1], rhs=st["TD"][:, 64:128],
                  start=False, stop=False)
            for ri in range(2):
                fw.op("pe", "matmul", r=tdk, w=[kS_], out=pS_[0:16, 0:64], lhsT=SBF[:, ri, g2, 128:K1],
                      rhs=ccp[:, par, ri, 192:256], start=False, stop=(ri == 1))
            fw.op("act", "activation", r=[kS_], w=["Zs2"], out=Zs2[:, :, g, :],
                  in_=pS_[0:16, 0:64].rearrange("p (j c) -> p j c", c=CH), func=AF.Gelu_apprx_tanh)

        p2_stageA(0)
        for g in range(G):
            if g + 1 < G:
                p2_stageA(g + 1)
            p2_stageB(g)
        self.dbg("Z", Z2, ["Z2"], BF16)
        self.dbg("Zs", Zs2, ["Zs2"], BF16)


def prep_core_ssm(inp, c):
    d = {}
    pl = lambda a: a.reshape(32, 2, 64, 16).transpose(1, 2, 0, 3).reshape(128, 32, 16)
    hre = inp["state_ssm_re"][0, 16 * c:16 * c + 16].transpose(1, 2, 0)
    him = inp["state_ssm_im"][0, 16 * c:16 * c + 16].transpose(1, 2, 0)
    d["h0"] = np.ascontiguousarray(np.stack([pl(hre), pl(him)], 1))
    return d


class KernRest:
    def linear_fm(self, wd, kcw, nout, in_fn, in_keys, epilogue, wkeys_extra=(), blocks=BLOCKS, kc_tiles=None):
        fw = self.fw
        wst = self.wstream([(wd[b], kcw, 256) for b in range(nout // 256)], ahead=2)
        for blk in range(nout // 256):
            W, wk = next(wst)
            for mm in range(2):
                mo = 2 * blk + mm
                banks = [self.bank() for _ in blocks]
                for bi, ((ps, pk), (c0, c1)) in enumerate(zip(banks, blocks)):
                    for kc in range(kcw):
                        try:
                            ik = list(in_keys(kc, bi))
                        except TypeError:
                            ik = list(in_keys(kc))
                        fw.op("pe", "matmul", r=[wk] + ik, w=[pk], out=ps[:, 0:c1 - c0],
                              lhsT=W[:, kc, mm * 128:(mm + 1) * 128], rhs=in_fn(kc, c0, c1),
                              start=(kc == 0), stop=(kc == kcw - 1))
                epilogue(mo, [(ps, pk, c0, c1) for (ps, pk), (c0, c1) in zip(banks, blocks)])

    def layer_norm(self, tag, gname, bname, last=False):
        fw = self.fw
        RT, AT, BT = self.RT, self.AT, self.BTt
        g_d = self.din(gname, [128, KC])
        b_d = self.din(bname, [128, KC])
        GB = self.sb("GB_" + tag, [128, 2, KC], F32, self.a2_off(256))
        fw.dma("sp", None, "gb_" + tag, w=["GB" + tag], out=GB[:, 0, :], in_=g_d)
        fw.dma("sp", None, "gb2_" + tag, w=["GB" + tag], out=GB[:, 1, :], in_=b_d)
        ONES = self.ONES
        MSQ, VAR = self.LNT[0], self.LNT[1]
        for kc in range(KC):
            fw.retire(["AT%d.%d" % (kc, bi) for bi in range(3)], ["AT%d" % kc])
        for kc in range(KC):
            self.copy("dve", AT[:, kc, :], RT[:, kc, :], ["RT%d" % kc], ["AT%d" % kc])
            fw.op("act", "activation", r=["RT%d" % kc], w=["BT%d" % kc], out=BT[:, kc, :], in_=RT[:, kc, :],
                  func=AF.Square)
        s1, s2 = [], []
        for _ in BLOCKS:
            s1.append(self.bank())
            s2.append(self.bank())
        inv = 1.0 / D
        fw.retire(["LNT0", "LNT2"], ["LNTA", "LNTB"])
        fw.retire(["LNT1"], ["MSQ0", "MSQ1", "MSQ2"])
        TMPS = [(self.LNT[2], "LNTA"), (self.LNT[0], "LNTB")]
        for bi, ((p1, k1), (p2, k2), (c0, c1)) in enumerate(zip(s1, s2, BLOCKS)):
            n = c1 - c0
            for kc in range(KC):
                fw.op("pe", "matmul", r=["AT%d" % kc, "ONES"], w=[k1], out=p1[:, 0:n], lhsT=ONES,
                      rhs=AT[:, kc, c0:c1], start=(kc == 0), stop=(kc == KC - 1))
            for kc in range(KC):
                fw.op("pe", "matmul", r=["BT%d" % kc, "ONES"], w=[k2], out=p2[:, 0:n], lhsT=ONES,
                      rhs=BT[:, kc, c0:c1], start=(kc == 0), stop=(kc == KC - 1))
            fw.retire(["AT%d" % kc for kc in range(KC)], ["AT%d.%d" % (kc, bi) for kc in range(KC)])
            fw.op("act", "activation", r=[k1], w=[k1], out=p1[:, 0:n], in_=p1[:, 0:n], func=AF.Copy, scale=inv)
            fw.op("act", "activation", r=[k1], w=["MSQ%d" % bi], out=VAR[:, c0:c1], in_=p1[:, 0:n], func=AF.Square)
            fw.op("dve", "scalar_tensor_tensor", r=[k2, "MSQ%d" % bi], w=["MSQ%d" % bi], out=VAR[:, c0:c1],
                  in0=p2[:, 0:n], scalar=inv, in1=VAR[:, c0:c1], op0=ALU.mult, op1=ALU.subtract)
            self.ts("dve", VAR[:, c0:c1], VAR[:, c0:c1], EPS, ALU.add, ["MSQ%d" % bi], ["MSQ%d" % bi])
            fw.op("act", "activation", r=["MSQ%d" % bi], w=["MSQ%d" % bi], out=VAR[:, c0:c1], in_=VAR[:, c0:c1],
                  func=AF.Sqrt)
            fw.op("dve", "reciprocal", r=["MSQ%d" % bi, k2], w=[k2], out=p2[:, 0:n], in_=VAR[:, c0:c1])
            for kc in range(KC):
                TMP, tk = TMPS[kc % 2]
                self.tt("dve", TMP[:, c0:c1], RT[:, kc, c0:c1], p1[:, 0:n], ALU.subtract, ["RT%d" % kc, k1], [tk])
                self.tt("dve", TMP[:, c0:c1], TMP[:, c0:c1], p2[:, 0:n], ALU.mult, [tk, k2], [tk])
                fw.op("act", "activation", r=[tk, "GB" + tag], w=(["RTo.%d" % bi] if last else ["RT%d" % kc]),
                      out=RT[:, kc, c0:c1], in_=TMP[:, c0:c1],
                      func=AF.Identity, scale=GB[:, 0, kc:kc + 1], bias=GB[:, 1, kc:kc + 1])
                if not last:
                    wk_ = ["AT%d.%d" % (kc, bi)] + (["AT%d" % kc] if bi == 2 else [])
                    fw.op("act", "activation", r=[tk, "GB" + tag], w=wk_, out=AT[:, kc, c0:c1], in_=TMP[:, c0:c1],
                          func=AF.Identity, scale=GB[:, 0, kc:kc + 1], bias=GB[:, 1, kc:kc + 1])
            if last:
                fw.dma("sp", None, "yT%d" % bi, r=["RTo.%d" % bi], out=self.yT_out[:, :, c0:c1], in_=RT[:, :, c0:c1])
        fw.retire(["MSQ0", "MSQ1", "MSQ2"], ["LNT1"])
        fw.retire(["LNTA", "LNTB"], ["LNT0", "LNT2"])

    def a2_off(self, nbytes):
        off = self.a2
        self.a2 += (nbytes + 31) // 32 * 32
        assert self.a2 <= 16832, self.a2
        return OFF_MISC + off

    def phase3_glu(self):
        fw = self.fw
        UU, UUs, IDB, psbf = self.UU, self.UUs, self.IDB, self.psbf
        Z2, Zs2 = self.Z2, self.Zs2
        w_glu = self.din("w_glu", [4, 128, 8, 256])
        bglu_d = self.din("b_glu", [128, 8])
        allz = ["Z2"]
        allzs = ["Zs2"]
        ZT = self.ZT = self.sb("ZT", [128, 8, NCOL], BF16, OFF_AT)
        fw.retire(["YT"], ["ZT%d" % m for m in range(8)])
        for m in range(8):
            for j in range(16):
                fw.op("pe", "transpose", r=allz + ["CST"], w=["psbf"], out=psbf[:, j * 64:(j + 1) * 64],
                      in_=Z2[64:128, j, 8 * m:8 * m + 8, :].rearrange("p g c -> p (g c)"), identity=IDB[64:128, 64:128])
            self.copy(self.evac_eng(), ZT[:, m, 0:NOWN], psbf[:, 0:1024], ["psbf"], ["ZT%d" % m])
        for m in range(8):
            fw.op("pool", "memset", w=["ZT%d" % m], ap=ZT[:, m, C_HALO:NCOL], constant=0.0)
        for m in range(8):
            for jj in range(2):
                fw.op("pe", "transpose", r=allz + ["CST"], w=["psbf"],
                      out=psbf[:, (2 * m + jj) * 32:(2 * m + jj + 1) * 32],
                      in_=Z2[32:64, 14 + jj, 8 * m:8 * m + 8, :].rearrange("p g c -> p (g c)"),
                      identity=IDB[32:64, 32:64])
        self.copy("dve", ZT[:, :, C_HALO + 14:NCOL],
                  psbf[:, 0:512].rearrange("p (m j k) -> p m j k", m=8, j=2)[:, :, :, 31],
                  ["psbf"], ["ZT%d" % m for m in range(8)])
        for m in range(8):
            for i in range(4):
                fw.op("pe", "transpose", r=allzs + ["CST"], w=["psbf"],
                      out=psbf[:, (4 * m + i) * 16:(4 * m + i + 1) * 16],
                      in_=Zs2[:, i, 8 * m:8 * m + 8, :].rearrange("p g c -> p (g c)"), identity=IDB[0:16, 0:16])
        self.copy("act", ZT[:, :, C_SMP:C_HALO], psbf[:, 0:512].rearrange("p (m c) -> p m c", m=8),
                  ["psbf"], ["ZT%d" % m for m in range(8)])
        self.dbg("ZT", ZT, ["ZT%d" % m for m in range(8)], BF16)
        fw.retire(["SBF", "SPS", "CUR0", "CUR1", "TT0", "TT1", "TT2", "UT0", "UT1", "CCP0", "CCP1", "TDO0", "TDO1",
                   "TMP0", "TMP1", "BCT0", "BCT1", "D40", "D41"], ["ws%d" % i for i in range(NWS)])
        self.a2 = 0
        fw.retire(["UUs"] + ["UUs%d" % g for g in range(G)] + allzs + ["PSC", "PWre", "PWim", "MV"], ["A2"])
        BG = self.sb("BGLU", [128, 8], F32, self.a2_off(32))
        fw.dma("sp", None, "bglu", r=["A2"], w=["BGLU"], out=BG, in_=bglu_d)
        self.sg_off = [self.a2_off(1024) for i in range(2)]
        SG = [self.sb("SG%d" % i, [128, 512], BF16, self.sg_off[i]) for i in range(2)]
        ABT = self.ABT
        fw.retire(["H0t", "TQ"], ["ABT%d" % m for m in range(8, 16)])
        sgi = [0]

        def epi(mo, banks):
            for (ps, pk, c0, c1) in banks:
                i = sgi[0] % 2
                sgi[0] += 1
                fw.op("act", "activation", r=[pk, "BGLU", "A2"], w=["SG%d" % i], out=SG[i][:, 0:c1 - c0],
                      in_=ps[:, 0:c1 - c0], func=AF.Sigmoid, bias=BG[:, mo:mo + 1])
                self.tt("dve", ABT[:, 8 + mo, c0:c1], SG[i][:, 0:c1 - c0], ZT[:, mo, c0:c1], ALU.mult,
                        ["SG%d" % i, "ZT%d" % mo], ["ABT%d" % (8 + mo)])

        self.linear_fm(w_glu, 8, DP, lambda kc, c0, c1: ZT[:, kc, c0:c1], lambda kc: ["ZT%d" % kc], epi)
        self.dbg("bT", ABT[:, 8:16, :], ["ABT%d" % m for m in range(8, 16)], BF16)

    def phase4(self):
        fw = self.fw
        xr = self.din("xr", [128, KC, NCOL])
        w_out = self.din("w_out", [8, 128, KC, 256])
        RT = self.RT = self.sb("RT", [128, KC, NCOL], F32, OFF_RT)
        self.AT = self.sb("AT", [128, KC, NCOL], BF16, OFF_AT)
        self.BTt = self.sb("BTt", [128, KC, NCOL], BF16, OFF_BT)
        allz = ["UU", "Z2"] + ["UU%d" % g for g in range(G)]
        fw.retire(allz + ["SP", "SPo", "T0", "T1", "T2", "T3", "SPR"], ["RT%d" % kc for kc in range(KC)])
        for q in range(4):
            fw.dma("sp", None, "xr%d" % q, w=["RT%d" % kc for kc in range(4 * q, 4 * q + 4)],
                   out=RT[:, 4 * q:4 * q + 4, :], in_=xr[:, 4 * q:4 * q + 4, :])
        ABT = self.ABT

        def epi(mo, banks):
            for (ps, pk, c0, c1) in banks:
                fw.op("dve", "scalar_tensor_tensor", r=[pk, "RT%d" % mo], w=["RT%d" % mo], out=RT[:, mo, c0:c1],
                      in0=RT[:, mo, c0:c1], scalar=ALPHA, in1=ps[:, 0:c1 - c0], op0=ALU.mult, op1=ALU.add)

        self.linear_fm(w_out, KC, D, lambda kc, c0, c1: ABT[:, kc, c0:c1], lambda kc: ["ABT%d" % kc], epi)
        self.lnt_off = self.a2_off(3 * NCOL * 4)
        self.LNT = [self.sb("LNT%d" % i, [128, NCOL], F32, self.lnt_off + i * NCOL * 4) for i in range(3)]
        self.ONES = self.sb("ONES", [128, 128], BF16, OFF_MISC + 16832 - 256 - 0) if False else None
        ones_d = self.din("ones_bf", [128, 128])
        self.ONES = self.sb("ONES", [128, 128], BF16, self.a2_off(256))
        fw.dma("pool", None, "ones", r=["A2"], w=["ONES"], out=self.ONES, in_=ones_d)
        fw.retire(["ZT%d" % m for m in range(8)], ["AT%d" % kc for kc in range(KC)])
        fw.retire(["ABT%d" % m for m in range(16)] + ["H0t", "TQ"], ["BT%d" % kc for kc in range(KC)])
        fw.retire(["SG0", "SG1", "BGLU"], ["LNT0", "LNT1", "LNT2"])
        self.layer_norm("1", "ln1_g", "ln1_b")
        self.dbg("h1", RT, ["RT%d" % kc for kc in range(KC)])


def prep_shared_rest(inp):
    sh = {}
    sh["w_glu"] = _tile_w(inp["w_glu"][0], 256)
    sh["b_glu"] = _fm(inp["b_glu"][0], 8)
    sh["w_out"] = _tile_w(inp["w_out"][0], 256)
    for n in ("ln1_g", "ln1_b", "ln2_g", "ln2_b", "ln3_g", "ln3_b"):
        sh[n] = _fm(inp[n][0], KC)
    sh["ones_bf"] = np.ones((128, 128), np.float32)
    return sh


def _cols(own, smp, halo):
    F_ = own.shape[1]
    a = own.reshape(64, 16, F_).transpose(1, 0, 2).reshape(NOWN, F_)
    b = smp.transpose(1, 0, 2).reshape(NSMP, F_)
    return np.concatenate([a, b, halo], 0).T


def prep_core_rest(inp, c):
    b, half = c // 2, c % 2
    xp = inp["x_prompt"][b]
    own = xp[half * NOWN:(half + 1) * NOWN]
    halo = xp[NOWN - 16:NOWN] if half == 1 else np.zeros((16, D), np.float32)
    xs = inp["x_sample"][16 * c:16 * c + 16]
    xr = _cols(own, xs, halo)
    d = {"xr": np.ascontiguousarray(xr.reshape(KC, 128, NCOL).transpose(1, 0, 2))}
    return d


def uncols(a):
    a = a.T
    own = a[:NOWN].reshape(16, 64, -1).transpose(1, 0, 2).reshape(NOWN, -1)
    smp = a[NOWN:NOWN + NSMP].reshape(4, 16, -1).transpose(1, 0, 2)
    return own, smp, a[NOWN + NSMP:]


class KernAttn:
    def softmax_pv(self, S, sk, np_, OT_out, okeys, vfn, vkeys, ident_rows):
        raise NotImplementedError

    def phase5(self, stop=None):
        fw = self.fw
        RT, AT, BT = self.RT, self.AT, self.BTt
        IDB, psbf = self.IDB, self.psbf
        memT_d = self.din("memT", [128, KC, NMEM])
        w_k = self.din("w_k", [8, 128, KC, 256])
        w_v = self.din("w_v", [8, 128, KC, 256])
        w_q = self.din("w_q", [8, 128, KC, 256])
        w_o = self.din("w_o", [8, 128, KC, 256])
        kTs_d = self.din("kTs", [16, 128, KC, NMEM])
        vs_d = self.din("vs", [16, 128, 2, D])
        p_mem_k = self.dout("p_mem_kT", [KC, 128, NMEM])
        p_mem_v = self.dout("p_mem_v", [2, 128, D])
        KT = self.sb("KT", [128, KC, NMEM], BF16, OFF_XT)
        VV = self.sb("VV", [128, 2, D], BF16, OFF_XT + 8192)
        fw.retire(["X"], ["KT", "VV"])
        MT = self.sb("MT", [128, KC, NMEM], BF16, OFF_BT)
        btk = ["BT%d" % kc for kc in range(KC)]
        fw.dma("pool", None, "memT", r=[], w=btk, out=MT, in_=memT_d, max_dma_last_dim=8192)
        STG = [self.sb("STG%d" % i, [128, 512], F32, OFF_BT + 8192 + 2048 * i) for i in range(2)]
        sti = [0]
        wk_ = self.wstream([(w_k[b], KC, 256) for b in range(8)] + [(w_v[b], KC, 256) for b in range(8)], ahead=2)
        for blk in range(8):
            W, wk = next(wk_)
            for mm in range(2):
                mo = 2 * blk + mm
                ps, pk = self.bank()
                for kc in range(KC):
                    fw.op("pe", "matmul", r=[wk] + btk, w=[pk], out=ps[:, 0:NMEM], lhsT=W[:, kc, mm * 128:(mm + 1) * 128],
                          rhs=MT[:, kc, :], start=(kc == 0), stop=(kc == KC - 1))
                i = sti[0] % 2
                sti[0] += 1
                self.copy("act", KT[:, mo, :], ps[:, 0:NMEM], [pk], ["KT"])
                self.copy("dve", STG[i][:, 0:NMEM], ps[:, 0:NMEM], [pk] + btk, ["STG%d" % i])
                fw.dma("sp", None, "stg%d" % i, r=["STG%d" % i], out=p_mem_k[mo], in_=STG[i][:, 0:NMEM])
        for blk in range(8):
            W, wk = next(wk_)
            for mt in range(2):
                ps, pk = self.bank()
                for kc in range(KC):
                    fw.op("pe", "matmul", r=[wk] + btk, w=[pk], out=ps[:, 0:256], lhsT=MT[:, kc, mt * 128:(mt + 1) * 128],
                          rhs=W[:, kc, :], start=(kc == 0), stop=(kc == KC - 1))
                i = sti[0] % 2
                sti[0] += 1
                self.copy("act", VV[:, mt, blk * 256:(blk + 1) * 256], ps[:, 0:256], [pk], ["VV"])
                self.copy("dve", STG[i][:, 0:256], ps[:, 0:256], [pk] + btk, ["STG%d" % i])
                fw.dma("sp", None, "stg%d" % i, r=["STG%d" % i], out=p_mem_v[mt, :, blk * 256:(blk + 1) * 256],
                       in_=STG[i][:, 0:256])
        if stop == "kv":
            return
        QT = BT
        qk = ["QT%d" % kc for kc in range(KC)]
        fw.retire(btk + ["STG0", "STG1"], qk)
        scale = float(HD) ** -0.5

        def epi_q(mo, banks):
            for (ps, pk, c0, c1) in banks:
                fw.op("act", "activation", r=[pk], w=["QT%d" % mo], out=QT[:, mo, c0:c1], in_=ps[:, 0:c1 - c0],
                      func=AF.Copy, scale=scale)

        self.linear_fm(w_q, KC, D, lambda kc, c0, c1: AT[:, kc, c0:c1],
                       lambda kc, bi=None: ["AT%d" % kc] if bi is None else ["AT%d.%d" % (kc, bi)], epi_q)
        if stop == "q":
            return
        OT = AT
        ok = ["OT%d" % kc for kc in range(KC)]
        fw.retire(["AT%d" % kc for kc in range(KC)] + ["AT%d.%d" % (kc, bi) for kc in range(KC) for bi in range(3)], ok)
        R_ = 4
        lo = self.lnt_off
        PB = [self.sb("PB%d" % i, [128, 2, 256], BF16, lo + 1024 * i) for i in range(R_)]
        SM = [self.sb("SM%d" % i, [128, 8], F32, lo + 1024 * R_ + 32 * i) for i in range(R_)]
        fw.retire(["LNT0", "LNT1", "LNT2"], ["PB%d" % i for i in range(R_)] + ["SM%d" % i for i in range(R_)])
        psb = [(self.psbf, "psbf"), (self.psbf2, "psbf2")]

        def stageA(it, i):
            np_ = it["np"]
            P = PB[i % R_][:, 0, :]
            pkey, sm = "PB%d" % (i % R_), "SM%d" % (i % R_)
            S_ = SM[i % R_]
            pS, kS = self.bank()
            for dc in range(4):
                fw.op("pe", "matmul", r=qk + it["kkeys"], w=[kS], out=pS[0:np_, 0:NMEM], lhsT=it["q"](dc),
                      rhs=it["k"](dc), start=(dc == 0), stop=(dc == 3))
            fw.op("dve", "reduce_max", r=[kS], w=[sm], out=S_[0:np_, 0:1], in_=pS[0:np_, 0:NMEM], axis=AX.X)
            self.ts("dve", S_[0:np_, 1:2], S_[0:np_, 0:1], -1.0, ALU.mult, [sm], [sm])
            fw.op("act", "activation", r=[kS, sm], w=[pkey, sm], out=P[0:np_, :], in_=pS[0:np_, 0:NMEM],
                  func=AF.Exp, bias=S_[0:np_, 1:2], accum_out=S_[0:np_, 2:3])
            fw.op("dve", "reciprocal", r=[sm], w=[sm], out=S_[0:np_, 3:4], in_=S_[0:np_, 2:3])
            self.ts("dve", P[0:np_, :], P[0:np_, :], S_[0:np_, 3:4], ALU.mult, [pkey, sm], [pkey])

        def stageB(it, i):
            np_ = it["np"]
            P, PTr = PB[i % R_][:, 0, :], PB[i % R_][:, 1, :]
            pkey = "PB%d" % (i % R_)
            pb_, pbk = psb[i % 2]
            for mt in range(2):
                fw.op("pe", "transpose", r=[pkey, "CST"], w=[pbk], out=pb_[:, mt * 128:mt * 128 + np_],
                      in_=P[0:np_, mt * 128:(mt + 1) * 128], identity=IDB[0:np_, 0:np_])
            self.copy("dve", PTr.rearrange("p (t k) -> p t k", t=2)[:, :, 0:np_],
                      pb_[:, 0:256].rearrange("p (t k) -> p t k", t=2)[:, :, 0:np_], [pbk], [pkey])

        def stageC(it, i):
            np_ = it["np"]
            PTr = PB[i % R_][:, 1, :]
            pkey = "PB%d" % (i % R_)
            pO, kO = self.bank()
            for dc in range(4):
                for mt in range(2):
                    fw.op("pe", "matmul", r=[pkey] + it["vkeys"], w=[kO], out=pO[:, dc * 128:dc * 128 + np_],
                          lhsT=it["v"](mt, dc), rhs=PTr[:, mt * 128:mt * 128 + np_], start=(mt == 0), stop=(mt == 1))
            dst, dkeys = it["o"]()
            self.copy("act", dst, pO[:, 0:512].rearrange("p (d k) -> p d k", d=4)[:, :, 0:np_], [kO], dkeys)

        pmakers, smakers = [], []
        tiles = [(128, 128 * t) for t in range(8)] + [(NHALO, C_HALO)]
        if stop == "att1":
            tiles = tiles[:1]
        if stop == "atth":
            tiles = tiles[-1:]
        for (np_, c0) in tiles:
            for h in range(NH):
                pmakers.append(lambda h=h, c0=c0, np_=np_: dict(
                    np=np_, q=lambda dc: QT[:, 4 * h + dc, c0:c0 + np_], k=lambda dc: KT[:, 4 * h + dc, :],
                    kkeys=["KT"], v=lambda mt, dc: VV[:, mt, (4 * h + dc) * 128:(4 * h + dc + 1) * 128], vkeys=["VV"],
                    o=lambda: (OT[:, 4 * h:4 * h + 4, c0:c0 + np_], ["OT%d" % (4 * h + d_) for d_ in range(4)])))
        kvq = {}

        def smp_maker(q, h):
            if q not in kvq:
                kvq[q] = (self.wload(kTs_d[q], KC, NMEM), self.wload(vs_d[q], 2, D))
            (Kq, kq), (Vq, vq) = kvq[q]
            return dict(
                np=4, q=lambda dc: QT[:, 4 * h + dc, C_SMP:C_HALO].rearrange("p (i s) -> p i s", s=16)[:, :, q],
                k=lambda dc: Kq[:, 4 * h + dc, :], kkeys=[kq],
                v=lambda mt, dc: Vq[:, mt, (4 * h + dc) * 128:(4 * h + dc + 1) * 128], vkeys=[vq],
                o=lambda: (OT[:, 4 * h:4 * h + 4, C_SMP:C_HALO].rearrange("p d (i s) -> p d i s", s=16)[:, :, :, q],
                           ["OT%d" % (4 * h + d_) for d_ in range(4)]))

        if stop not in ("att1", "atth", "attp"):
            for q in range(16 if stop != "atts" else 1):
                for h in range(NH):
                    smakers.append(lambda q=q, h=h: smp_maker(q, h))
        makers = []
        np_i = 0
        nq = len(smakers) // NH
        for q in range(nq):
            makers += smakers[NH * q:NH * q + NH]
            tgt = (len(pmakers) * (q + 1)) // nq
            makers += pmakers[np_i:tgt]
            np_i = tgt
        makers += pmakers[np_i:]
        n_it = len(makers)
        its = {}
        for i in range(n_it + 2):
            if i < n_it:
                its[i] = makers[i]()
                stageA(its[i], i)
            if 0 <= i - 1 < n_it:
                stageB(its[i - 1], i - 1)
            if 0 <= i - 2 < n_it:
                stageC(its[i - 2], i - 2)
                del its[i - 2]
        if stop in ("att1", "atth", "attp"):
            return
        if stop == "atts":
            return
        def epi_o(mo, banks):
            for (ps, pk, c0, c1) in banks:
                fw.op("dve", "scalar_tensor_tensor", r=[pk, "RT%d" % mo], w=["RT%d" % mo], out=RT[:, mo, c0:c1],
                      in0=RT[:, mo, c0:c1], scalar=ALPHA, in1=ps[:, 0:c1 - c0], op0=ALU.mult, op1=ALU.add)

        self.linear_fm(w_o, KC, D, lambda kc, c0, c1: OT[:, kc, c0:c1], lambda kc: ["OT%d" % kc], epi_o)
        fw.retire(ok, ["AT%d" % kc for kc in range(KC)])
        fw.retire(qk, btk)
        fw.retire(["PB%d" % i for i in range(4)] + ["SM%d" % i for i in range(4)], ["LNT0", "LNT1", "LNT2"])
        self.layer_norm("2", "ln2_g", "ln2_b")
        self.dbg("h2", RT, ["RT%d" % kc for kc in range(KC)])


def prep_shared_attn(inp):
    sh = {}
    for n in ("w_k", "w_v", "w_q", "w_o"):
        sh[n] = _tile_w(inp[n][0], 256)
    return sh


def prep_core_attn(inp, c):
    b = c // 2
    d = {}
    mem = inp["mem_prompt"][b]
    d["memT"] = np.ascontiguousarray(mem.T.reshape(KC, 128, NMEM).transpose(1, 0, 2))
    ck = inp["cache_mem_k"][0, 16 * c:16 * c + 16].reshape(16, NMEM, D)
    d["kTs"] = np.ascontiguousarray(ck.transpose(0, 2, 1).reshape(16, KC, 128, NMEM).transpose(0, 2, 1, 3))
    cv = inp["cache_mem_v"][0, 16 * c:16 * c + 16].reshape(16, 2, 128, D)
    d["vs"] = np.ascontiguousarray(cv.transpose(0, 2, 1, 3))
    return d


NGRP = 6


class KernFFN:
    def phase6(self):
        fw = self.fw
        RT, AT = self.RT, self.AT
        w_gate = self.din("w_gate", [22, 128, KC, 256])
        w_up = self.din("w_up", [22, 128, KC, 256])
        w_down = self.din("w_down", [NGRP, 8, 128, 8, 256])
        cw_d = self.din("convw", [128, FC, 4])
        convT = self.din("convT", [FC, 128, 16, 2])
        flag_d = self.din("flag", [128, 1])
        p_conv = self.dout("p_conv", [128, FC, 2])
        s_conv = self.dout("s_conv", [128, FC, 16, 2])
        yT = self.dout("yT", [128, KC, NCOL])
        fw.retire(["KT", "VV"], ["GE0", "GE1", "CA", "GS", "AS"])
        o = OFF_XT
        GE = [self.sb("GE%d" % i, [128, 16, 65], F32, o + 4160 * i) for i in range(2)]; o += 8320
        CA = self.sb("CA", [128, 16, 65], F32, o); o += 4160
        GS = self.sb("GS", [128, 16, 6], F32, o); o += 384
        AS = self.sb("AS", [128, 16, 4], F32, o); o += 256
        CW = self.sb("CW", [128, FC, 4], F32, o); o += 704
        FL = self.sb("FL", [128, 1], F32, o); o += 32
        assert o <= OFF_XT + 16384
        lnt0_off = self.lnt_off
        SCV = self.sb("SCV", [128, FC, 16, 2], F32, lnt0_off)
        PCV = self.sb("PCV", [128, FC, 2], F32, lnt0_off + 5632)
        SL = self.sb("SL", [128, NCOL], BF16, lnt0_off + 5632 + 352)
        UE = [self.sb("UE%d" % i, [128, NCOL], BF16, lnt0_off + 2 * 4416 + 2208 * i) for i in range(2)]
        fw.retire(["LNT0", "LNT1", "LNT2"], ["SCV", "PCV", "SL", "UE0", "UE1"])
        fw.dma("sp", None, "cw", r=["GE0"], w=["CW"], out=CW, in_=cw_d)
        fw.dma("sp", None, "fl", r=["GE0"], w=["FL"], out=FL, in_=flag_d)
        ACTG = [self.sb("ACTG%d" % i, [128, 8, NCOL], BF16, OFF_BT + 17664 * i) for i in range(2)]
        btk = ["BT%d" % kc for kc in range(KC)]
        fw.retire(btk, ["ACTG0", "ACTG1"])
        for i in range(2):
            fw.op("pool", "memset", w=["ACTG%d" % i], ap=ACTG[i][:, :, C_HALO:NCOL], constant=0.0)
        atk = ["AT%d" % kc for kc in range(KC)]

        gu_specs = []
        for blk in range(22):
            gu_specs += [(w_gate[blk], KC, 256), (w_up[blk], KC, 256)]
        fw.retire(["ws3"], ["ws3a", "ws3b"])
        gus = self.wstream(gu_specs, ahead=1, ring="gu")

        def gate_up(gi):
            nf = 8 if gi < NGRP - 1 else FC - 8 * (NGRP - 1)
            for b2 in range(nf // 2):
                blk = 4 * gi + b2
                Wg, wgk = next(gus)
                Wu, wuk = next(gus)
                for mm in range(2):
                    f = 2 * blk + mm
                    fl = f - 8 * gi
                    ge = GE[f % 2]
                    gek = "GE%d" % (f % 2)
                    ue = UE[f % 2]
                    uek = "UE%d" % (f % 2)
                    gb = [self.bank() for _ in BLOCKS]
                    for bi, ((ps, pk), (c0, c1)) in enumerate(zip(gb, BLOCKS)):
                        for kc in range(KC):
                            fw.op("pe", "matmul", r=[wgk, "AT%d.%d" % (kc, bi)], w=[pk], out=ps[:, 0:c1 - c0],
                                  lhsT=Wg[:, kc, mm * 128:(mm + 1) * 128], rhs=AT[:, kc, c0:c1],
                                  start=(kc == 0), stop=(kc == KC - 1))
                    self.copy("act", ge[:, 0:8, 1:65], gb[0][0].rearrange("p (s k) -> p s k", k=64), [gb[0][1]], [gek])
                    self.copy("act", ge[:, 8:16, 1:65], gb[1][0].rearrange("p (s k) -> p s k", k=64), [gb[1][1]], [gek])
                    fw.op("act", "activation", r=[gb[2][1], "FL"], w=[gek], out=ge[:, :, 0], in_=gb[2][0][:, 64:80],
                          func=AF.Copy, scale=FL[:, 0:1])
                    self.copy("dve", GS[:, :, 2:6], gb[2][0][:, 0:64].rearrange("p (i q) -> p q i", q=16),
                              [gb[2][1]], ["GS"])
                    fw.dma("sp", None, "convh", w=["GS"], out=GS[:, :, 0:2], in_=convT[f])
                    ub = [self.bank() for _ in BLOCKS]
                    for bi, ((ps, pk), (c0, c1)) in enumerate(zip(ub, BLOCKS)):
                        for kc in range(KC):
                            fw.op("pe", "matmul", r=[wuk, "AT%d.%d" % (kc, bi)], w=[pk], out=ps[:, 0:c1 - c0],
                                  lhsT=Wu[:, kc, mm * 128:(mm + 1) * 128], rhs=AT[:, kc, c0:c1],
                                  start=(kc == 0), stop=(kc == KC - 1))
                    for (ps, pk), (c0, c1) in zip(ub, BLOCKS):
                        self.copy("act" if c0 == 0 else "dve", ue[:, c0:c1], ps[:, 0:c1 - c0], [pk], [uek])
                    self.copy("act", PCV[:, f, :], ge[:, 14:16, 64], [gek], ["PCV"])
                    self.copy("act", SCV[:, f, :, :], GS[:, :, 4:6], ["GS"], ["SCV"])
                    w0, w1, w2, bb = (CW[:, f, j:j + 1] for j in range(4))
                    fw.op("act", "activation", r=[gek, "CW"], w=["CA"], out=CA, in_=ge, func=AF.Identity, scale=w2, bias=bb)
                    stt = lambda out, in0, sc, in1, r, w: fw.op("dve", "scalar_tensor_tensor", r=r, w=w, out=out, in0=in0,
                                                               scalar=sc, in1=in1, op0=ALU.mult, op1=ALU.add)
                    stt(CA[:, 1:16, :], ge[:, 0:15, :], w1, CA[:, 1:16, :], [gek, "CA", "CW"], ["CA"])
                    stt(CA[:, 0, 1:65], ge[:, 15, 0:64], w1, CA[:, 0, 1:65], [gek, "CA", "CW"], ["CA"])
                    stt(CA[:, 2:16, :], ge[:, 0:14, :], w0, CA[:, 2:16, :], [gek, "CA", "CW"], ["CA"])
                    stt(CA[:, 0:2, 1:65], ge[:, 14:16, 0:64], w0, CA[:, 0:2, 1:65], [gek, "CA", "CW"], ["CA"])
                    fw.op("act", "activation", r=["GS", "CW"], w=["AS"], out=AS, in_=GS[:, :, 2:6], func=AF.Identity,
                          scale=w2, bias=bb)
                    stt(AS, GS[:, :, 1:5], w1, AS, ["GS", "AS", "CW"], ["AS"])
                    stt(AS, GS[:, :, 0:4], w0, AS, ["GS", "AS", "CW"], ["AS"])
                    fw.op("act", "activation", r=["CA"], w=["SL"], out=SL[:, 0:NOWN].rearrange("p (s k) -> p s k", k=64),
                          in_=CA[:, :, 1:65], func=AF.Silu)
                    fw.op("act", "activation", r=["AS"], w=["SL"],
                          out=SL[:, C_SMP:C_HALO].rearrange("p (i q) -> p q i", q=16), in_=AS, func=AF.Silu)
                    self.tt("dve", ACTG[gi % 2][:, fl, 0:C_HALO], SL[:, 0:C_HALO], ue[:, 0:C_HALO], ALU.mult,
                            ["SL", uek], ["ACTG%d" % (gi % 2)])

        def down(gi):
            nf = 8 if gi < NGRP - 1 else FC - 8 * (NGRP - 1)
            A = ACTG[gi % 2]
            ak = "ACTG%d" % (gi % 2)
            wds = self.wstream([(w_down[gi, nb][:, 0:nf, :], nf, 256) for nb in range(8)], ahead=1, ring="dn")
            for nb in range(8):
                W, wk = next(wds)
                for mm in range(2):
                    mo = 2 * nb + mm
                    banks = [self.bank() for _ in BLOCKS]
                    for fl in range(nf):
                        for (ps, pk), (c0, c1) in zip(banks, BLOCKS):
                            fw.op("pe", "matmul", r=[wk, ak], w=[pk], out=ps[:, 0:c1 - c0],
                                  lhsT=W[:, fl, mm * 128:(mm + 1) * 128], rhs=A[:, fl, c0:c1],
                                  start=(fl == 0), stop=(fl == nf - 1))
                    for (ps, pk), (c0, c1) in zip(banks, BLOCKS):
                        if gi == 0:
                            fw.op("dve", "scalar_tensor_tensor", r=[pk, "RT%d" % mo], w=["RT%d" % mo],
                                  out=RT[:, mo, c0:c1], in0=RT[:, mo, c0:c1], scalar=ALPHA, in1=ps[:, 0:c1 - c0],
                                  op0=ALU.mult, op1=ALU.add)
                        else:
                            self.tt("dve", RT[:, mo, c0:c1], RT[:, mo, c0:c1], ps[:, 0:c1 - c0], ALU.add,
                                    [pk, "RT%d" % mo], ["RT%d" % mo])

        gate_up(0)
        for gi in range(NGRP):
            if gi + 1 < NGRP:
                gate_up(gi + 1)
            down(gi)
        fw.dma("sp", None, "pcv", r=["PCV"], out=p_conv, in_=PCV)
        fw.dma("sp", None, "scv", r=["SCV"], out=s_conv, in_=SCV)
        fw.retire(["ACTG0", "ACTG1"], btk)
        fw.retire(["SCV", "PCV", "SL", "UE0", "UE1"], ["LNT0", "LNT1", "LNT2"])
        self.yT_out = yT
        self.layer_norm("3", "ln3_g", "ln3_b", last=True)
        self.dbg("yT", RT, ["RTo.%d" % bi for bi in range(3)])


def prep_shared_ffn(inp):
    sh = {}
    sh["w_gate"] = _tile_w(inp["w_gate"][0], 256)
    sh["w_up"] = _tile_w(inp["w_up"][0], 256)
    wd = inp["w_down"][0].reshape(FC, 128, 8, 256)
    wdp = np.zeros((NGRP * 8, 128, 8, 256), np.float32)
    wdp[:FC] = wd
    sh["w_down"] = np.ascontiguousarray(wdp.reshape(NGRP, 8, 128, 8, 256).transpose(0, 3, 2, 1, 4))
    cw = np.concatenate([inp["conv_w"][0], inp["conv_b"][0][None]], 0)
    sh["convw"] = np.ascontiguousarray(cw.reshape(4, FC, 128).transpose(2, 1, 0))
    return sh


def prep_core_ffn(inp, c):
    d = {}
    sc = inp["state_conv"][0, 16 * c:16 * c + 16]
    d["convT"] = np.ascontiguousarray(sc.transpose(2, 0, 1).reshape(FC, 128, 16, 2))
    d["flag"] = np.full((128, 1), float(c % 2), np.float32)
    return d


class Kern(KernFFN, KernAttn, KernRest, KernSSM2, KernSSM, Kern0):
    pass


_BUILT = {}


def build():
    if "kb" not in _BUILT:
        kb = Kern()
        kb.setup()
        kb.phase1()
        kb.phase2()
        kb.phase3_tables()
        kb.phase3_main()
        kb.phase3_glu()
        kb.phase4()
        kb.phase5()
        kb.phase6()
        kb.fw.emit()
        _BUILT["kb"] = kb
    return _BUILT["kb"]


def kernel(**inputs):
    inp = {k: np.asarray(v) for k, v in inputs.items()}
    kb = build()
    sh = {}
    for f in (prep_shared, prep_shared_ssm, prep_shared_rest, prep_shared_attn, prep_shared_ffn):
        sh.update(f(inp))
    in_maps = []
    for c in range(NCORES):
        d = dict(sh)
        for f in (prep_core, prep_core_ssm, prep_core_rest, prep_core_attn, prep_core_ffn):
            d.update(f(inp, c))
        in_maps.append({k: np.ascontiguousarray(v, dtype=np.float32) for k, v in d.items() if k in kb.ins})
    res = run_bass_kernel_spmd(kb.nc, in_maps, core_ids=list(range(NCORES))).results
    B, S = 4, 2048
    y_p = np.zeros((B, S, D), np.float32)
    y_s = np.zeros((128, 4, D), np.float32)
    p_pool = np.zeros((1, B, 15, DP), np.float32)
    p_re = np.zeros((1, B, G, NST), np.float32)
    p_im = np.zeros((1, B, G, NST), np.float32)
    p_conv = np.zeros((1, B, 2, DFF), np.float32)
    p_mk = np.zeros((1, B, NMEM, NH, HD), np.float32)
    p_mv = np.zeros((1, B, NMEM, NH, HD), np.float32)
    s_pool = np.zeros((1, 128, 15, DP), np.float32)
    s_re = np.zeros((1, 128, G, NST), np.float32)
    s_im = np.zeros((1, 128, G, NST), np.float32)
    s_conv = np.zeros((1, 128, 2, DFF), np.float32)
    for c in range(NCORES):
        o = {k: np.asarray(v) for k, v in res[c].items()}
        b, half = c // 2, c % 2
        yT = o["yT"].transpose(1, 0, 2).reshape(D, NCOL)
        own, smp, _ = uncols(yT)
        y_p[b, half * NOWN:(half + 1) * NOWN] = own
        y_s[16 * c:16 * c + 16] = smp
        qs = slice(16 * c, 16 * c + 16)
        s_pool[0, qs] = o["s_pool"].transpose(2, 3, 0, 1).reshape(16, 15, DP)
        ss = o["s_ssm"].reshape(2, 64, 2, 32, 16)
        s_re[0, qs] = ss[:, :, 0].transpose(3, 2, 0, 1).reshape(16, G, NST)
        s_im[0, qs] = ss[:, :, 1].transpose(3, 2, 0, 1).reshape(16, G, NST)
        s_conv[0, qs] = o["s_conv"].transpose(2, 3, 1, 0).reshape(16, 2, DFF)
        if half == 1:
            p_pool[0, b] = o["p_pool"].transpose(2, 1, 0).reshape(15, DP)
            ps_ = o["p_ssm"].reshape(2, 64, 2, 32)
            p_re[0, b] = ps_[:, :, 0].transpose(2, 0, 1).reshape(G, NST)
            p_im[0, b] = ps_[:, :, 1].transpose(2, 0, 1).reshape(G, NST)
            p_conv[0, b] = o["p_conv"].transpose(2, 1, 0).reshape(2, DFF)
            p_mk[0, b] = o["p_mem_kT"].reshape(D, NMEM).T.reshape(NMEM, NH, HD)
            p_mv[0, b] = o["p_mem_v"].reshape(NMEM, D).reshape(NMEM, NH, HD)
    return (y_p, y_s, p_pool, p_re, p_im, p_conv, p_mk, p_mv, s_pool, s_re, s_im, s_conv)
```
